# Optimizing a Trainium2 kernel written in Bass

```python
import math
import jax, jax.numpy as jnp
from jax import lax
import numpy as np

D_MODEL = 1024
BATCH = 16
SEQ = 256
DEPTH = 2
DEC_BATCH = 8
DEC_SEQ = 4096
PAST_LEN = 512

GRID_W = 64
DIFF_HEADS = 4
DIFF_DH = 64
RET_HEADS = 4
RET_DK = 64
RET_DV = 128
RET_CHUNK = 128
MLA_HEADS = 4
MLA_Q_LORA = 384
MLA_KV_LORA = 256
MLA_NOPE = 128
MLA_ROPE = 64
MLA_V = 128
N_BRANCH = 3
BRANCH_W = 512
D_FF = 4 * D_MODEL
Q_BLOCK = 128
ROT_DIM = 64
ROPE_BASE = 10000.0
EPS = 1e-6
SPLIT_SIZES = (
    DIFF_HEADS * 2 * DIFF_DH,
    DIFF_HEADS * 2 * DIFF_DH,
    DIFF_HEADS * 2 * DIFF_DH,
    RET_HEADS * RET_DK,
    RET_HEADS * RET_DK,
    RET_HEADS * RET_DV,
    RET_HEADS * RET_DV,
    MLA_Q_LORA,
    MLA_KV_LORA,
    MLA_ROPE,
    N_BRANCH * D_MODEL,
)
D_IN = sum(SPLIT_SIZES)

kernel_name = "hybrid_diffretmla_diffusion_step"

F32 = jnp.float32


def rmsnorm(x, g=None):
    x32 = x.astype(F32)
    y = x32 * lax.rsqrt(jnp.mean(jnp.square(x32), axis=-1, keepdims=True) + EPS)
    if g is not None:
        y = y * g.astype(F32)
    return y.astype(x.dtype)


def axial_rope_tables(n):
    n_rows = n // GRID_W
    row = jnp.repeat(jnp.arange(n_rows, dtype=F32), GRID_W)
    col = jnp.tile(jnp.arange(GRID_W, dtype=F32), n_rows)
    axis_dim = ROT_DIM // 2
    inv = ROPE_BASE ** (-jnp.arange(0, axis_dim, 2, dtype=F32) / axis_dim)
    ang_r = row[:, None] * inv[None, :]
    ang_c = col[:, None] * inv[None, :]
    return (jnp.cos(ang_r), jnp.sin(ang_r), jnp.cos(ang_c), jnp.sin(ang_c))


def _rotate(x, cos, sin):
    x1, x2 = jnp.split(x, 2, axis=-1)
    cos = cos[:, None, :].astype(x.dtype)
    sin = sin[:, None, :].astype(x.dtype)
    return jnp.concatenate([x1 * cos - x2 * sin, x2 * cos + x1 * sin], axis=-1)


def apply_axial_rope(x, tabs):
    cr, sr, cc, sc = tabs
    xr, xc = jnp.split(x, 2, axis=-1)
    return jnp.concatenate([_rotate(xr, cr, sr), _rotate(xc, cc, sc)], axis=-1)


def over_query_blocks(fn, q):
    B, N = q.shape[:2]
    nb = N // Q_BLOCK
    qb = jnp.moveaxis(q.reshape((B, nb, Q_BLOCK) + q.shape[2:]), 1, 0)
    ob = lax.map(fn, qb)
    return jnp.moveaxis(ob, 0, 1).reshape((B, N) + ob.shape[3:])


def diff_attention(q, k, v, lam):
    scale = DIFF_DH ** -0.5

    def blk(qb):
        s = jnp.einsum('bqgd,bkgd->bgqk', qb, k).astype(F32) * scale
        p = jax.nn.softmax(s, axis=-1)
        p = p.reshape(p.shape[0], DIFF_HEADS, 2, p.shape[2], p.shape[3])
        a = (p[:, :, 0] - lam * p[:, :, 1]).astype(v.dtype)
        return jnp.einsum('bhqk,bkhe->bqhe', a, v)

    return over_query_blocks(blk, q)


def softmax_attention(q, k, v, scale):
    def blk(qb):
        s = jnp.einsum('bqhd,bkhd->bhqk', qb, k).astype(F32) * scale
        p = jax.nn.softmax(s, axis=-1).astype(v.dtype)
        return jnp.einsum('bhqk,bkhe->bqhe', p, v)

    return over_query_blocks(blk, q)


def retention_scan(q, k, v, log_gamma, r0, strict):
    out_dtype = v.dtype
    B, N, H, _ = q.shape
    dv = v.shape[-1]
    C = RET_CHUNK
    nc = N // C

    def chunks(t):
        return jnp.moveaxis(t.astype(F32).reshape(B, nc, C, H, t.shape[-1]), 1, 0)

    idx = jnp.arange(C, dtype=F32)
    rel = idx[:, None] - idx[None, :]
    mask = (rel > 0) if strict else (rel >= 0)
    decay_in = jnp.where(mask[None], jnp.exp(jnp.where(mask, rel, 0.0)[None] * log_gamma[:, None, None]), 0.0)
    q_dec = jnp.exp((idx + 1.0)[:, None] * log_gamma[None, :])
    k_dec = jnp.exp((C - 1.0 - idx)[:, None] * log_gamma[None, :])
    chunk_dec = jnp.exp(C * log_gamma)[None, :, None, None]

    def step(R, xs):
        qc, kc, vc = xs
        s = jnp.einsum('bihd,bjhd->bhij', qc, kc) * decay_in[None]
        o = (jnp.einsum('bhij,bjhe->bihe', s, vc)
             + jnp.einsum('bihd,bhde->bihe', qc * q_dec[None, :, :, None], R))
        R = chunk_dec * R + jnp.einsum('bjhd,bjhe->bhde', kc * k_dec[None, :, :, None], vc)
        return R, o

    R, o = lax.scan(step, r0.astype(F32), (chunks(q), chunks(k), chunks(v)))
    o = jnp.moveaxis(o, 0, 1).reshape(B, N, H, dv).astype(out_dtype)
    return o, R


def mla_keys(ckv, kr, w_kvb, kn):
    B, K = ckv.shape[:2]
    kv = (ckv @ w_kvb).reshape(B, K, MLA_HEADS, MLA_NOPE + MLA_V)
    k_nope, v = jnp.split(kv, [MLA_NOPE], axis=-1)
    k = jnp.concatenate([k_nope, jnp.broadcast_to(kr[:, :, None, :], (B, K, MLA_HEADS, MLA_ROPE))], axis=-1)
    return rmsnorm(k, kn), v


def token_mixers(h, lw, lam_init, rope, ctx):
    B, N, _ = h.shape
    latent = ctx is not None
    offs = np.cumsum(SPLIT_SIZES)[:-1].tolist()
    z = h @ lw['w_in']
    d_q, d_k, d_v, r_q, r_k, r_v, r_g, m_qa, m_kva, m_kr, g = jnp.split(z, offs, axis=-1)

    dq = rmsnorm(d_q.reshape(B, N, 2 * DIFF_HEADS, DIFF_DH), lw['diff_qn'])
    dk = rmsnorm(d_k.reshape(B, N, 2 * DIFF_HEADS, DIFF_DH), lw['diff_kn'])
    dv = d_v.reshape(B, N, DIFF_HEADS, 2 * DIFF_DH)
    if latent:
        dq = apply_axial_rope(dq, rope)
        dk_all = jnp.concatenate([apply_axial_rope(dk, rope), ctx[0]], axis=1)
        dv_all = jnp.concatenate([dv, ctx[1]], axis=1)
    else:
        dk_all, dv_all = dk, dv
    lmb = lw['diff_lambda'].astype(F32)
    lam = jnp.exp(jnp.sum(lmb[0] * lmb[1])) - jnp.exp(jnp.sum(lmb[2] * lmb[3])) + lam_init
    oa = diff_attention(dq, dk_all, dv_all, lam)
    oa = (rmsnorm(oa, lw['diff_subln']) * (1.0 - lam_init)).reshape(B, N, BRANCH_W)

    rq = r_q.reshape(B, N, RET_HEADS, RET_DK)
    rk = r_k.reshape(B, N, RET_HEADS, RET_DK) * (RET_DK ** -0.5)
    rv = r_v.reshape(B, N, RET_HEADS, RET_DV)
    if latent:
        rq = apply_axial_rope(rq, rope)
        rk = apply_axial_rope(rk, rope)
        r0f, r0b = ctx[4][:, 0], ctx[4][:, 1]
    else:
        r0f = jnp.zeros((B, RET_HEADS, RET_DK, RET_DV), F32)
        r0b = r0f
    lg = jax.nn.log_sigmoid(lw['ret_decay'].astype(F32))
    of, Rf = retention_scan(rq, rk, rv, lg[0], r0f, strict=False)
    ob, Rb = retention_scan(jnp.flip(rq, 1), jnp.flip(rk, 1), jnp.flip(rv, 1), lg[1], r0b, strict=True)
    ob = jnp.flip(ob, 1)
    orr = rmsnorm(of + ob, lw['ret_gn']) * jax.nn.silu(r_g.reshape(B, N, RET_HEADS, RET_DV))
    orr = orr.reshape(B, N, BRANCH_W)

    mq = (rmsnorm(m_qa, lw['mla_qa_norm']) @ lw['w_mla_qb']).reshape(B, N, MLA_HEADS, MLA_NOPE + MLA_ROPE)
    mq = rmsnorm(mq, lw['mla_qn'])
    ckv = rmsnorm(m_kva, lw['mla_kva_norm'])
    mk, mv = mla_keys(ckv, m_kr, lw['w_mla_kvb'], lw['mla_kn'])
    if latent:
        mq = jnp.concatenate([mq[..., :MLA_NOPE], apply_axial_rope(mq[..., MLA_NOPE:], rope)], axis=-1)
        mk = jnp.concatenate([mk[..., :MLA_NOPE], apply_axial_rope(mk[..., MLA_NOPE:], rope)], axis=-1)
        ck, cv = mla_keys(ctx[2], ctx[3], lw['w_mla_kvb'], lw['mla_kn'])
        mk = jnp.concatenate([mk, ck], axis=1)
        mv = jnp.concatenate([mv, cv], axis=1)
    om = softmax_attention(mq, mk, mv, (MLA_NOPE + MLA_ROPE) ** -0.5).reshape(B, N, BRANCH_W)

    gates = jnp.split(g, N_BRANCH, axis=-1)
    merged = 0.0
    for i, o in enumerate((oa, orr, om)):
        merged = merged + jax.nn.sigmoid(gates[i]) * (o @ lw['w_branch'][i])
    out = merged @ lw['w_out']

    if latent:
        return out, None
    return out, (dk, dv, ckv, m_kr, jnp.stack([Rf, Rb], axis=1))


def block(x, cvec, lw, lam_init, rope, ctx):
    mod = (jax.nn.silu(cvec) @ lw['w_mod'] + lw['b_mod'])[:, None, :]
    sh1, sc1, g1, sh2, sc2, g2 = jnp.split(mod, 6, axis=-1)
    h = rmsnorm(x, lw['norm1']) * (1.0 + sc1) + sh1
    mix, ctx_out = token_mixers(h, lw, lam_init, rope, ctx)
    x = x + g1 * mix
    h = rmsnorm(x, lw['norm2']) * (1.0 + sc2) + sh2
    u = jnp.square(jax.nn.relu(h @ lw['w_up']))
    x = x + g2 * (u @ lw['w_down'])
    return x, ctx_out


def setup_inputs(seed: int = 0) -> dict:
    key = jax.random.key(seed)
    ks = iter(jax.random.split(key, 40))

    def nrm(shape, scale=1.0):
        return jax.random.normal(next(ks), shape, F32) * scale

    def gain(shape):
        return 1.0 + nrm(shape, 0.01)

    a = 5.0 + jnp.arange(RET_HEADS, dtype=F32)
    decay_logit = jnp.log(2.0 ** a - 1.0)
    return {
        "x_prompt": nrm((BATCH, SEQ, D_MODEL)),
        "x_sample": nrm((DEC_BATCH, DEC_SEQ, D_MODEL)),
        "cache_diff_k": nrm((DEC_BATCH, DEPTH, PAST_LEN, 2 * DIFF_HEADS, DIFF_DH)),
        "cache_diff_v": nrm((DEC_BATCH, DEPTH, PAST_LEN, DIFF_HEADS, 2 * DIFF_DH)),
        "cache_mla_ckv": nrm((DEC_BATCH, DEPTH, PAST_LEN, MLA_KV_LORA)),
        "cache_mla_krope": nrm((DEC_BATCH, DEPTH, PAST_LEN, MLA_ROPE)),
        "state_ret": nrm((DEC_BATCH, DEPTH, 2, RET_HEADS, RET_DK, RET_DV), 0.5),
        "c": nrm((DEC_BATCH, D_MODEL)),
        "c_ctx": nrm((D_MODEL,)),
        "w_mod": nrm((DEPTH, D_MODEL, 6 * D_MODEL), 0.5 * D_MODEL ** -0.5),
        "b_mod": nrm((DEPTH, 6 * D_MODEL), 0.01),
        "norm1": gain((DEPTH, D_MODEL)),
        "norm2": gain((DEPTH, D_MODEL)),
        "w_in": nrm((DEPTH, D_MODEL, D_IN), D_MODEL ** -0.5),
        "diff_qn": gain((DEPTH, DIFF_DH)),
        "diff_kn": gain((DEPTH, DIFF_DH)),
        "diff_lambda": nrm((DEPTH, 4, DIFF_DH), 0.1),
        "diff_subln": gain((DEPTH, 2 * DIFF_DH)),
        "ret_decay": decay_logit[None, None, :] + nrm((DEPTH, 2, RET_HEADS), 0.1),
        "ret_gn": gain((DEPTH, RET_DV)),
        "mla_qa_norm": gain((DEPTH, MLA_Q_LORA)),
        "w_mla_qb": nrm((DEPTH, MLA_Q_LORA, MLA_HEADS * (MLA_NOPE + MLA_ROPE)), MLA_Q_LORA ** -0.5),
        "mla_kva_norm": gain((DEPTH, MLA_KV_LORA)),
        "w_mla_kvb": nrm((DEPTH, MLA_KV_LORA, MLA_HEADS * (MLA_NOPE + MLA_V)), MLA_KV_LORA ** -0.5),
        "mla_qn": gain((DEPTH, MLA_NOPE + MLA_ROPE)),
        "mla_kn": gain((DEPTH, MLA_NOPE + MLA_ROPE)),
        "w_branch": nrm((DEPTH, N_BRANCH, BRANCH_W, D_MODEL), BRANCH_W ** -0.5),
        "w_out": nrm((DEPTH, D_MODEL, D_MODEL), D_MODEL ** -0.5),
        "w_up": nrm((DEPTH, D_MODEL, D_FF), D_MODEL ** -0.5),
        "w_down": nrm((DEPTH, D_FF, D_MODEL), D_FF ** -0.5),
    }


def reference(x_prompt, x_sample, cache_diff_k, cache_diff_v, cache_mla_ckv, cache_mla_krope, state_ret,
              c, c_ctx, w_mod, b_mod, norm1, norm2, w_in, diff_qn, diff_kn, diff_lambda, diff_subln,
              ret_decay, ret_gn, mla_qa_norm, w_mla_qb, mla_kva_norm, w_mla_kvb, mla_qn, mla_kn,
              w_branch, w_out, w_up, w_down):
    def layer_weights(l):
        return {
            'w_mod': w_mod[l], 'b_mod': b_mod[l], 'norm1': norm1[l], 'norm2': norm2[l],
            'w_in': w_in[l], 'diff_qn': diff_qn[l], 'diff_kn': diff_kn[l],
            'diff_lambda': diff_lambda[l], 'diff_subln': diff_subln[l],
            'ret_decay': ret_decay[l], 'ret_gn': ret_gn[l],
            'mla_qa_norm': mla_qa_norm[l], 'w_mla_qb': w_mla_qb[l], 'mla_kva_norm': mla_kva_norm[l],
            'w_mla_kvb': w_mla_kvb[l], 'mla_qn': mla_qn[l], 'mla_kn': mla_kn[l],
            'w_branch': w_branch[l], 'w_out': w_out[l], 'w_up': w_up[l], 'w_down': w_down[l],
        }

    def lambda_init(l):
        return 0.8 - 0.6 * math.exp(-0.3 * l)

    xp = x_prompt
    ks_, vs_, ckvs_, krs_, rs_ = [], [], [], [], []
    for l in range(DEPTH):
        xp, ctx_out = block(xp, c_ctx[None, :], layer_weights(l), lambda_init(l), None, None)
        ks_.append(ctx_out[0]); vs_.append(ctx_out[1]); ckvs_.append(ctx_out[2])
        krs_.append(ctx_out[3]); rs_.append(ctx_out[4])
    y_prompt = xp
    new_diff_k = jnp.stack(ks_, axis=1)
    new_diff_v = jnp.stack(vs_, axis=1)
    new_mla_ckv = jnp.stack(ckvs_, axis=1)
    new_mla_krope = jnp.stack(krs_, axis=1)
    new_state_ret = jnp.stack(rs_, axis=1)

    rope = axial_rope_tables(x_sample.shape[1])
    xs = x_sample
    for l in range(DEPTH):
        ctx = (cache_diff_k[:, l], cache_diff_v[:, l], cache_mla_ckv[:, l], cache_mla_krope[:, l], state_ret[:, l])
        xs, _ = block(xs, c, layer_weights(l), lambda_init(l), rope, ctx)
    y_sample = xs

    return (y_prompt, y_sample, new_diff_k, new_diff_v, new_mla_ckv, new_mla_krope, new_state_ret)
```

```python
import math
import os
import numpy as np
import concourse.bass as bass
import concourse.mybir as mybir
from concourse.bass_utils import run_bass_kernel_spmd

F32 = mybir.dt.float32
BF16 = mybir.dt.bfloat16
AF = mybir.ActivationFunctionType
ALU = mybir.AluOpType
AX = mybir.AxisListType

D = 1024
DEPTH = 2
EPS = 1e-6
D_IN = 6848
NPH_A = 3776
GRID_W = 64


class Buf:
    __slots__ = ("name", "w", "r")

    def __init__(self, name):
        self.name = name
        self.w = None
        self.r = []


class Op:
    __slots__ = ("eng", "fn", "deps", "dma", "sig", "need", "selfsig", "odeps", "reorder")

    def __init__(self, eng, fn, dma):
        self.eng = eng
        self.fn = fn
        self.deps = []
        self.dma = dma
        self.sig = None
        self.need = False
        self.selfsig = False
        self.odeps = []
        self.reorder = False


class T:
    def __init__(self, h, buf):
        self.h = h
        self.b = buf
        self.tok = None

    def __getitem__(self, k):
        return self.h[k]


ENGS = ("pe", "act", "dve", "pool", "sp")
NDMASEM = 16


class KB:
    def __init__(self):
        self.nc = bass.Bass("TRN2", target_bir_lowering=False)
        nc = self.nc
        self.e = {"pe": nc.tensor, "act": nc.scalar, "dve": nc.vector, "pool": nc.gpsimd, "sp": nc.sync}
        self.ops = {k: [] for k in ENGS}
        self.allops = []
        self.csem = {k: nc.alloc_semaphore("c_" + k) for k in ENGS}
        self.dsem = {q: [nc.alloc_semaphore(f"d_{q}{i}") for i in range(NDMASEM)] for q in ("sp", "pool", "act")}
        self.dcnt = {q: 0 for q in self.dsem}
        self.dlast = {}
        self.bar = None
        self.bar_seen = {k: None for k in ENGS}
        self.sb_off = 16512
        self.sb_mark = []
        self.nbuf = 0
        self.names = 0
        self.cache = {}
        self.reorder = False

    def buf(self, name="b"):
        self.nbuf += 1
        return Buf(f"{name}{self.nbuf}")

    def sb(self, shape, dt, name="t"):
        self.names += 1
        nbytes = int(np.prod(shape[1:])) * (4 if dt == F32 else 2)
        nbytes = (nbytes + 63) // 64 * 64
        h = self.nc.alloc_sbuf_tensor_at(f"{name}_{self.names}", list(shape), dt, offset=self.sb_off)
        self.sb_off += nbytes
        assert self.sb_off <= 229344, f"SBUF overflow {self.sb_off}"
        return T(h, self.buf(name))

    def push(self):
        self.sb_mark.append((self.sb_off, dict(self.cache)))

    def pop(self):
        self.sb_off, self.cache = self.sb_mark.pop()

    def tile(self, name, shape, dt):
        t = self.cache.get(name)
        if t is None:
            t = self.sb(shape, dt, name)
            self.cache[name] = t
        return t

    def dram(self, name, shape, dt, kind="Internal"):
        return self.nc.dram_tensor(name, list(shape), dt, kind=kind).ap()

    def op(self, eng, fn, R=(), W=(), dma=False):
        o = Op(eng, fn, dma)
        o.reorder = self.reorder
        deps = {}
        toks = [t.tok for t in R if isinstance(t, T) and t.tok is not None]
        if toks:
            W = list(W) + toks
        for t in R:
            b = t.b if isinstance(t, T) else t
            if b.w is not None:
                deps[id(b.w)] = (b.w, "raw")
        for t in W:
            b = t.b if isinstance(t, T) else t
            if b.w is not None:
                deps[id(b.w)] = (b.w, "waw")
            for r in b.r:
                if id(r) not in deps:
                    deps[id(r)] = (r, "war")
        for d, kind in deps.values():
            if d is o:
                continue
            if not d.dma and not dma and d.eng == eng:
                if eng == "pe":
                    o.odeps.append(d)
                    continue
            o.deps.append(d)
        if self.bar is not None and self.bar_seen[eng] is not self.bar:
            o.deps.append(self.bar)
            self.bar_seen[eng] = self.bar
        if dma:
            q = eng
            i = self.dcnt[q] % (2 if q == "pool" else NDMASEM)
            self.dcnt[q] += 1
            s = self.dsem[q][i]
            prev = self.dlast.get((q, i))
            if prev is not None:
                o.deps.append(prev[0])
                cnt = prev[1] + 1
            else:
                cnt = 1
            o.sig = (s, 16 * cnt)
            self.dlast[(q, i)] = (o, cnt)
            o.need = True
        for d in o.deps:
            d.need = True
        for t in R:
            b = t.b if isinstance(t, T) else t
            b.r.append(o)
        for t in W:
            b = t.b if isinstance(t, T) else t
            b.w = o
            b.r = []
        self.ops[eng].append(o)
        self.allops.append(o)
        return o

    def barrier(self):
        deps = []
        for k in ENGS:
            for o in reversed(self.ops[k]):
                if not o.dma:
                    deps.append(o)
                    break
        for (q, i), (o, c) in self.dlast.items():
            deps.append(o)
        b = Op("sp", lambda e: e.sem_inc(self.csem["sp"], 1), False)
        b.deps = [d for d in deps]
        for d in deps:
            d.need = True
        b.need = True
        b.selfsig = True
        self.ops["sp"].append(b)
        self.allops.append(b)
        self.bar = b

    def dma(self, out, in_, R=(), W=(), q="sp", **kw):
        return self.op(q, lambda e: e.dma_start(out=out, in_=in_, **kw), R, W, dma=True)

    def mm(self, out, lhsT, rhs, start, stop, R=(), W=()):
        return self.op("pe", lambda e: e.matmul(out, lhsT=lhsT, rhs=rhs, start=start, stop=stop), R, W)

    def tr(self, out, in_, ident, R=(), W=()):
        return self.op("pe", lambda e: e.transpose(out=out, in_=in_, identity=ident), R, W)

    def act(self, out, in_, func, R=(), W=(), **kw):
        return self.op("act", lambda e: e.activation(out=out, in_=in_, func=func, **kw), R, W)

    def tt(self, eng, out, in0, in1, op, R=(), W=()):
        return self.op(eng, lambda e: e.tensor_tensor(out=out, in0=in0, in1=in1, op=op), R, W)

    def ts(self, eng, out, in0, s1, s2, op0, op1=None, R=(), W=()):
        if op1 is None:
            return self.op(eng, lambda e: e.tensor_scalar(out=out, in0=in0, scalar1=s1, scalar2=None, op0=op0), R, W)
        return self.op(eng, lambda e: e.tensor_scalar(out=out, in0=in0, scalar1=s1, scalar2=s2, op0=op0, op1=op1), R, W)

    def stt(self, eng, out, in0, scalar, in1, op0, op1, R=(), W=()):
        return self.op(eng, lambda e: e.scalar_tensor_tensor(out=out, in0=in0, scalar=scalar, in1=in1, op0=op0, op1=op1), R, W)

    def cp(self, eng, out, in_, R=(), W=()):
        if eng == "act":
            return self.op("act", lambda e: e.activation(out=out, in_=in_, func=AF.Identity), R, W)
        return self.op(eng, lambda e: e.tensor_copy(out=out, in_=in_), R, W)

    def red(self, eng, out, in_, R=(), W=()):
        return self.op(eng, lambda e: e.tensor_reduce(out=out, in_=in_, axis=AX.X, op=ALU.add), R, W)

    def memset(self, eng, ap, val, W=()):
        return self.op(eng, lambda e: e.memset(ap, val), (), W)

    def sched_seg(self, ops):
        idx = {id(o): i for i, o in enumerate(ops)}
        fin = {}
        per = {kx: [o for o in ops if o.eng == kx] for kx in ENGS}
        free = {kx: 0.0 for kx in ENGS}
        DUR = {"pe": 0.3, "act": 0.6, "dve": 0.5, "pool": 0.6, "sp": 0.05}
        out = []
        Wn = 40
        nleft = len(ops)
        while nleft:
            best = None
            for kx in ENGS:
                lst = per[kx]
                fk = free[kx]
                for j in range(min(Wn, len(lst))):
                    o = lst[j]
                    t = fk
                    ok = True
                    for d in o.deps:
                        f = fin.get(id(d))
                        if f is None:
                            if id(d) in idx:
                                ok = False
                                break
                            continue
                        f += 0.2
                        if f > t:
                            t = f
                    if ok:
                        for d in o.odeps:
                            f = fin.get(id(d))
                            if f is None:
                                if id(d) in idx:
                                    ok = False
                                    break
                                continue
                            if f > t:
                                t = f
                    if not ok:
                        continue
                    key = (t, idx[id(o)])
                    if best is None or key < best[0]:
                        best = (key, kx, j, o, t)
                    if t <= fk:
                        break
            assert best is not None, "scheduler stuck"
            _, kx, j, o, t = best
            per[kx].pop(j)
            if o.dma:
                occ = 0.8 if kx == "pool" else 0.05
                fin[id(o)] = t + occ + 2.5
            else:
                occ = DUR[kx]
                fin[id(o)] = t + occ
            free[kx] = t + occ
            out.append(o)
            nleft -= 1
        return out

    def finalize(self):
        segs = []
        cur = []
        for o in self.allops:
            if o.selfsig:
                segs.append((cur, o)); cur = []
            else:
                cur.append(o)
        if cur:
            segs.append((cur, None))
        bars = set(id(b) for _, b in segs if b is not None)
        final = {kx: [] for kx in ENGS}
        lastdma = {}
        prevbar = None
        order_all = []
        for ops, bar in segs:
            for o in ops:
                o.deps = [d for d in o.deps if id(d) not in bars]
            order = self.sched_seg(ops) if any(o.reorder for o in ops) else ops
            seen = set()
            for o in order:
                if o.eng not in seen:
                    seen.add(o.eng)
                    if prevbar is not None:
                        o.deps.append(prevbar)
                final[o.eng].append(o)
                order_all.append(o)
                if o.dma:
                    lastdma[o.sig[0].num] = o
            if bar is not None:
                deps = []
                for kx in ENGS:
                    for o in reversed(final[kx]):
                        if not o.dma:
                            deps.append(o)
                            break
                deps += list(lastdma.values())
                bar.deps = deps
                for d in deps:
                    d.need = True
                final["sp"].append(bar)
                order_all.append(bar)
                prevbar = bar
        self.ops = final
        self.allops = order_all

    def emit(self):
        self.finalize()
        cnt = {k: 0 for k in ENGS}
        for k_ in ENGS:
            for o in self.ops[k_]:
                if o.dma:
                    continue
                if o.need:
                    cnt[o.eng] += 1
                    o.sig = (self.csem[o.eng], cnt[o.eng])
        for k in ENGS:
            eng = self.e[k]
            waited = {}
            for o in self.ops[k]:
                for d in o.deps:
                    s, v = d.sig
                    key = s.num
                    if waited.get(key, 0) >= v:
                        continue
                    waited[key] = v
                    eng.wait_ge(s, v)
                ins = o.fn(eng)
                if o.dma:
                    ins.then_inc(o.sig[0], 16)
                elif o.need and not o.selfsig:
                    ins.then_inc(o.sig[0], 1)


def rstd_ops(k, ss, n_inv, mh, R=(), W=()):
    k.ts("pool", ss, ss, n_inv, EPS, ALU.mult, ALU.add, R=R, W=W)
    k.tt("pool", ss, ss, mh, ALU.pow, R=list(R) + list(W), W=W)


W_SPECS = [
    ("w_mod", [DEPTH, D, 6 * D]), ("b_mod", [DEPTH, 6 * D]), ("norm1", [DEPTH, D]), ("norm2", [DEPTH, D]),
    ("w_in", [DEPTH, D, D_IN]), ("diff_qn", [DEPTH, 64]), ("diff_kn", [DEPTH, 64]), ("diff_lambda", [DEPTH, 4, 64]),
    ("diff_subln", [DEPTH, 128]), ("ret_decay", [DEPTH, 2, 4]), ("ret_gn", [DEPTH, 128]),
    ("mla_qa_norm", [DEPTH, 384]), ("w_mla_qb", [DEPTH, 384, 768]), ("mla_kva_norm", [DEPTH, 256]),
    ("w_mla_kvb", [DEPTH, 256, 1024]), ("mla_qn", [DEPTH, 192]), ("mla_kn", [DEPTH, 192]),
    ("w_branch", [DEPTH, 3, 512, D]), ("w_out", [DEPTH, D, D]), ("w_up", [DEPTH, D, 4 * D]), ("w_down", [DEPTH, 4 * D, D]),
]


def build(NS, NP, PAST, dbg=(), stop=99):
    k = KB()
    nc = k.nc
    NT = NS + 2 * NP
    NTT = NT // 128
    NK = NT + PAST
    L = DEPTH
    ein = lambda n, s: k.dram(n, s, F32, kind="ExternalInput")
    eout = lambda n, s: k.dram(n, s, F32, kind="ExternalOutput")
    x_all = ein("x_all", [NT, D]); cvec = ein("cvec", [2, D])
    cdk = ein("cdk", [L, PAST, 512]); cdv = ein("cdv", [L, PAST, 512])
    cckv = ein("cckv", [L, PAST, 256]); ckr = ein("ckr", [L, PAST, 64]); sret = ein("sret", [L, 2, 4, 64, 128])
    Wt = {n: ein(n, s) for n, s in W_SPECS}
    rope_tab = ein("rope_tab", [NT, 128]); retc = ein("retc", [4, 128, 128]); retcol = ein("retcol", [128, 4])
    y_all = eout("y_all", [NT, D])
    ndk = eout("ndk", [2, L, NP, 512]); ndv = eout("ndv", [2, L, NP, 512])
    nckv = eout("nckv", [2, L, NP, 256]); nkr = eout("nkr", [2, L, NP, 64]); nsr = eout("nsr", [2, L, 2, 4, 64, 128])

    def scr(n, s, dt):
        return k.dram(n, s, dt, kind=("ExternalOutput" if n in dbg else "Internal"))
    modrow = scr("modrow", [2, 6 * D], F32)
    hTs = scr("hTs", [NTT, 128, 8, 128], BF16); h2Ts = scr("h2Ts", [NTT, 128, 8, 128], BF16)
    dqT = scr("dqT", [4, 128, NT], BF16); dkT = scr("dkT", [4, 128, NK], BF16); dvs = scr("dvs", [NK, 512], BF16)
    rqT = scr("rqT", [2, 128, NT], BF16); rkT = scr("rkT", [2, 128, NT], BF16); QQ = scr("QQ", [4, 128, NT], BF16)
    kks = scr("kks", [NT, 512], BF16); rvs = scr("rvs", [NT, 512], BF16); rgs = scr("rgs", [NT, 512], F32)
    mqTn = scr("mqTn", [4, 128, NT], BF16); mqTr = scr("mqTr", [2, 128, NT], BF16)
    mkTn = scr("mkTn", [4, 128, NK], BF16); mkTr = scr("mkTr", [2, 128, NK], BF16); mvs = scr("mvs", [NK, 512], BF16)
    oT = scr("oT", [3, 4, 128, NT], BF16)
    xa = scr("xa", [NT, D], F32); xb = scr("xb", [NT, D], F32)

    seqs = [dict(t0=0, nt=NS // 128, ctx=True, var=0, pi=None),
            dict(t0=NS // 128, nt=NP // 128, ctx=False, var=1, pi=0),
            dict(t0=(NS + NP) // 128, nt=NP // 128, ctx=False, var=1, pi=1)]

    def seq_of(t):
        for s in seqs:
            if s["t0"] <= t < s["t0"] + s["nt"]:
                return s

    PS = []
    for i in range(8):
        h = nc.alloc_psum_tensor(f"ps{i}", [128, 512], F32)
        PS.append(T(h, k.buf("ps")))
        PS[-1].tok = k.buf("pstok")
    psb = lambda i: PS[i][:].bitcast(BF16)

    identf = k.sb([128, 128], F32, "identf"); ident = k.sb([128, 128], BF16, "ident")
    mh = k.sb([128, 8], F32, "mh")
    k.memset("pool", identf[:], 0.0, W=[identf])
    k.op("pool", lambda e: e.affine_select(out=identf[:], in_=identf[:], pattern=[[-1, 128]], compare_op=ALU.not_equal,
                                           fill=1.0, base=0, channel_multiplier=1), R=[identf], W=[identf])
    k.cp("dve", ident[:], identf[:], R=[identf], W=[ident])
    k.memset("pool", mh[:], -0.5, W=[mh])
    rc = k.sb([128, 4, 128], F32, "retc")
    k.dma(rc[:], retc.rearrange("c p i -> p c i"), W=[rc])
    rcol = k.sb([128, 4], F32, "retcol")
    k.dma(rcol[:], retcol, W=[rcol])
    MT = k.sb([128, 4, 128], F32, "MT"); qdc = k.sb([128, 4, 2], F32, "qdc"); kdc = k.sb([128, 4, 2], F32, "kdc")
    decC = k.sb([128, 4, 128], F32, "decC"); lg = k.sb([128, 2, 4], F32, "lg"); neglam = k.sb([128, 1], F32, "neglam")

    def rstd(ss, n_inv, extraR=()):
        k.ts("pool", ss[:], ss[:], n_inv, EPS, ALU.mult, ALU.add, R=[ss] + list(extraR), W=[ss])
        k.tt("pool", ss[:], ss[:], mh[:, 0:ss.h.shape[1]] if len(ss.h.shape) == 2 else mh[:], ALU.pow, R=[ss, mh], W=[ss])

    def wload(dst, src_ap, nchunk=1):
        n = src_ap.shape[-1]
        step = max(256, ((n + nchunk - 1) // nchunk + 255) // 256 * 256)
        for c0 in range(0, n, step):
            c1 = min(n, c0 + step)
            k.dma(dst[:, :, c0:c1], src_ap[:, c0:c1].rearrange("(kc p) n -> p kc n", p=128), W=[dst], q="pool")

    def bload(dst_ap, src_ap, W):
        k.dma(dst_ap, src_ap.partition_broadcast(128), W=W)

    def transposes(src_aps, bank, dst, R, n_out_part=128):
        pb = psb(bank)
        n = len(src_aps)
        for i, a in enumerate(src_aps):
            k.tr(pb[:, i * 128:(i + 1) * 128], a, ident[:], R=list(R) + [ident], W=[PS[bank]])
        k.cp("dve", dst[:].rearrange("p n c -> p (n c)"), pb[:, 0:n * 128], R=[PS[bank]], W=[dst])

    def layer_consts(l):
        k.push()
        lam_init = 0.8 - 0.6 * math.exp(-0.3 * l)
        dl = k.sb([128, 4, 64], F32); pr = k.sb([128, 2, 64], F32); sm = k.sb([128, 2], F32)
        bload(dl[:].rearrange("p a d -> p (a d)"), Wt["diff_lambda"][l].rearrange("a d -> (a d)"), W=[dl])
        dl4 = dl[:].rearrange("p (a b) d -> p a b d", b=2)
        k.tt("dve", pr[:], dl4[:, :, 0, :], dl4[:, :, 1, :], ALU.mult, R=[dl], W=[pr])
        k.red("dve", sm[:], pr[:], R=[pr], W=[sm])
        k.act(sm[:], sm[:], AF.Exp, R=[sm], W=[sm])
        k.stt("dve", neglam[:], sm[:, 1:2], -lam_init, sm[:, 0:1], ALU.add, ALU.subtract, R=[sm], W=[neglam])
        bload(lg[:].rearrange("p a h -> p (a h)"), Wt["ret_decay"][l].rearrange("a h -> (a h)"), W=[lg])
        k.act(lg[:], lg[:], AF.Exp, R=[lg], W=[lg], scale=-1.0)
        k.act(lg[:], lg[:], AF.Ln, R=[lg], W=[lg], bias=1.0)
        k.ts("dve", lg[:], lg[:], -1.0, None, ALU.mult, R=[lg], W=[lg])
        tmp = k.sb([128, 4, 128], F32); tmp2 = k.sb([128, 4, 128], F32)
        for h in range(4):
            k.ts("dve", tmp[:, h, :], rc[:, 0, :], lg[:, 0, h:h + 1], None, ALU.mult, R=[rc, lg], W=[tmp])
            k.ts("dve", tmp2[:, h, :], rc[:, 2, :], lg[:, 1, h:h + 1], None, ALU.mult, R=[rc, lg], W=[tmp2])
        k.act(tmp[:], tmp[:], AF.Exp, R=[tmp], W=[tmp])
        k.act(tmp2[:], tmp2[:], AF.Exp, R=[tmp2], W=[tmp2])
        k.tt("dve", tmp[:], tmp[:], rc[:, 1:2, :].to_broadcast([128, 4, 128]), ALU.mult, R=[tmp, rc], W=[tmp])
        k.tt("dve", tmp2[:], tmp2[:], rc[:, 3:4, :].to_broadcast([128, 4, 128]), ALU.mult, R=[tmp2, rc], W=[tmp2])
        k.tt("dve", MT[:], tmp[:], tmp2[:], ALU.add, R=[tmp, tmp2], W=[MT])
        for d in range(2):
            k.ts("dve", qdc[:, :, d], lg[:, d, :], rcol[:, d:d + 1], None, ALU.mult, R=[lg, rcol], W=[qdc])
            k.ts("dve", kdc[:, :, d], lg[:, d, :], rcol[:, 2 + d:3 + d], None, ALU.mult, R=[lg, rcol], W=[kdc])
        k.act(qdc[:], qdc[:], AF.Exp, R=[qdc], W=[qdc])
        k.act(kdc[:], kdc[:], AF.Exp, R=[kdc], W=[kdc])
        dc = k.sb([128, 4], F32)
        k.act(dc[0:64, :], lg[0:64, 0, :], AF.Exp, R=[lg], W=[dc], scale=128.0)
        k.act(dc[64:128, :], lg[64:128, 1, :], AF.Exp, R=[lg], W=[dc], scale=128.0)
        k.cp("dve", decC[:], dc[:].unsqueeze(2).to_broadcast([128, 4, 128]), R=[dc], W=[decC])
        k.pop()
        return lam_init

    def phase_mod(l):
        k.push()
        cT = k.sb([128, 2, 8], F32); sT = k.sb([128, 2, 8], F32); rep = k.sb([128, 16, 128], BF16)
        for v in range(2):
            k.dma(cT[:, v, :], cvec[v].rearrange("(kc p) -> p kc", p=128), W=[cT], allow_slow_non_contiguous=True)
        k.act(sT[:], cT[:], AF.Silu, R=[cT], W=[sT])
        k.cp("dve", rep[:], sT[:].rearrange("p v c -> p (v c)").unsqueeze(2).to_broadcast([128, 16, 128]), R=[sT], W=[rep])
        wm = [k.sb([128, 8, 512], BF16) for _ in range(2)]
        bm = [k.sb([128, 512], F32) for _ in range(2)]
        res = [k.sb([128, 512], F32) for _ in range(2)]
        for n in range(12):
            w_, b_ = wm[n % 2], bm[n % 2]
            wload(w_, Wt["w_mod"][l][:, n * 512:(n + 1) * 512])
            bload(b_[:], Wt["b_mod"][l][n * 512:(n + 1) * 512], W=[b_])
            for v in range(2):
                bank = PS[v]
                for kc in range(8):
                    k.mm(bank[:], rep[:, v * 8 + kc, :], w_[:, kc, :], kc == 0, kc == 7, R=[rep, w_], W=[bank])
                r_ = res[v]
                k.tt("dve", r_[:], bank[:], b_[:], ALU.add, R=[bank, b_], W=[r_])
                k.dma(modrow[v:v + 1, n * 512:(n + 1) * 512], r_[0:1, :], R=[r_])
        k.pop()

    def rope_mul(zv, H, Gap, G, rt, out3, tmpA, tmpB, R, W, scale_bcast=None):
        gc = k.tile("rope_gc", [128, 64], F32); gs = k.tile("rope_gs", [128, 64], F32)
        k.tt("dve", gc[:], Gap, rt[:, 0:64], ALU.mult, R=[G, rt], W=[gc])
        g4 = Gap.rearrange("p (r h x) -> p r h x", r=2, h=2)
        s4 = rt[:, 64:128].rearrange("p (r h x) -> p r h x", r=2, h=2)
        gs4 = gs[:].rearrange("p (r h x) -> p r h x", r=2, h=2)
        k.tt("dve", gs4[:, :, 0, :], g4[:, :, 1, :], s4[:, :, 0, :], ALU.mult, R=[G, rt], W=[gs])
        k.tt("dve", gs4[:, :, 1, :], g4[:, :, 0, :], s4[:, :, 1, :], ALU.mult, R=[G, rt, gs], W=[gs])
        k.tt("dve", tmpA[:, 0:H, :], zv, gc[:].unsqueeze(1).to_broadcast([128, H, 64]), ALU.mult, R=list(R) + [gc], W=[tmpA])
        for r in range(2):
            for hf in range(2):
                zs = zv[:, :, r * 32 + (1 - hf) * 16: r * 32 + (1 - hf) * 16 + 16]
                o_ = tmpB[:, 0:H, r * 32 + hf * 16: r * 32 + hf * 16 + 16]
                g_ = gs[:, r * 32 + hf * 16: r * 32 + hf * 16 + 16].unsqueeze(1).to_broadcast([128, H, 16])
                k.tt("dve", o_, zs, g_, ALU.mult, R=list(R) + [gs], W=[tmpB])
        if scale_bcast is None:
            k.tt("pool", out3, tmpA[:, 0:H, :], tmpB[:, 0:H, :], ALU.add, R=[tmpA, tmpB], W=W)
        else:
            k.tt("pool", tmpA[:, 0:H, :], tmpA[:, 0:H, :], tmpB[:, 0:H, :], ALU.add, R=[tmpA, tmpB], W=[tmpA])
            k.tt("dve", out3, tmpA[:, 0:H, :], scale_bcast, ALU.mult, R=[tmpA] + list(R), W=W)

    def store_T(src, nblk, bank, dstT_ap_fn, R):
        tT = k.tile(f"stT{bank}_{nblk}", [128, nblk, 128], BF16)
        transposes([src[:, i * 128:(i + 1) * 128] for i in range(nblk)], bank, tT, R=[src])
        k.dma(dstT_ap_fn(), tT[:], R=[tT])

    def phase_A(l, xsrc):
        k.push()
        k.reorder = True
        wA = k.sb([128, 8, NPH_A], BF16, "wA"); wload(wA, Wt["w_in"][l][:, 0:NPH_A], nchunk=8)
        wqb = k.sb([128, 3, 768], BF16, "wqb"); wload(wqb, Wt["w_mla_qb"][l])
        wkvb = k.sb([128, 2, 1024], BF16, "wkvb"); wload(wkvb, Wt["w_mla_kvb"][l])
        A1, B1 = [], []
        n1 = k.sb([128, D], F32, "n1"); bload(n1[:], Wt["norm1"][l], W=[n1])
        for v in range(2):
            a = k.sb([128, D], F32, "A1"); b = k.sb([128, D], F32, "B1")
            bload(a[:], modrow[v, D:2 * D], W=[a]); bload(b[:], modrow[v, 0:D], W=[b])
            k.stt("dve", a[:], a[:], 1.0, n1[:], ALU.add, ALU.mult, R=[a, n1], W=[a])
            A1.append(a); B1.append(b)
        gq = k.sb([128, 64], F32, "gq"); bload(gq[:], Wt["diff_qn"][l], W=[gq])
        gk = k.sb([128, 64], F32, "gk"); bload(gk[:], Wt["diff_kn"][l], W=[gk])
        ones64 = k.sb([128, 64], F32, "ones"); k.memset("pool", ones64[:], 1.0, W=[ones64])
        eighth = k.sb([128, 64], F32, "eighth"); k.memset("pool", eighth[:], 0.125, W=[eighth])
        gqa = k.sb([128, 384], F32, "gqa"); bload(gqa[:], Wt["mla_qa_norm"][l], W=[gqa])
        gkva = k.sb([128, 256], F32, "gkva"); bload(gkva[:], Wt["mla_kva_norm"][l], W=[gkva])
        gmqn = k.sb([128, 192], F32, "gmqn"); bload(gmqn[:], Wt["mla_qn"][l], W=[gmqn])
        gmkn = k.sb([128, 192], F32, "gmkn"); bload(gmkn[:], Wt["mla_kn"][l], W=[gmkn])
        rt1 = k.sb([128, 128], F32, "rt1")
        k.memset("pool", rt1[:, 0:64], 1.0, W=[rt1]); k.memset("pool", rt1[:, 64:128], 0.0, W=[rt1])
        tA = k.sb([128, 8, 64], F32, "tA"); tB = k.sb([128, 8, 64], F32, "tB")

        def mla_kv(ckvb, krf, rt, col):
            ckvT = k.tile("ckvT", [128, 2, 128], BF16)
            transposes([ckvb[:, 0:128], ckvb[:, 128:256]], 5, ckvT, R=[ckvb])
            for c in range(2):
                for kc in range(2):
                    k.mm(PS[6 + c][:], ckvT[:, kc, :], wkvb[:, kc, c * 512:(c + 1) * 512], kc == 0, kc == 1, R=[ckvT, wkvb], W=[PS[6 + c]])
            sqk = k.tile("sqk", [128, 4, 128], F32); ssn = k.tile("ssn", [128, 4], F32)
            kv4 = [PS[6 + c][:].rearrange("p (h x d) -> p h x d", h=2, x=2) for c in range(2)]
            for c in range(2):
                k.act(sqk[:, 2 * c:2 * c + 2, :], kv4[c][:, :, 0, :], AF.Square, R=[PS[6 + c]], W=[sqk])
            k.red("dve", ssn[:], sqk[:], R=[sqk], W=[ssn])
            skr = k.tile("skr", [128, 1], F32); junk3 = k.tile("junk3", [128, 64], F32)
            k.act(junk3[:], krf[:], AF.Square, R=[krf], W=[junk3, skr], accum_out=skr[:])
            k.ts("dve", ssn[:], ssn[:], skr[:, 0:1], None, ALU.add, R=[ssn, skr], W=[ssn])
            rstd(ssn, 1.0 / 192)
            krg = k.tile("krg", [128, 1, 64], F32)
            rope_mul(krf[:].unsqueeze(1), 1, gmkn[:, 128:192], gmkn, rt, krg[:], tA, tB, R=[krf], W=[krg])
            mkn = k.tile("mkn", [128, 4, 128], BF16); mkr = k.tile("mkr", [128, 4, 64], BF16); mvb = k.tile("mvb", [128, 4, 128], BF16)
            for c in range(2):
                tn = k.tile("tn", [128, 2, 128], F32)
                k.tt("dve", tn[:], kv4[c][:, :, 0, :], gmkn[:, 0:128].unsqueeze(1).to_broadcast([128, 2, 128]), ALU.mult, R=[PS[6 + c], gmkn], W=[tn])
                k.tt("dve", mkn[:, 2 * c:2 * c + 2, :], tn[:], ssn[:, 2 * c:2 * c + 2].unsqueeze(2).to_broadcast([128, 2, 128]), ALU.mult, R=[tn, ssn], W=[mkn])
                k.cp("act", mvb[:, 2 * c:2 * c + 2, :], kv4[c][:, :, 1, :], R=[PS[6 + c]], W=[mvb])
            for h_ in range(4):
                k.ts("dve", mkr[:, h_, :], krg[:, 0, :], ssn[:, h_:h_ + 1], None, ALU.mult, R=[krg, ssn], W=[mkr])
            store_T(T(mkn.h[:].rearrange("p h d -> p (h d)"), mkn.b), 4, 5, lambda: mkTn[:, :, col:col + 128].rearrange("h p c -> p h c"), R=[mkn])
            store_T(T(mkr.h[:].rearrange("p h d -> p (h d)"), mkr.b), 2, 5, lambda: mkTr[:, :, col:col + 128].rearrange("h p c -> p h c"), R=[mkr])
            k.dma(mvs[col:col + 128, :], mvb[:].rearrange("p h d -> p (h d)"), R=[mvb])

        for ct in range(PAST // 128 if int(os.environ.get("A_CTX", "1")) else 0):
            col = NT + ct * 128
            rows = slice(ct * 128, (ct + 1) * 128)
            CP = int(os.environ.get("A_CTXP", "9"))
            ckb = k.tile(f"ckb{ct % 2}", [128, 512], BF16)
            k.dma(ckb[:], cdk[l, rows, :], W=[ckb], q="pool")
            if CP >= 1:
                store_T(ckb, 4, 5, lambda: dkT[:, :, col:col + 128].rearrange("h p c -> p h c"), R=[ckb])
            cvb = k.tile(f"cvb{ct % 2}", [128, 512], BF16)
            if CP >= 2:
                k.dma(cvb[:], cdv[l, rows, :], W=[cvb], q="pool")
                k.dma(dvs[col:col + 128, :], cvb[:], R=[cvb])
            cc = k.tile(f"ccb{ct % 2}", [128, 256], BF16)
            crf = k.tile(f"crf{ct % 2}", [128, 64], F32)
            if CP >= 3:
                k.dma(cc[:], cckv[l, rows, :], W=[cc], q="pool")
                k.dma(crf[:], ckr[l, rows, :], W=[crf])
            if CP >= 4:
                mla_kv(cc, crf, rt1, col)

        def loads(t):
            xt = k.tile(f"x{t % 2}", [128, D], F32); rt = k.tile(f"rt{t % 2}", [128, 128], F32)
            k.dma(xt[:], xsrc[t * 128:(t + 1) * 128, :], W=[xt])
            k.dma(rt[:], rope_tab[t * 128:(t + 1) * 128, :], W=[rt])

        loads(0)
        A_TILES = int(os.environ.get("A_TILES", "999")); A_PARTS = int(os.environ.get("A_PARTS", "99"))
        for t in range(min(NTT, A_TILES)):
            if t + 1 < NTT:
                loads(t + 1)
            s = seq_of(t); v = s["var"]; pi = s["pi"]
            lrows = slice((t - s["t0"]) * 128, (t - s["t0"] + 1) * 128)
            grow = slice(t * 128, (t + 1) * 128)
            xt = k.tile(f"x{t % 2}", [128, D], F32); rt = k.tile(f"rt{t % 2}", [128, 128], F32)
            junk = k.tile("junk", [128, D], BF16); ssx = k.tile("ssx", [128, 1], F32)
            k.act(junk[:], xt[:], AF.Square, R=[xt], W=[junk, ssx], accum_out=ssx[:])
            rstd(ssx, 1.0 / D)
            tmp = k.tile("htmp", [128, D], F32); hb = k.tile("hb", [128, D], BF16)
            k.stt("dve", tmp[:], xt[:], ssx[:, 0:1], A1[v][:], ALU.mult, ALU.mult, R=[xt, ssx, A1[v]], W=[tmp])
            k.tt("pool", hb[:], tmp[:], B1[v][:], ALU.add, R=[tmp, B1[v]], W=[hb])
            hT = k.tile(f"hT{t % 2}", [128, 8, 128], BF16)
            transposes([hb[:, i * 128:(i + 1) * 128] for i in range(8)], 4, hT, R=[hb])
            k.dma(hTs[t], hT[:], R=[hT])

            def zmm(c0, c1, bank):
                for kc in range(8):
                    k.mm(PS[bank][:, 0:c1 - c0], hT[:, kc, :], wA[:, kc, c0:c1], kc == 0, kc == 7, R=[hT, wA], W=[PS[bank]])

            def qknorm(bank, G, name, f32out):
                z3 = PS[bank][:].rearrange("p (h d) -> p h d", d=64)
                sq = k.tile("sq", [128, 512], F32); ss8 = k.tile("ss8", [128, 8], F32)
                k.act(sq[:], PS[bank][:], AF.Square, R=[PS[bank]], W=[sq])
                k.red("dve", ss8[:], sq[:].rearrange("p (h d) -> p h d", d=64), R=[sq], W=[ss8])
                rstd(ss8, 1.0 / 64)
                ob = k.tile(name + "b", [128, 512], BF16)
                sc = ss8[:].unsqueeze(2).to_broadcast([128, 8, 64])
                if f32out:
                    of = k.tile(name + "f", [128, 512], F32)
                    rope_mul(z3, 8, G[:], G, rt, of[:].rearrange("p (h d) -> p h d", d=64), tA, tB, R=[PS[bank], ss8], W=[of], scale_bcast=sc)
                    k.cp("act", ob[:], of[:], R=[of], W=[ob])
                    return ob, of
                rope_mul(z3, 8, G[:], G, rt, ob[:].rearrange("p (h d) -> p h d", d=64), tA, tB, R=[PS[bank], ss8], W=[ob], scale_bcast=sc)
                return ob, None

            if A_PARTS <= 0:
                continue
            zmm(0, 512, 0)
            qb, _ = qknorm(0, gq, "dq", False)
            store_T(qb, 4, 5, lambda: dqT[:, :, grow].rearrange("h p c -> p h c"), R=[qb])
            if A_PARTS <= 1:
                continue
            zmm(512, 1024, 1)
            API = int(os.environ.get("A_PI", "7"))
            kb_, kf_ = qknorm(1, gk, "dk", pi is not None and (API & 1))
            store_T(kb_, 4, 5, lambda: dkT[:, :, grow].rearrange("h p c -> p h c"), R=[kb_])
            if pi is not None and (API & 1):
                k.dma(ndk[pi, l, lrows, :], kf_[:], R=[kf_])
            if A_PARTS <= 2:
                continue
            zmm(1024, 1536, 2)
            dvb = k.tile("dvb", [128, 512], BF16)
            k.cp("act", dvb[:], PS[2][:], R=[PS[2]], W=[dvb])
            k.dma(dvs[grow, :], dvb[:], R=[dvb])
            if pi is not None and (API & 2):
                dvf = k.tile("dvf", [128, 512], F32)
                k.cp("act", dvf[:], PS[2][:], R=[PS[2]], W=[dvf])
                k.dma(ndv[pi, l, lrows, :], dvf[:], R=[dvf])
            if A_PARTS <= 3:
                continue
            zmm(1536, 2048, 3)
            z = PS[3]
            rqf = k.tile("rqf", [128, 4, 64], F32); rkf = k.tile("rkf", [128, 4, 64], F32)
            rope_mul(z[:, 0:256].rearrange("p (h d) -> p h d", d=64), 4, ones64[:], ones64, rt, rqf[:], tA, tB, R=[z], W=[rqf])
            rope_mul(z[:, 256:512].rearrange("p (h d) -> p h d", d=64), 4, eighth[:], eighth, rt, rkf[:], tA, tB, R=[z], W=[rkf])
            rqb = k.tile("rqb", [128, 256], BF16); rkb = k.tile("rkb", [128, 256], BF16)
            k.cp("act", rqb[:], rqf[:].rearrange("p h d -> p (h d)"), R=[rqf], W=[rqb])
            k.cp("act", rkb[:], rkf[:].rearrange("p h d -> p (h d)"), R=[rkf], W=[rkb])
            rq2 = k.tile("rq2", [128, 4, 2, 64], BF16); kk = k.tile("kk", [128, 4, 2, 64], BF16)
            for d_ in range(2):
                k.tt("dve", rq2[:, :, d_, :], rqf[:], qdc[:, :, d_:d_ + 1].to_broadcast([128, 4, 64]), ALU.mult, R=[rqf, qdc], W=[rq2])
                k.tt("dve", kk[:, :, d_, :], rkf[:], kdc[:, :, d_:d_ + 1].to_broadcast([128, 4, 64]), ALU.mult, R=[rkf, kdc], W=[kk])
            store_T(rqb, 2, 5, lambda: rqT[:, :, grow].rearrange("h p c -> p h c"), R=[rqb])
            store_T(rkb, 2, 5, lambda: rkT[:, :, grow].rearrange("h p c -> p h c"), R=[rkb])
            store_T(T(rq2.h[:].rearrange("p h a d -> p (h a d)"), rq2.b), 4, 5, lambda: QQ[:, :, grow].rearrange("h p c -> p h c"), R=[rq2])
            k.dma(kks[grow, :], kk[:].rearrange("p h a d -> p (h a d)"), R=[kk])
            if A_PARTS <= 4:
                continue
            zmm(2048, 2560, 0)
            rvb = k.tile("rvb", [128, 512], BF16)
            k.cp("act", rvb[:], PS[0][:], R=[PS[0]], W=[rvb])
            k.dma(rvs[grow, :], rvb[:], R=[rvb])
            if A_PARTS <= 5:
                continue
            zmm(2560, 3072, 1)
            rgf = k.tile("rgf", [128, 512], F32)
            k.cp("act", rgf[:], PS[1][:], R=[PS[1]], W=[rgf])
            k.dma(rgs[grow, :], rgf[:], R=[rgf])
            if A_PARTS <= 6:
                continue
            zmm(3072, 3456, 2)
            z = PS[2]
            ssq = k.tile("ssq", [128, 1], F32); junk2 = k.tile("junk2", [128, 384], BF16)
            k.act(junk2[:], z[:, 0:384], AF.Square, R=[z], W=[junk2, ssq], accum_out=ssq[:])
            rstd(ssq, 1.0 / 384)
            qab = k.tile("qab", [128, 384], BF16)
            k.stt("dve", qab[:], z[:, 0:384], ssq[:, 0:1], gqa[:], ALU.mult, ALU.mult, R=[z, ssq, gqa], W=[qab])
            qaT = k.tile("qaT", [128, 3, 128], BF16)
            transposes([qab[:, i * 128:(i + 1) * 128] for i in range(3)], 5, qaT, R=[qab])
            mqn = k.tile("mqn", [128, 4, 128], BF16); mqr = k.tile("mqr", [128, 4, 64], BF16)
            for c in range(2):
                bk = PS[6 + c]
                for kc in range(3):
                    k.mm(bk[:, 0:384], qaT[:, kc, :], wqb[:, kc, c * 384:(c + 1) * 384], kc == 0, kc == 2, R=[qaT, wqb], W=[bk])
                sq = k.tile("sq", [128, 512], F32); ss2 = k.tile("ss2", [128, 2], F32)
                k.act(sq[:, 0:384], bk[:, 0:384], AF.Square, R=[bk], W=[sq])
                k.red("dve", ss2[:], sq[:, 0:384].rearrange("p (h d) -> p h d", d=192), R=[sq], W=[ss2])
                rstd(ss2, 1.0 / 192)
                tq = k.tile("tq", [128, 2, 192], F32); tqr = k.tile("tqr", [128, 2, 64], F32)
                k.tt("dve", tq[:], bk[:, 0:384].rearrange("p (h d) -> p h d", d=192), gmqn[:].unsqueeze(1).to_broadcast([128, 2, 192]), ALU.mult, R=[bk, gmqn], W=[tq])
                rope_mul(tq[:, :, 128:192], 2, ones64[:], ones64, rt, tqr[:], tA, tB, R=[tq], W=[tqr])
                k.tt("dve", mqn[:, 2 * c:2 * c + 2, :], tq[:, :, 0:128], ss2[:].unsqueeze(2).to_broadcast([128, 2, 128]), ALU.mult, R=[tq, ss2], W=[mqn])
                k.tt("dve", mqr[:, 2 * c:2 * c + 2, :], tqr[:], ss2[:].unsqueeze(2).to_broadcast([128, 2, 64]), ALU.mult, R=[tqr, ss2], W=[mqr])
            store_T(T(mqn.h[:].rearrange("p h d -> p (h d)"), mqn.b), 4, 5, lambda: mqTn[:, :, grow].rearrange("h p c -> p h c"), R=[mqn])
            store_T(T(mqr.h[:].rearrange("p h d -> p (h d)"), mqr.b), 2, 5, lambda: mqTr[:, :, grow].rearrange("h p c -> p h c"), R=[mqr])
            if A_PARTS <= 7:
                continue
            zmm(3456, 3776, 3)
            z = PS[3]
            ssk = k.tile("ssk", [128, 1], F32)
            k.act(junk2[:, 0:256], z[:, 0:256], AF.Square, R=[z], W=[junk2, ssk], accum_out=ssk[:])
            rstd(ssk, 1.0 / 256)
            ckvf = k.tile("ckvf", [128, 256], F32); krf = k.tile("krf", [128, 64], F32); ckvb = k.tile("ckvb", [128, 256], BF16)
            k.stt("dve", ckvf[:], z[:, 0:256], ssk[:, 0:1], gkva[:], ALU.mult, ALU.mult, R=[z, ssk, gkva], W=[ckvf])
            k.cp("act", krf[:], z[:, 256:320], R=[z], W=[krf])
            k.cp("act", ckvb[:], ckvf[:], R=[ckvf], W=[ckvb])
            if pi is not None and (API & 4):
                k.dma(nckv[pi, l, lrows, :], ckvf[:], R=[ckvf])
                k.dma(nkr[pi, l, lrows, :], krf[:], R=[krf])
            mla_kv(ckvb, krf, rt, t * 128)
        k.pop()
        k.reorder = False

    def phase_ret(l):
        for s in seqs:
            k.push()
            nch = s["nt"]; t0 = s["t0"]; pi = s["pi"]
            gn = k.sb([128, 128], F32, "gn"); bload(gn[:], Wt["ret_gn"][l], W=[gn])
            rv = k.sb([128, nch, 512], BF16, "rv")
            k.dma(rv[:], rvs[t0 * 128:(t0 + nch) * 128, :].rearrange("(c p) f -> p c f", p=128), W=[rv])
            U = k.sb([128, nch, 512], F32, "U"); RR = k.sb([128, nch, 512], BF16, "RR"); S = k.sb([128, 512], F32, "S")
            Ub = [k.buf() for _ in range(nch)]; RRf = [k.buf() for _ in range(nch)]; RRb = [k.buf() for _ in range(nch)]
            Sf = k.buf(); Sb = k.buf()
            for c in range(nch):
                kkt = k.tile(f"kkt{c % 2}", [128, 512], BF16)
                k.dma(kkt[:], kks[(t0 + c) * 128:(t0 + c + 1) * 128, :], W=[kkt])
                bank = PS[c % 2]
                for h in range(4):
                    hs = slice(h * 128, (h + 1) * 128)
                    k.mm(bank[:, hs], kkt[:, hs], rv[:, c, hs], True, True, R=[kkt, rv], W=[bank])
                k.cp("act", U[:, c, :], bank[:], R=[bank], W=[Ub[c]])
            RS = int(os.environ.get("R_STOP", "9"))
            if RS < 1:
                k.pop(); k.barrier(); continue
            if s["ctx"]:
                for d in range(2):
                    k.dma(S[64 * d:64 * d + 64, :].rearrange("k (h v) -> k h v", h=4), sret[l, d].rearrange("h k v -> k h v"), W=[Sf if d == 0 else Sb])
            else:
                k.memset("dve", S[0:64, :], 0.0, W=[Sf]); k.memset("pool", S[64:128, :], 0.0, W=[Sb])
            dec2 = decC[:].rearrange("p h e -> p (h e)")
            for c in range(nch):
                k.cp("dve", RR[0:64, c, :], S[0:64, :], R=[Sf], W=[RRf[c]])
                k.tt("dve", S[0:64, :], S[0:64, :], dec2[0:64, :], ALU.mult, R=[Sf, decC], W=[Sf])
                k.tt("dve", S[0:64, :], S[0:64, :], U[0:64, c, :], ALU.add, R=[Sf, Ub[c]], W=[Sf])
            for c in reversed(range(nch)):
                k.cp("act", RR[64:128, c, :], S[64:128, :], R=[Sb], W=[RRb[c]])
                k.tt("pool", S[64:128, :], S[64:128, :], dec2[64:128, :], ALU.mult, R=[Sb, decC], W=[Sb])
                k.tt("pool", S[64:128, :], S[64:128, :], U[64:128, c, :], ALU.add, R=[Sb, Ub[c]], W=[Sb])
            if RS < 2:
                k.pop(); k.barrier(); continue
            if pi is not None:
                k.dma(nsr[pi, l, 0].rearrange("h k v -> k h v"), S[0:64, :].rearrange("k (h v) -> k h v", h=4), R=[Sf])
                k.dma(nsr[pi, l, 1].rearrange("h k v -> k h v"), S[64:128, :].rearrange("k (h v) -> k h v", h=4), R=[Sb])
            if RS < 3:
                k.pop(); k.barrier(); continue
            for c in range(nch):
                t = t0 + c; cols = slice(t * 128, (t + 1) * 128)
                kT = k.tile(f"rkT{c % 2}", [128, 2, 128], BF16); qT = k.tile(f"rqT{c % 2}", [128, 2, 128], BF16)
                qq = k.tile(f"rqq{c % 2}", [128, 4, 128], BF16); rg = k.tile(f"rrg{c % 2}", [128, 512], F32)
                k.dma(kT[:], rkT[:, :, cols].rearrange("h p c -> p h c"), W=[kT])
                k.dma(qT[:], rqT[:, :, cols].rearrange("h p c -> p h c"), W=[qT])
                k.dma(qq[:], QQ[:, :, cols].rearrange("h p c -> p h c"), W=[qq])
                k.dma(rg[:], rgs[cols, :], W=[rg])
                bsr = [PS[2 - 2 * (c % 2)], PS[3 - 2 * (c % 2)]]; bo = PS[4 + c % 2]
                for h in range(4):
                    pp = slice(64 * (h % 2), 64 * (h % 2) + 64)
                    k.mm(bsr[h % 2][:, (h // 2) * 128:(h // 2 + 1) * 128], kT[pp, h // 2, :], qT[pp, h // 2, :], True, True, R=[kT, qT], W=[bsr[h % 2]])
                P = k.tile(f"rP{c % 2}", [128, 512], BF16)
                P4 = P[:].rearrange("p (a r i) -> p a r i", a=2, r=2)
                MT4 = MT[:].rearrange("p (a r) i -> p a r i", r=2)
                for r_ in range(2):
                    k.tt("dve", P4[:, :, r_, :], bsr[r_][:, 0:256].rearrange("p (a i) -> p a i", a=2), MT4[:, :, r_, :], ALU.mult, R=[bsr[r_], MT], W=[P])
                for h in range(4):
                    hs = slice(h * 128, (h + 1) * 128)
                    k.mm(bo[:, hs], P[:, hs], rv[:, c, hs], True, False, R=[P, rv], W=[bo])
                    k.mm(bo[:, hs], qq[:, h, :], RR[:, c, hs], False, True, R=[qq, RRf[c], RRb[c]], W=[bo])
                sq = k.tile("rsq", [128, 512], F32); ss4 = k.tile("rss4", [128, 4], F32)
                k.act(sq[:], bo[:], AF.Square, R=[bo], W=[sq])
                k.red("dve", ss4[:], sq[:].rearrange("p (h e) -> p h e", h=4), R=[sq], W=[ss4])
                rstd(ss4, 1.0 / 128)
                to = k.tile("rto", [128, 4, 128], F32)
                k.tt("dve", to[:], bo[:].rearrange("p (h e) -> p h e", h=4), ss4[:].unsqueeze(2).to_broadcast([128, 4, 128]), ALU.mult, R=[bo, ss4], W=[to])
                k.tt("dve", to[:], to[:], gn[:].unsqueeze(1).to_broadcast([128, 4, 128]), ALU.mult, R=[to, gn], W=[to])
                sg = k.tile("rsg", [128, 512], F32)
                k.act(sg[:], rg[:], AF.Silu, R=[rg], W=[sg])
                orb = k.tile("orb", [128, 512], BF16)
                k.tt("pool", orb[:], to[:].rearrange("p h e -> p (h e)"), sg[:], ALU.mult, R=[to, sg], W=[orb])
                store_T(orb, 4, 6, lambda: oT[1][:, :, cols].rearrange("h p c -> p h c"), R=[orb])
            k.pop()
            k.barrier()

    onesb = k.sb([128, 128], BF16, "onesb"); k.memset("dve", onesb[:], 1.0, W=[onesb])
    mh512 = k.sb([128, 512], F32, "mh512"); k.memset("dve", mh512[:], -0.5, W=[mh512])
    mapctr = [0]

    def attn_core(pairs_fn, V, nkt, QC, scale, R):
        Ob = PS[2 + mapctr[0] % 2]; Sb = PS[4 + mapctr[0] % 2]
        mapctr[0] += 1

        def st(kt):
            bank = PS[kt % 2]
            prs = pairs_fn(kt)
            for i, (a_, b_) in enumerate(prs):
                k.mm(bank[:, 0:QC], a_, b_, i == 0, i == len(prs) - 1, R=R, W=[bank])

        def pv(kt):
            bank = PS[kt % 2]
            PT = k.tile(f"PT{kt % 3}", [128, 512], BF16)
            k.act(PT[:, 0:QC], bank[:, 0:QC], AF.Exp, R=[bank], W=[PT], scale=scale)
            k.mm(Ob[:, 0:QC], V[:, kt, :], PT[:, 0:QC], kt == 0, kt == nkt - 1, R=[PT, V], W=[Ob])
            k.mm(Sb[:, 0:QC], onesb[:], PT[:, 0:QC], kt == 0, kt == nkt - 1, R=[PT, onesb], W=[Sb])
        st(0)
        for kt in range(nkt):
            if kt + 1 < nkt:
                st(kt + 1)
            pv(kt)
        recip = k.tile("recip", [128, 512], F32)
        k.op("dve", lambda e: e.reciprocal(out=recip[:, 0:QC], in_=Sb[:, 0:QC]), R=[Sb], W=[recip])
        return Ob, recip

    def phase_attn(l, lam_init):
        k.push()
        gsub = k.sb([128, 1], F32, "gsub")
        k.dma(gsub[:], Wt["diff_subln"][l].rearrange("(e o) -> e o", o=1), W=[gsub])
        k.ts("dve", gsub[:], gsub[:], 1.0 - lam_init, None, ALU.mult, R=[gsub], W=[gsub])
        for s in seqs:
            k.push()
            n = s["nt"] * 128; q0 = s["t0"] * 128
            nctx = PAST if s["ctx"] else 0
            nkt = (n + nctx) // 128
            QC = min(512, n)
            V = k.sb([128, nkt, 128], BF16, "V"); kTa = k.sb([128, nkt * 128], BF16, "kTa"); kTb = k.sb([128, nkt * 128], BF16, "kTb")

            def load_keys(dst, src2d):
                k.dma(dst[:, 0:n], src2d[:, q0:q0 + n], W=[dst])
                if nctx:
                    k.dma(dst[:, n:n + nctx], src2d[:, NT:NT + nctx], W=[dst])

            def load_V(src, h):
                k.dma(V[:, 0:n // 128, :], src[q0:q0 + n, h * 128:(h + 1) * 128].rearrange("(c p) e -> p c e", p=128), W=[V])
                if nctx:
                    k.dma(V[:, n // 128:nkt, :], src[NT:NT + nctx, h * 128:(h + 1) * 128].rearrange("(c p) e -> p c e", p=128), W=[V])

            for h in range(4):
                load_keys(kTa, dkT[h]); load_V(dvs, h)
                for qc in range(n // QC):
                    qs = slice(q0 + qc * QC, q0 + (qc + 1) * QC)
                    qp = []
                    for m in range(2):
                        nm = f"qTd{m}_{qc % 2}"
                        fresh = nm not in k.cache
                        t_ = k.tile(nm, [128, QC], BF16)
                        if fresh:
                            k.memset("dve", t_[64 * (1 - m):64 * (1 - m) + 64, :], 0.0, W=[t_])
                        k.dma(t_[64 * m:64 * m + 64, :], dqT[h][64 * m:64 * m + 64, qs], W=[t_])
                        qp.append(t_)
                    Oa = k.tile("Oa", [128, 512], F32)
                    for m in range(2):
                        pp = slice(64 * m, 64 * m + 64)
                        Ob, recip = attn_core(lambda kt: [(kTa[:, kt * 128:(kt + 1) * 128], qp[m][:, :])], V, nkt, QC, 0.125, R=[kTa, qp[m]])
                        if m == 0:
                            k.tt("dve", Oa[:, 0:QC], Ob[:, 0:QC], recip[:, 0:QC], ALU.mult, R=[Ob, recip], W=[Oa])
                    odt = k.tile("odt", [128, 512], F32); od = k.tile("od", [128, 512], F32)
                    k.tt("dve", odt[:, 0:QC], Ob[:, 0:QC], recip[:, 0:QC], ALU.mult, R=[Ob, recip], W=[odt])
                    k.stt("dve", od[:, 0:QC], odt[:, 0:QC], neglam[:, 0:1], Oa[:, 0:QC], ALU.mult, ALU.add, R=[odt, neglam, Oa], W=[od])
                    sq = k.tile("osq", [128, 512], F32); hi = k.tile("ohi", [128, 512], BF16); lo = k.tile("olo", [128, 512], BF16)
                    k.tt("pool", sq[:, 0:QC], od[:, 0:QC], od[:, 0:QC], ALU.mult, R=[od], W=[sq])
                    k.cp("act", hi[:, 0:QC], sq[:, 0:QC], R=[sq], W=[hi])
                    k.tt("dve", lo[:, 0:QC], sq[:, 0:QC], hi[:, 0:QC], ALU.subtract, R=[sq, hi], W=[lo])
                    k.mm(PS[6][:, 0:QC], onesb[:], hi[:, 0:QC], True, False, R=[onesb, hi], W=[PS[6]])
                    k.mm(PS[6][:, 0:QC], onesb[:], lo[:, 0:QC], False, True, R=[onesb, lo], W=[PS[6]])
                    rs = k.tile("orstd", [128, 512], F32)
                    k.ts("dve", rs[:, 0:QC], PS[6][:, 0:QC], 1.0 / 128, EPS, ALU.mult, ALU.add, R=[PS[6]], W=[rs])
                    k.tt("pool", rs[:, 0:QC], rs[:, 0:QC], mh512[:, 0:QC], ALU.pow, R=[rs, mh512], W=[rs])
                    ob = k.tile("odb", [128, 512], BF16)
                    k.stt("dve", ob[:, 0:QC], od[:, 0:QC], gsub[:, 0:1], rs[:, 0:QC], ALU.mult, ALU.mult, R=[od, gsub, rs], W=[ob])
                    k.dma(oT[0, h][:, qs], ob[:, 0:QC], R=[ob])
            for h in range(4):
                load_keys(kTa, mkTn[h]); load_keys(kTb, mkTr[h // 2]); load_V(mvs, h)
                pp = slice(64 * (h % 2), 64 * (h % 2) + 64)
                for qc in range(n // QC):
                    qs = slice(q0 + qc * QC, q0 + (qc + 1) * QC)
                    qTn = k.tile(f"qTn{qc % 2}", [128, QC], BF16)
                    nm = f"qTr{h % 2}_{qc % 2}"
                    fresh = nm not in k.cache
                    qTr = k.tile(nm, [128, QC], BF16)
                    if fresh:
                        k.memset("dve", qTr[64 * (1 - h % 2):64 * (1 - h % 2) + 64, :], 0.0, W=[qTr])
                    k.dma(qTn[:], mqTn[h][:, qs], W=[qTn]); k.dma(qTr[pp, :], mqTr[h // 2][pp, qs], W=[qTr])
                    Ob, recip = attn_core(lambda kt: [(kTa[:, kt * 128:(kt + 1) * 128], qTn[:, :]), (kTb[:, kt * 128:(kt + 1) * 128], qTr[:, :])],
                                          V, nkt, QC, 192 ** -0.5, R=[kTa, kTb, qTn, qTr])
                    omb = k.tile("omb", [128, 512], BF16)
                    k.tt("dve", omb[:, 0:QC], Ob[:, 0:QC], recip[:, 0:QC], ALU.mult, R=[Ob, recip], W=[omb])
                    k.dma(oT[2, h][:, qs], omb[:, 0:QC], R=[omb])
            k.pop()
            k.barrier()
        k.pop()

    def mod_tiles(l, which):
        out = {}
        for name, idx in which:
            out[name] = []
            for v in range(2):
                a = k.sb([128, D], F32, name)
                bload(a[:], modrow[v, idx * D:(idx + 1) * D], W=[a])
                out[name].append(a)
        return out

    def phase_merge(l, xsrc):
        k.push()
        k.reorder = True
        wg = k.sb([128, 8, 3072], BF16, "wg"); wload(wg, Wt["w_in"][l][:, NPH_A:D_IN], nchunk=6)
        wb = k.sb([128, 12, 1024], BF16, "wb")
        for i in range(3):
            wload(T(wb.h[:, 4 * i:4 * i + 4, :], wb.b), Wt["w_branch"][l, i])
        wo = k.sb([128, 8, 1024], BF16, "wo"); wload(wo, Wt["w_out"][l], nchunk=2)
        m = mod_tiles(l, [("G1", 2), ("B2", 3), ("A2", 4)])
        n2 = k.sb([128, D], F32, "n2"); bload(n2[:], Wt["norm2"][l], W=[n2])
        for v in range(2):
            a = m["A2"][v]
            k.stt("dve", a[:], a[:], 1.0, n2[:], ALU.add, ALU.mult, R=[a, n2], W=[a])

        def loads(t):
            cols = slice(t * 128, (t + 1) * 128)
            xt = k.tile(f"x{t % 2}", [128, D], F32); hT = k.tile(f"hT{t % 2}", [128, 8, 128], BF16)
            oTt = k.tile(f"oTt{t % 2}", [128, 12, 128], BF16)
            k.dma(xt[:], xsrc[cols, :], W=[xt]); k.dma(hT[:], hTs[t], W=[hT])
            for i in range(3):
                k.dma(oTt[:, 4 * i:4 * i + 4, :], oT[i][:, :, cols].rearrange("h p c -> p h c"), W=[oTt])
        loads(0)
        for t in range(NTT):
            if t + 1 < NTT:
                loads(t + 1)
            v = seq_of(t)["var"]; grow = slice(t * 128, (t + 1) * 128)
            xt = k.tile(f"x{t % 2}", [128, D], F32); hT = k.tile(f"hT{t % 2}", [128, 8, 128], BF16)
            oTt = k.tile(f"oTt{t % 2}", [128, 12, 128], BF16)
            mer = k.tile("mer", [128, D], F32)
            for nn in range(2):
                cs = slice(nn * 512, (nn + 1) * 512)
                for i in range(3):
                    j = nn * 3 + i
                    bg = PS[j % 2]; bo = PS[2 + j % 2]
                    for kc in range(8):
                        k.mm(bg[:], hT[:, kc, :], wg[:, kc, i * 1024 + nn * 512:i * 1024 + nn * 512 + 512], kc == 0, kc == 7, R=[hT, wg], W=[bg])
                    for kc in range(4):
                        k.mm(bo[:], oTt[:, i * 4 + kc, :], wb[:, i * 4 + kc, cs], kc == 0, kc == 3, R=[oTt, wb], W=[bo])
                    sig = k.tile(f"sig{j % 2}", [128, 512], F32)
                    k.act(sig[:], bg[:], AF.Sigmoid, R=[bg], W=[sig])
                    if i == 0:
                        k.tt("dve", mer[:, cs], sig[:], bo[:], ALU.mult, R=[sig, bo], W=[mer])
                    else:
                        tmpm = k.tile(f"tmpm{j % 2}", [128, 512], F32)
                        k.tt("dve", tmpm[:], sig[:], bo[:], ALU.mult, R=[sig, bo], W=[tmpm])
                        k.tt("pool", mer[:, cs], mer[:, cs], tmpm[:], ALU.add, R=[mer, tmpm], W=[mer])
            merb = k.tile("merb", [128, D], BF16)
            k.cp("act", merb[:], mer[:], R=[mer], W=[merb])
            mT = k.tile("mT", [128, 8, 128], BF16)
            transposes([merb[:, i * 128:(i + 1) * 128] for i in range(8)], 6, mT, R=[merb])
            x1 = k.tile("xout", [128, D], F32)
            for nn in range(2):
                cs = slice(nn * 512, (nn + 1) * 512)
                bo = PS[4 + nn]
                for kc in range(8):
                    k.mm(bo[:], mT[:, kc, :], wo[:, kc, cs], kc == 0, kc == 7, R=[mT, wo], W=[bo])
                tmpo = k.tile(f"tmpo{nn}", [128, 512], F32)
                k.tt("dve", tmpo[:], bo[:], m["G1"][v][:, cs], ALU.mult, R=[bo, m["G1"][v]], W=[tmpo])
                k.tt("pool", x1[:, cs], xt[:, cs], tmpo[:], ALU.add, R=[xt, tmpo], W=[x1])
            k.dma(xa[grow, :], x1[:], R=[x1])
            junk = k.tile("junk", [128, D], BF16); ssx = k.tile("ssx", [128, 1], F32)
            k.act(junk[:], x1[:], AF.Square, R=[x1], W=[junk, ssx], accum_out=ssx[:])
            rstd(ssx, 1.0 / D)
            tmp = k.tile("htmp", [128, D], F32); hb = k.tile("hb", [128, D], BF16)
            k.stt("dve", tmp[:], x1[:], ssx[:, 0:1], m["A2"][v][:], ALU.mult, ALU.mult, R=[x1, ssx, m["A2"][v]], W=[tmp])
            k.tt("pool", hb[:], tmp[:], m["B2"][v][:], ALU.add, R=[tmp, m["B2"][v]], W=[hb])
            h2T = k.tile("h2T", [128, 8, 128], BF16)
            transposes([hb[:, i * 128:(i + 1) * 128] for i in range(8)], 7, h2T, R=[hb])
            k.dma(h2Ts[t], h2T[:], R=[h2T])
        k.pop()
        k.reorder = False

    def phase_mlp(l, xdst):
        k.push()
        wu = k.sb([128, 8, 4096], BF16, "wu"); wload(wu, Wt["w_up"][l], nchunk=8)
        wd = k.sb([128, 32, 1024], BF16, "wd"); wload(wd, Wt["w_down"][l], nchunk=4)
        m = mod_tiles(l, [("G2", 5)])
        uT = k.sb([128, 32, 512], BF16, "uT"); uTb = [k.buf() for _ in range(32)]
        groups = []
        for s in seqs:
            ts_ = list(range(s["t0"], s["t0"] + s["nt"]))
            for i in range(0, len(ts_), 4):
                groups.append((s["var"], ts_[i:i + 4]))
        for gi, (v, tiles) in enumerate(groups):
            ncol = len(tiles) * 128
            h2 = k.tile("h2g", [128, 8, 512], BF16)
            for j, t in enumerate(tiles):
                k.dma(h2[:, :, j * 128:(j + 1) * 128], h2Ts[t], W=[h2])
            for f in range(32):
                bank = PS[f % 2]
                for kc in range(8):
                    k.mm(bank[:, 0:ncol], wu[:, kc, f * 128:(f + 1) * 128], h2[:, kc, 0:ncol], kc == 0, kc == 7, R=[wu, h2], W=[bank])
                r = k.tile(f"relu{f % 2}", [128, 512], F32)
                k.act(r[:, 0:ncol], bank[:, 0:ncol], AF.Relu, R=[bank], W=[r])
                k.tt("dve" if f % 2 else "pool", uT[:, f, 0:ncol], r[:, 0:ncol], r[:, 0:ncol], ALU.mult, R=[r], W=[uTb[f]])
            for j, t in enumerate(tiles):
                rows = slice(t * 128, (t + 1) * 128)
                x1 = k.tile(f"mx{j % 2}", [128, D], F32); xo = k.tile("mxo", [128, D], F32)
                k.dma(x1[:], xa[rows, :], W=[x1])
                for nn in range(2):
                    cs = slice(nn * 512, (nn + 1) * 512)
                    bank = PS[2 + (j * 2 + nn) % 4]
                    for f in range(32):
                        k.mm(bank[:], uT[:, f, j * 128:(j + 1) * 128], wd[:, f, cs], f == 0, f == 31, R=[uTb[f], wd], W=[bank])
                    tmp = k.tile(f"mtmp{nn}", [128, 512], F32)
                    k.tt("dve", tmp[:], bank[:], m["G2"][v][:, cs], ALU.mult, R=[bank, m["G2"][v]], W=[tmp])
                    k.tt("pool", xo[:, cs], x1[:, cs], tmp[:], ALU.add, R=[x1, tmp], W=[xo])
                k.dma(xdst[rows, :], xo[:], R=[xo])
        k.pop()

    nph = 0
    for l in range(L):
        lam_init = 0.8 - 0.6 * math.exp(-0.3 * l)
        phases = [lambda: layer_consts(l), lambda: phase_mod(l), lambda: phase_A(l, x_all if l == 0 else xb),
                  lambda: phase_ret(l), lambda: phase_attn(l, lam_init), lambda: phase_merge(l, x_all if l == 0 else xb),
                  lambda: phase_mlp(l, xb if l < L - 1 else y_all)]
        for ph in phases:
            if nph < stop:
                ph(); k.barrier()
            nph += 1
    k.emit()
    return k


def _rope_table(NS, NP):
    n_rows = NS // GRID_W
    row = np.repeat(np.arange(n_rows, dtype=np.float32), GRID_W)
    col = np.tile(np.arange(GRID_W, dtype=np.float32), n_rows)
    inv = (10000.0 ** (-np.arange(0, 32, 2, dtype=np.float32) / 32)).astype(np.float32)
    ar = (row[:, None] * inv[None, :]).astype(np.float32); ac = (col[:, None] * inv[None, :]).astype(np.float32)
    cr, sr, cc, sc = np.cos(ar), np.sin(ar), np.cos(ac), np.sin(ac)
    tab = np.concatenate([cr, cr, cc, cc, -sr, sr, -sc, sc], axis=1).astype(np.float32)
    ptab = np.concatenate([np.ones((2 * NP, 64), np.float32), np.zeros((2 * NP, 64), np.float32)], axis=1)
    return np.ascontiguousarray(np.concatenate([tab, ptab], axis=0))


def _ret_consts():
    C = 128
    j = np.arange(C, dtype=np.float32)[:, None]; i = np.arange(C, dtype=np.float32)[None, :]
    relf = np.maximum(i - j, 0.0); maskf = (i >= j).astype(np.float32)
    relb = np.maximum(j - i, 0.0); maskb = (j > i).astype(np.float32)
    retc = np.stack([relf, maskf, relb, maskb]).astype(np.float32)
    p = np.arange(C, dtype=np.float32)
    retcol = np.stack([p + 1.0, C - p, C - 1.0 - p, p], axis=1).astype(np.float32)
    return np.ascontiguousarray(retc), np.ascontiguousarray(retcol)


_CACHE = {}


def _run(inputs, NS, NP, PAST, dbg=(), stop=99):
    key = (NS, NP, PAST, tuple(dbg), stop)
    if key not in _CACHE:
        _CACHE[key] = build(NS, NP, PAST, dbg, stop)
    kb = _CACHE[key]
    f = lambda a: np.ascontiguousarray(np.asarray(a, dtype=np.float32))
    rope_tab = _rope_table(NS, NP); retc, retcol = _ret_consts()
    xp = f(inputs["x_prompt"]); xs = f(inputs["x_sample"])
    L = DEPTH
    in_maps = []
    for c in range(8):
        m = {
            "x_all": np.ascontiguousarray(np.concatenate([xs[c], xp[2 * c], xp[2 * c + 1]], axis=0)),
            "cvec": np.ascontiguousarray(np.stack([f(inputs["c"])[c], f(inputs["c_ctx"])])),
            "cdk": f(inputs["cache_diff_k"])[c].reshape(L, PAST, 512),
            "cdv": f(inputs["cache_diff_v"])[c].reshape(L, PAST, 512),
            "cckv": f(inputs["cache_mla_ckv"])[c], "ckr": f(inputs["cache_mla_krope"])[c],
            "sret": f(inputs["state_ret"])[c],
            "rope_tab": rope_tab, "retc": retc, "retcol": retcol,
        }
        for n, _ in W_SPECS:
            m[n] = f(inputs[n])
        in_maps.append({a: np.ascontiguousarray(b) for a, b in m.items()})
    res = run_bass_kernel_spmd(kb.nc, in_maps, core_ids=list(range(8)))
    return res.results


def kernel(**inputs):
    NS = inputs["x_sample"].shape[1]; NP = inputs["x_prompt"].shape[1]; PAST = inputs["cache_diff_k"].shape[2]
    B = inputs["x_prompt"].shape[0]
    r = _run(inputs, NS, NP, PAST)
    L = DEPTH
    y_prompt = np.zeros((B, NP, D), np.float32); y_sample = np.zeros((8, NS, D), np.float32)
    ndk = np.zeros((B, L, NP, 8, 64), np.float32); ndv = np.zeros((B, L, NP, 4, 128), np.float32)
    nckv = np.zeros((B, L, NP, 256), np.float32); nkr = np.zeros((B, L, NP, 64), np.float32)
    nsr = np.zeros((B, L, 2, 4, 64, 128), np.float32)
    for c in range(8):
        ya = r[c]["y_all"]
        y_sample[c] = ya[:NS]
        for p in range(2):
            b = 2 * c + p
            y_prompt[b] = ya[NS + p * NP: NS + (p + 1) * NP]
            ndk[b] = r[c]["ndk"][p].reshape(L, NP, 8, 64); ndv[b] = r[c]["ndv"][p].reshape(L, NP, 4, 128)
            nckv[b] = r[c]["nckv"][p]; nkr[b] = r[c]["nkr"][p]; nsr[b] = r[c]["nsr"][p]
    return (y_prompt, y_sample, ndk, ndv, nckv, nkr, nsr)
```

```python
import math
import os
import numpy as np
import concourse.bass as bass
import concourse.mybir as mybir
from concourse.bass_utils import run_bass_kernel_spmd

F32 = mybir.dt.float32
BF16 = mybir.dt.bfloat16
AF = mybir.ActivationFunctionType
ALU = mybir.AluOpType
AX = mybir.AxisListType

D = 1024
DEPTH = 2
EPS = 1e-6
D_IN = 6848
NPH_A = 3776
GRID_W = 64


class Buf:
    __slots__ = ("name", "w", "r")

    def __init__(self, name):
        self.name = name
        self.w = None
        self.r = []


class Op:
    __slots__ = ("eng", "fn", "deps", "dma", "sig", "need", "selfsig", "odeps", "reorder")

    def __init__(self, eng, fn, dma):
        self.eng = eng
        self.fn = fn
        self.deps = []
        self.dma = dma
        self.sig = None
        self.need = False
        self.selfsig = False
        self.odeps = []
        self.reorder = False


class T:
    def __init__(self, h, buf):
        self.h = h
        self.b = buf
        self.tok = None

    def __getitem__(self, k):
        return self.h[k]


ENGS = ("pe", "act", "dve", "pool", "sp")
NDMASEM = 16


class KB:
    def __init__(self):
        self.nc = bass.Bass("TRN2", target_bir_lowering=False)
        nc = self.nc
        self.e = {"pe": nc.tensor, "act": nc.scalar, "dve": nc.vector, "pool": nc.gpsimd, "sp": nc.sync}
        self.ops = {k: [] for k in ENGS}
        self.allops = []
        self.csem = {k: nc.alloc_semaphore("c_" + k) for k in ENGS}
        self.dsem = {q: [nc.alloc_semaphore(f"d_{q}{i}") for i in range(NDMASEM)] for q in ("sp", "pool", "act")}
        self.dcnt = {q: 0 for q in self.dsem}
        self.dlast = {}
        self.bar = None
        self.bar_seen = {k: None for k in ENGS}
        self.sb_off = 16512
        self.sb_mark = []
        self.nbuf = 0
        self.names = 0
        self.cache = {}
        self.reorder = False

    def buf(self, name="b"):
        self.nbuf += 1
        return Buf(f"{name}{self.nbuf}")

    def sb(self, shape, dt, name="t"):
        self.names += 1
        nbytes = int(np.prod(shape[1:])) * (4 if dt == F32 else 2)
        nbytes = (nbytes + 63) // 64 * 64
        h = self.nc.alloc_sbuf_tensor_at(f"{name}_{self.names}", list(shape), dt, offset=self.sb_off)
        self.sb_off += nbytes
        assert self.sb_off <= 229344, f"SBUF overflow {self.sb_off}"
        return T(h, self.buf(name))

    def push(self):
        self.sb_mark.append((self.sb_off, dict(self.cache)))

    def pop(self):
        self.sb_off, self.cache = self.sb_mark.pop()

    def tile(self, name, shape, dt):
        t = self.cache.get(name)
        if t is None:
            t = self.sb(shape, dt, name)
            self.cache[name] = t
        return t

    def dram(self, name, shape, dt, kind="Internal"):
        return self.nc.dram_tensor(name, list(shape), dt, kind=kind).ap()

    def op(self, eng, fn, R=(), W=(), dma=False):
        o = Op(eng, fn, dma)
        o.reorder = self.reorder
        deps = {}
        toks = [t.tok for t in R if isinstance(t, T) and t.tok is not None]
        if toks:
            W = list(W) + toks
        for t in R:
            b = t.b if isinstance(t, T) else t
            if b.w is not None:
                deps[id(b.w)] = (b.w, "raw")
        for t in W:
            b = t.b if isinstance(t, T) else t
            if b.w is not None:
                deps[id(b.w)] = (b.w, "waw")
            for r in b.r:
                if id(r) not in deps:
                    deps[id(r)] = (r, "war")
        for d, kind in deps.values():
            if d is o:
                continue
            if not d.dma and not dma and d.eng == eng:
                if eng == "pe":
                    o.odeps.append(d)
                    continue
            o.deps.append(d)
        if self.bar is not None and self.bar_seen[eng] is not self.bar:
            o.deps.append(self.bar)
            self.bar_seen[eng] = self.bar
        if dma:
            q = eng
            i = self.dcnt[q] % (2 if q == "pool" else NDMASEM)
            self.dcnt[q] += 1
            s = self.dsem[q][i]
            prev = self.dlast.get((q, i))
            if prev is not None:
                o.deps.append(prev[0])
                cnt = prev[1] + 1
            else:
                cnt = 1
            o.sig = (s, 16 * cnt)
            self.dlast[(q, i)] = (o, cnt)
            o.need = True
        for d in o.deps:
            d.need = True
        for t in R:
            b = t.b if isinstance(t, T) else t
            b.r.append(o)
        for t in W:
            b = t.b if isinstance(t, T) else t
            b.w = o
            b.r = []
        self.ops[eng].append(o)
        self.allops.append(o)
        return o

    def barrier(self):
        deps = []
        for k in ENGS:
            for o in reversed(self.ops[k]):
                if not o.dma:
                    deps.append(o)
                    break
        for (q, i), (o, c) in self.dlast.items():
            deps.append(o)
        b = Op("sp", lambda e: e.sem_inc(self.csem["sp"], 1), False)
        b.deps = [d for d in deps]
        for d in deps:
            d.need = True
        b.need = True
        b.selfsig = True
        self.ops["sp"].append(b)
        self.allops.append(b)
        self.bar = b

    def dma(self, out, in_, R=(), W=(), q="sp", **kw):
        return self.op(q, lambda e: e.dma_start(out=out, in_=in_, **kw), R, W, dma=True)

    def mm(self, out, lhsT, rhs, start, stop, R=(), W=()):
        return self.op("pe", lambda e: e.matmul(out, lhsT=lhsT, rhs=rhs, start=start, stop=stop), R, W)

    def tr(self, out, in_, ident, R=(), W=()):
        return self.op("pe", lambda e: e.transpose(out=out, in_=in_, identity=ident), R, W)

    def act(self, out, in_, func, R=(), W=(), **kw):
        return self.op("act", lambda e: e.activation(out=out, in_=in_, func=func, **kw), R, W)

    def tt(self, eng, out, in0, in1, op, R=(), W=()):
        return self.op(eng, lambda e: e.tensor_tensor(out=out, in0=in0, in1=in1, op=op), R, W)

    def ts(self, eng, out, in0, s1, s2, op0, op1=None, R=(), W=()):
        if op1 is None:
            return self.op(eng, lambda e: e.tensor_scalar(out=out, in0=in0, scalar1=s1, scalar2=None, op0=op0), R, W)
        return self.op(eng, lambda e: e.tensor_scalar(out=out, in0=in0, scalar1=s1, scalar2=s2, op0=op0, op1=op1), R, W)

    def stt(self, eng, out, in0, scalar, in1, op0, op1, R=(), W=()):
        return self.op(eng, lambda e: e.scalar_tensor_tensor(out=out, in0=in0, scalar=scalar, in1=in1, op0=op0, op1=op1), R, W)

    def cp(self, eng, out, in_, R=(), W=()):
        if eng == "act":
            return self.op("act", lambda e: e.activation(out=out, in_=in_, func=AF.Identity), R, W)
        return self.op(eng, lambda e: e.tensor_copy(out=out, in_=in_), R, W)

    def red(self, eng, out, in_, R=(), W=()):
        return self.op(eng, lambda e: e.tensor_reduce(out=out, in_=in_, axis=AX.X, op=ALU.add), R, W)

    def memset(self, eng, ap, val, W=()):
        return self.op(eng, lambda e: e.memset(ap, val), (), W)

    def sched_seg(self, ops):
        idx = {id(o): i for i, o in enumerate(ops)}
        fin = {}
        per = {kx: [o for o in ops if o.eng == kx] for kx in ENGS}
        free = {kx: 0.0 for kx in ENGS}
        DUR = {"pe": 0.3, "act": 0.6, "dve": 0.5, "pool": 0.6, "sp": 0.05}
        out = []
        Wn = 40
        nleft = len(ops)
        while nleft:
            best = None
            for kx in ENGS:
                lst = per[kx]
                fk = free[kx]
                for j in range(min(Wn, len(lst))):
                    o = lst[j]
                    t = fk
                    ok = True
                    for d in o.deps:
                        f = fin.get(id(d))
                        if f is None:
                            if id(d) in idx:
                                ok = False
                                break
                            continue
                        f += 0.2
                        if f > t:
                            t = f
                    if ok:
                        for d in o.odeps:
                            f = fin.get(id(d))
                            if f is None:
                                if id(d) in idx:
                                    ok = False
                                    break
                                continue
                            if f > t:
                                t = f
                    if not ok:
                        continue
                    key = (t, idx[id(o)])
                    if best is None or key < best[0]:
                        best = (key, kx, j, o, t)
                    if t <= fk:
                        break
            assert best is not None, "scheduler stuck"
            _, kx, j, o, t = best
            per[kx].pop(j)
            if o.dma:
                occ = 0.8 if kx == "pool" else 0.05
                fin[id(o)] = t + occ + 2.5
            else:
                occ = DUR[kx]
                fin[id(o)] = t + occ
            free[kx] = t + occ
            out.append(o)
            nleft -= 1
        return out

    def finalize(self):
        segs = []
        cur = []
        for o in self.allops:
            if o.selfsig:
                segs.append((cur, o)); cur = []
            else:
                cur.append(o)
        if cur:
            segs.append((cur, None))
        bars = set(id(b) for _, b in segs if b is not None)
        final = {kx: [] for kx in ENGS}
        lastdma = {}
        prevbar = None
        order_all = []
        for ops, bar in segs:
            for o in ops:
                o.deps = [d for d in o.deps if id(d) not in bars]
            order = self.sched_seg(ops)
            seen = set()
            for o in order:
                if o.eng not in seen:
                    seen.add(o.eng)
                    if prevbar is not None:
                        o.deps.append(prevbar)
                final[o.eng].append(o)
                order_all.append(o)
                if o.dma:
                    lastdma[o.sig[0].num] = o
            if bar is not None:
                deps = []
                for kx in ENGS:
                    for o in reversed(final[kx]):
                        if not o.dma:
                            deps.append(o)
                            break
                deps += list(lastdma.values())
                bar.deps = deps
                for d in deps:
                    d.need = True
                final["sp"].append(bar)
                order_all.append(bar)
                prevbar = bar
        self.ops = final
        self.allops = order_all

    def emit(self):
        self.finalize()
        cnt = {k: 0 for k in ENGS}
        for k_ in ENGS:
            for o in self.ops[k_]:
                if o.dma:
                    continue
                if o.need:
                    cnt[o.eng] += 1
                    o.sig = (self.csem[o.eng], cnt[o.eng])
        for k in ENGS:
            eng = self.e[k]
            waited = {}
            for o in self.ops[k]:
                for d in o.deps:
                    s, v = d.sig
                    key = s.num
                    if waited.get(key, 0) >= v:
                        continue
                    waited[key] = v
                    eng.wait_ge(s, v)
                ins = o.fn(eng)
                if o.dma:
                    ins.then_inc(o.sig[0], 16)
                elif o.need and not o.selfsig:
                    ins.then_inc(o.sig[0], 1)


def rstd_ops(k, ss, n_inv, mh, R=(), W=()):
    k.ts("pool", ss, ss, n_inv, EPS, ALU.mult, ALU.add, R=R, W=W)
    k.tt("pool", ss, ss, mh, ALU.pow, R=list(R) + list(W), W=W)


W_SPECS = [
    ("w_mod", [DEPTH, D, 6 * D]), ("b_mod", [DEPTH, 6 * D]), ("norm1", [DEPTH, D]), ("norm2", [DEPTH, D]),
    ("w_in", [DEPTH, D, D_IN]), ("diff_qn", [DEPTH, 64]), ("diff_kn", [DEPTH, 64]), ("diff_lambda", [DEPTH, 4, 64]),
    ("diff_subln", [DEPTH, 128]), ("ret_decay", [DEPTH, 2, 4]), ("ret_gn", [DEPTH, 128]),
    ("mla_qa_norm", [DEPTH, 384]), ("w_mla_qb", [DEPTH, 384, 768]), ("mla_kva_norm", [DEPTH, 256]),
    ("w_mla_kvb", [DEPTH, 256, 1024]), ("mla_qn", [DEPTH, 192]), ("mla_kn", [DEPTH, 192]),
    ("w_branch", [DEPTH, 3, 512, D]), ("w_out", [DEPTH, D, D]), ("w_up", [DEPTH, D, 4 * D]), ("w_down", [DEPTH, 4 * D, D]),
]


def build(NS, NP, PAST, dbg=(), stop=99):
    k = KB()
    nc = k.nc
    NT = NS + 2 * NP
    NTT = NT // 128
    NK = NT + PAST
    L = DEPTH
    ein = lambda n, s: k.dram(n, s, F32, kind="ExternalInput")
    eout = lambda n, s: k.dram(n, s, F32, kind="ExternalOutput")
    x_all = ein("x_all", [NT, D]); cvec = ein("cvec", [2, D])
    cdk = ein("cdk", [L, PAST, 512]); cdv = ein("cdv", [L, PAST, 512])
    cckv = ein("cckv", [L, PAST, 256]); ckr = ein("ckr", [L, PAST, 64]); sret = ein("sret", [L, 2, 4, 64, 128])
    Wt = {n: ein(n, s) for n, s in W_SPECS}
    rope_tab = ein("rope_tab", [NT, 128]); retc = ein("retc", [4, 128, 128]); retcol = ein("retcol", [128, 4])
    y_all = eout("y_all", [NT, D])
    ndk = eout("ndk", [2, L, NP, 512]); ndv = eout("ndv", [2, L, NP, 512])
    nckv = eout("nckv", [2, L, NP, 256]); nkr = eout("nkr", [2, L, NP, 64]); nsr = eout("nsr", [2, L, 2, 4, 64, 128])

    def scr(n, s, dt):
        return k.dram(n, s, dt, kind=("ExternalOutput" if n in dbg else "Internal"))
    modrow = scr("modrow", [2, 6 * D], F32)
    hTs = scr("hTs", [NTT, 128, 8, 128], BF16); h2Ts = scr("h2Ts", [NTT, 128, 8, 128], BF16)
    dqT = scr("dqT", [4, 128, NT], BF16); dkT = scr("dkT", [4, 128, NK], BF16); dvs = scr("dvs", [NK, 512], BF16)
    rqT = scr("rqT", [2, 128, NT], BF16); rkT = scr("rkT", [2, 128, NT], BF16); QQ = scr("QQ", [4, 128, NT], BF16)
    kks = scr("kks", [NT, 512], BF16); rvs = scr("rvs", [NT, 512], BF16); rgs = scr("rgs", [NT, 512], F32)
    mqTn = scr("mqTn", [4, 128, NT], BF16); mqTr = scr("mqTr", [2, 128, NT], BF16)
    mkTn = scr("mkTn", [4, 128, NK], BF16); mkTr = scr("mkTr", [2, 128, NK], BF16); mvs = scr("mvs", [NK, 512], BF16)
    oT = scr("oT", [3, 4, 128, NT], BF16)
    xa = scr("xa", [NT, D], F32); xb = scr("xb", [NT, D], F32)

    seqs = [dict(t0=0, nt=NS // 128, ctx=True, var=0, pi=None),
            dict(t0=NS // 128, nt=NP // 128, ctx=False, var=1, pi=0),
            dict(t0=(NS + NP) // 128, nt=NP // 128, ctx=False, var=1, pi=1)]

    def seq_of(t):
        for s in seqs:
            if s["t0"] <= t < s["t0"] + s["nt"]:
                return s

    PS = []
    for i in range(8):
        h = nc.alloc_psum_tensor(f"ps{i}", [128, 512], F32)
        PS.append(T(h, k.buf("ps")))
        PS[-1].tok = k.buf("pstok")
    psb = lambda i: PS[i][:].bitcast(BF16)

    identf = k.sb([128, 128], F32, "identf"); ident = k.sb([128, 128], BF16, "ident")
    mh = k.sb([128, 8], F32, "mh")
    k.memset("pool", identf[:], 0.0, W=[identf])
    k.op("pool", lambda e: e.affine_select(out=identf[:], in_=identf[:], pattern=[[-1, 128]], compare_op=ALU.not_equal,
                                           fill=1.0, base=0, channel_multiplier=1), R=[identf], W=[identf])
    k.cp("dve", ident[:], identf[:], R=[identf], W=[ident])
    k.memset("pool", mh[:], -0.5, W=[mh])
    rc = k.sb([128, 4, 128], F32, "retc")
    k.dma(rc[:], retc.rearrange("c p i -> p c i"), W=[rc])
    rcol = k.sb([128, 4], F32, "retcol")
    k.dma(rcol[:], retcol, W=[rcol])
    MT = k.sb([128, 4, 128], F32, "MT"); qdc = k.sb([128, 4, 2], F32, "qdc"); kdc = k.sb([128, 4, 2], F32, "kdc")
    decC = k.sb([128, 4, 128], F32, "decC"); lg = k.sb([128, 2, 4], F32, "lg"); neglam = k.sb([128, 1], F32, "neglam")

    def rstd(ss, n_inv, extraR=()):
        k.ts("pool", ss[:], ss[:], n_inv, EPS, ALU.mult, ALU.add, R=[ss] + list(extraR), W=[ss])
        k.tt("pool", ss[:], ss[:], mh[:, 0:ss.h.shape[1]] if len(ss.h.shape) == 2 else mh[:], ALU.pow, R=[ss, mh], W=[ss])

    def wload(dst, src_ap, nchunk=1):
        n = src_ap.shape[-1]
        step = max(256, ((n + nchunk - 1) // nchunk + 255) // 256 * 256)
        for c0 in range(0, n, step):
            c1 = min(n, c0 + step)
            k.dma(dst[:, :, c0:c1], src_ap[:, c0:c1].rearrange("(kc p) n -> p kc n", p=128), W=[dst], q="pool")

    def bload(dst_ap, src_ap, W):
        k.dma(dst_ap, src_ap.partition_broadcast(128), W=W)

    def transposes(src_aps, bank, dst, R, n_out_part=128):
        pb = psb(bank)
        n = len(src_aps)
        for i, a in enumerate(src_aps):
            k.tr(pb[:, i * 128:(i + 1) * 128], a, ident[:], R=list(R) + [ident], W=[PS[bank]])
        k.cp("dve", dst[:].rearrange("p n c -> p (n c)"), pb[:, 0:n * 128], R=[PS[bank]], W=[dst])

    def layer_consts(l):
        k.push()
        lam_init = 0.8 - 0.6 * math.exp(-0.3 * l)
        dl = k.sb([128, 4, 64], F32); pr = k.sb([128, 2, 64], F32); sm = k.sb([128, 2], F32)
        bload(dl[:].rearrange("p a d -> p (a d)"), Wt["diff_lambda"][l].rearrange("a d -> (a d)"), W=[dl])
        dl4 = dl[:].rearrange("p (a b) d -> p a b d", b=2)
        k.tt("dve", pr[:], dl4[:, :, 0, :], dl4[:, :, 1, :], ALU.mult, R=[dl], W=[pr])
        k.red("dve", sm[:], pr[:], R=[pr], W=[sm])
        k.act(sm[:], sm[:], AF.Exp, R=[sm], W=[sm])
        k.stt("dve", neglam[:], sm[:, 1:2], -lam_init, sm[:, 0:1], ALU.add, ALU.subtract, R=[sm], W=[neglam])
        bload(lg[:].rearrange("p a h -> p (a h)"), Wt["ret_decay"][l].rearrange("a h -> (a h)"), W=[lg])
        k.act(lg[:], lg[:], AF.Exp, R=[lg], W=[lg], scale=-1.0)
        k.act(lg[:], lg[:], AF.Ln, R=[lg], W=[lg], bias=1.0)
        k.ts("dve", lg[:], lg[:], -1.0, None, ALU.mult, R=[lg], W=[lg])
        tmp = k.sb([128, 4, 128], F32); tmp2 = k.sb([128, 4, 128], F32)
        for h in range(4):
            k.ts("dve", tmp[:, h, :], rc[:, 0, :], lg[:, 0, h:h + 1], None, ALU.mult, R=[rc, lg], W=[tmp])
            k.ts("dve", tmp2[:, h, :], rc[:, 2, :], lg[:, 1, h:h + 1], None, ALU.mult, R=[rc, lg], W=[tmp2])
        k.act(tmp[:], tmp[:], AF.Exp, R=[tmp], W=[tmp])
        k.act(tmp2[:], tmp2[:], AF.Exp, R=[tmp2], W=[tmp2])
        k.tt("dve", tmp[:], tmp[:], rc[:, 1:2, :].to_broadcast([128, 4, 128]), ALU.mult, R=[tmp, rc], W=[tmp])
        k.tt("dve", tmp2[:], tmp2[:], rc[:, 3:4, :].to_broadcast([128, 4, 128]), ALU.mult, R=[tmp2, rc], W=[tmp2])
        k.tt("dve", MT[:], tmp[:], tmp2[:], ALU.add, R=[tmp, tmp2], W=[MT])
        for d in range(2):
            k.ts("dve", qdc[:, :, d], lg[:, d, :], rcol[:, d:d + 1], None, ALU.mult, R=[lg, rcol], W=[qdc])
            k.ts("dve", kdc[:, :, d], lg[:, d, :], rcol[:, 2 + d:3 + d], None, ALU.mult, R=[lg, rcol], W=[kdc])
        k.act(qdc[:], qdc[:], AF.Exp, R=[qdc], W=[qdc])
        k.act(kdc[:], kdc[:], AF.Exp, R=[kdc], W=[kdc])
        dc = k.sb([128, 4], F32)
        k.act(dc[0:64, :], lg[0:64, 0, :], AF.Exp, R=[lg], W=[dc], scale=128.0)
        k.act(dc[64:128, :], lg[64:128, 1, :], AF.Exp, R=[lg], W=[dc], scale=128.0)
        k.cp("dve", decC[:], dc[:].unsqueeze(2).to_broadcast([128, 4, 128]), R=[dc], W=[decC])
        k.pop()
        return lam_init

    def phase_mod(l):
        k.push()
        cT = k.sb([128, 2, 8], F32); sT = k.sb([128, 2, 8], F32); rep = k.sb([128, 16, 128], BF16)
        for v in range(2):
            k.dma(cT[:, v, :], cvec[v].rearrange("(kc p) -> p kc", p=128), W=[cT], allow_slow_non_contiguous=True)
        k.act(sT[:], cT[:], AF.Silu, R=[cT], W=[sT])
        k.cp("dve", rep[:], sT[:].rearrange("p v c -> p (v c)").unsqueeze(2).to_broadcast([128, 16, 128]), R=[sT], W=[rep])
        wm = [k.sb([128, 8, 512], BF16) for _ in range(2)]
        bm = [k.sb([128, 512], F32) for _ in range(2)]
        res = [k.sb([128, 512], F32) for _ in range(2)]
        for n in range(12):
            w_, b_ = wm[n % 2], bm[n % 2]
            wload(w_, Wt["w_mod"][l][:, n * 512:(n + 1) * 512])
            bload(b_[:], Wt["b_mod"][l][n * 512:(n + 1) * 512], W=[b_])
            for v in range(2):
                bank = PS[v]
                for kc in range(8):
                    k.mm(bank[:], rep[:, v * 8 + kc, :], w_[:, kc, :], kc == 0, kc == 7, R=[rep, w_], W=[bank])
                r_ = res[v]
                k.tt("dve", r_[:], bank[:], b_[:], ALU.add, R=[bank, b_], W=[r_])
                k.dma(modrow[v:v + 1, n * 512:(n + 1) * 512], r_[0:1, :], R=[r_])
        k.pop()

    def rope_mul(zv, H, Gap, G, rt, out3, tmpA, tmpB, R, W, scale_bcast=None):
        gc = k.tile("rope_gc", [128, 64], F32); gs = k.tile("rope_gs", [128, 64], F32)
        k.tt("dve", gc[:], Gap, rt[:, 0:64], ALU.mult, R=[G, rt], W=[gc])
        g4 = Gap.rearrange("p (r h x) -> p r h x", r=2, h=2)
        s4 = rt[:, 64:128].rearrange("p (r h x) -> p r h x", r=2, h=2)
        gs4 = gs[:].rearrange("p (r h x) -> p r h x", r=2, h=2)
        k.tt("dve", gs4[:, :, 0, :], g4[:, :, 1, :], s4[:, :, 0, :], ALU.mult, R=[G, rt], W=[gs])
        k.tt("dve", gs4[:, :, 1, :], g4[:, :, 0, :], s4[:, :, 1, :], ALU.mult, R=[G, rt, gs], W=[gs])
        k.tt("dve", tmpA[:, 0:H, :], zv, gc[:].unsqueeze(1).to_broadcast([128, H, 64]), ALU.mult, R=list(R) + [gc], W=[tmpA])
        for r in range(2):
            for hf in range(2):
                zs = zv[:, :, r * 32 + (1 - hf) * 16: r * 32 + (1 - hf) * 16 + 16]
                o_ = tmpB[:, 0:H, r * 32 + hf * 16: r * 32 + hf * 16 + 16]
                g_ = gs[:, r * 32 + hf * 16: r * 32 + hf * 16 + 16].unsqueeze(1).to_broadcast([128, H, 16])
                k.tt("dve", o_, zs, g_, ALU.mult, R=list(R) + [gs], W=[tmpB])
        if scale_bcast is None:
            k.tt("pool", out3, tmpA[:, 0:H, :], tmpB[:, 0:H, :], ALU.add, R=[tmpA, tmpB], W=W)
        else:
            k.tt("pool", tmpA[:, 0:H, :], tmpA[:, 0:H, :], tmpB[:, 0:H, :], ALU.add, R=[tmpA, tmpB], W=[tmpA])
            k.tt("dve", out3, tmpA[:, 0:H, :], scale_bcast, ALU.mult, R=[tmpA] + list(R), W=W)

    def store_T(src, nblk, bank, dstT_ap_fn, R):
        tT = k.tile(f"stT{bank}_{nblk}", [128, nblk, 128], BF16)
        transposes([src[:, i * 128:(i + 1) * 128] for i in range(nblk)], bank, tT, R=[src])
        k.dma(dstT_ap_fn(), tT[:], R=[tT])

    def phase_A(l, xsrc):
        k.push()
        k.reorder = True
        wA = k.sb([128, 8, NPH_A], BF16, "wA"); wload(wA, Wt["w_in"][l][:, 0:NPH_A], nchunk=8)
        wqb = k.sb([128, 3, 768], BF16, "wqb"); wload(wqb, Wt["w_mla_qb"][l])
        wkvb = k.sb([128, 2, 1024], BF16, "wkvb"); wload(wkvb, Wt["w_mla_kvb"][l])
        A1, B1 = [], []
        n1 = k.sb([128, D], F32, "n1"); bload(n1[:], Wt["norm1"][l], W=[n1])
        for v in range(2):
            a = k.sb([128, D], F32, "A1"); b = k.sb([128, D], F32, "B1")
            bload(a[:], modrow[v, D:2 * D], W=[a]); bload(b[:], modrow[v, 0:D], W=[b])
            k.stt("dve", a[:], a[:], 1.0, n1[:], ALU.add, ALU.mult, R=[a, n1], W=[a])
            A1.append(a); B1.append(b)
        gq = k.sb([128, 64], F32, "gq"); bload(gq[:], Wt["diff_qn"][l], W=[gq])
        gk = k.sb([128, 64], F32, "gk"); bload(gk[:], Wt["diff_kn"][l], W=[gk])
        ones64 = k.sb([128, 64], F32, "ones"); k.memset("pool", ones64[:], 1.0, W=[ones64])
        eighth = k.sb([128, 64], F32, "eighth"); k.memset("pool", eighth[:], 0.125, W=[eighth])
        gqa = k.sb([128, 384], F32, "gqa"); bload(gqa[:], Wt["mla_qa_norm"][l], W=[gqa])
        gkva = k.sb([128, 256], F32, "gkva"); bload(gkva[:], Wt["mla_kva_norm"][l], W=[gkva])
        gmqn = k.sb([128, 192], F32, "gmqn"); bload(gmqn[:], Wt["mla_qn"][l], W=[gmqn])
        gmkn = k.sb([128, 192], F32, "gmkn"); bload(gmkn[:], Wt["mla_kn"][l], W=[gmkn])
        rt1 = k.sb([128, 128], F32, "rt1")
        k.memset("pool", rt1[:, 0:64], 1.0, W=[rt1]); k.memset("pool", rt1[:, 64:128], 0.0, W=[rt1])
        tA = k.sb([128, 8, 64], F32, "tA"); tB = k.sb([128, 8, 64], F32, "tB")

        def mla_kv(ckvb, krf, rt, col):
            ckvT = k.tile("ckvT", [128, 2, 128], BF16)
            transposes([ckvb[:, 0:128], ckvb[:, 128:256]], 5, ckvT, R=[ckvb])
            for c in range(2):
                for kc in range(2):
                    k.mm(PS[6 + c][:], ckvT[:, kc, :], wkvb[:, kc, c * 512:(c + 1) * 512], kc == 0, kc == 1, R=[ckvT, wkvb], W=[PS[6 + c]])
            sqk = k.tile("sqk", [128, 4, 128], F32); ssn = k.tile("ssn", [128, 4], F32)
            kv4 = [PS[6 + c][:].rearrange("p (h x d) -> p h x d", h=2, x=2) for c in range(2)]
            for c in range(2):
                k.act(sqk[:, 2 * c:2 * c + 2, :], kv4[c][:, :, 0, :], AF.Square, R=[PS[6 + c]], W=[sqk])
            k.red("dve", ssn[:], sqk[:], R=[sqk], W=[ssn])
            skr = k.tile("skr", [128, 1], F32); junk3 = k.tile("junk3", [128, 64], F32)
            k.act(junk3[:], krf[:], AF.Square, R=[krf], W=[junk3, skr], accum_out=skr[:])
            k.ts("dve", ssn[:], ssn[:], skr[:, 0:1], None, ALU.add, R=[ssn, skr], W=[ssn])
            rstd(ssn, 1.0 / 192)
            krg = k.tile("krg", [128, 1, 64], F32)
            rope_mul(krf[:].unsqueeze(1), 1, gmkn[:, 128:192], gmkn, rt, krg[:], tA, tB, R=[krf], W=[krg])
            mkn = k.tile("mkn", [128, 4, 128], BF16); mkr = k.tile("mkr", [128, 4, 64], BF16); mvb = k.tile("mvb", [128, 4, 128], BF16)
            for c in range(2):
                tn = k.tile("tn", [128, 2, 128], F32)
                k.tt("dve", tn[:], kv4[c][:, :, 0, :], gmkn[:, 0:128].unsqueeze(1).to_broadcast([128, 2, 128]), ALU.mult, R=[PS[6 + c], gmkn], W=[tn])
                k.tt("dve", mkn[:, 2 * c:2 * c + 2, :], tn[:], ssn[:, 2 * c:2 * c + 2].unsqueeze(2).to_broadcast([128, 2, 128]), ALU.mult, R=[tn, ssn], W=[mkn])
                k.cp("act", mvb[:, 2 * c:2 * c + 2, :], kv4[c][:, :, 1, :], R=[PS[6 + c]], W=[mvb])
            for h_ in range(4):
                k.ts("dve", mkr[:, h_, :], krg[:, 0, :], ssn[:, h_:h_ + 1], None, ALU.mult, R=[krg, ssn], W=[mkr])
            store_T(T(mkn.h[:].rearrange("p h d -> p (h d)"), mkn.b), 4, 5, lambda: mkTn[:, :, col:col + 128].rearrange("h p c -> p h c"), R=[mkn])
            store_T(T(mkr.h[:].rearrange("p h d -> p (h d)"), mkr.b), 2, 5, lambda: mkTr[:, :, col:col + 128].rearrange("h p c -> p h c"), R=[mkr])
            k.dma(mvs[col:col + 128, :], mvb[:].rearrange("p h d -> p (h d)"), R=[mvb])

        for ct in range(PAST // 128 if int(os.environ.get("A_CTX", "1")) else 0):
            col = NT + ct * 128
            rows = slice(ct * 128, (ct + 1) * 128)
            CP = int(os.environ.get("A_CTXP", "9"))
            ckb = k.tile(f"ckb{ct % 2}", [128, 512], BF16)
            k.dma(ckb[:], cdk[l, rows, :], W=[ckb], q="pool")
            if CP >= 1:
                store_T(ckb, 4, 5, lambda: dkT[:, :, col:col + 128].rearrange("h p c -> p h c"), R=[ckb])
            cvb = k.tile(f"cvb{ct % 2}", [128, 512], BF16)
            if CP >= 2:
                k.dma(cvb[:], cdv[l, rows, :], W=[cvb], q="pool")
                k.dma(dvs[col:col + 128, :], cvb[:], R=[cvb])
            cc = k.tile(f"ccb{ct % 2}", [128, 256], BF16)
            crf = k.tile(f"crf{ct % 2}", [128, 64], F32)
            if CP >= 3:
                k.dma(cc[:], cckv[l, rows, :], W=[cc], q="pool")
                k.dma(crf[:], ckr[l, rows, :], W=[crf])
            if CP >= 4:
                mla_kv(cc, crf, rt1, col)

        def loads(t):
            xt = k.tile(f"x{t % 2}", [128, D], F32); rt = k.tile(f"rt{t % 2}", [128, 128], F32)
            k.dma(xt[:], xsrc[t * 128:(t + 1) * 128, :], W=[xt])
            k.dma(rt[:], rope_tab[t * 128:(t + 1) * 128, :], W=[rt])

        loads(0)
        A_TILES = int(os.environ.get("A_TILES", "999")); A_PARTS = int(os.environ.get("A_PARTS", "99"))
        for t in range(min(NTT, A_TILES)):
            if t + 1 < NTT:
                loads(t + 1)
            s = seq_of(t); v = s["var"]; pi = s["pi"]
            lrows = slice((t - s["t0"]) * 128, (t - s["t0"] + 1) * 128)
            grow = slice(t * 128, (t + 1) * 128)
            xt = k.tile(f"x{t % 2}", [128, D], F32); rt = k.tile(f"rt{t % 2}", [128, 128], F32)
            junk = k.tile("junk", [128, D], BF16); ssx = k.tile("ssx", [128, 1], F32)
            k.act(junk[:], xt[:], AF.Square, R=[xt], W=[junk, ssx], accum_out=ssx[:])
            rstd(ssx, 1.0 / D)
            tmp = k.tile("htmp", [128, D], F32); hb = k.tile("hb", [128, D], BF16)
            k.stt("dve", tmp[:], xt[:], ssx[:, 0:1], A1[v][:], ALU.mult, ALU.mult, R=[xt, ssx, A1[v]], W=[tmp])
            k.tt("pool", hb[:], tmp[:], B1[v][:], ALU.add, R=[tmp, B1[v]], W=[hb])
            hT = k.tile(f"hT{t % 2}", [128, 8, 128], BF16)
            transposes([hb[:, i * 128:(i + 1) * 128] for i in range(8)], 4, hT, R=[hb])
            k.dma(hTs[t], hT[:], R=[hT])

            def zmm(c0, c1, bank):
                for kc in range(8):
                    k.mm(PS[bank][:, 0:c1 - c0], hT[:, kc, :], wA[:, kc, c0:c1], kc == 0, kc == 7, R=[hT, wA], W=[PS[bank]])

            def qknorm(bank, G, name, f32out):
                z3 = PS[bank][:].rearrange("p (h d) -> p h d", d=64)
                sq = k.tile("sq", [128, 512], F32); ss8 = k.tile("ss8", [128, 8], F32)
                k.act(sq[:], PS[bank][:], AF.Square, R=[PS[bank]], W=[sq])
                k.red("dve", ss8[:], sq[:].rearrange("p (h d) -> p h d", d=64), R=[sq], W=[ss8])
                rstd(ss8, 1.0 / 64)
                ob = k.tile(name + "b", [128, 512], BF16)
                sc = ss8[:].unsqueeze(2).to_broadcast([128, 8, 64])
                if f32out:
                    of = k.tile(name + "f", [128, 512], F32)
                    rope_mul(z3, 8, G[:], G, rt, of[:].rearrange("p (h d) -> p h d", d=64), tA, tB, R=[PS[bank], ss8], W=[of], scale_bcast=sc)
                    k.cp("act", ob[:], of[:], R=[of], W=[ob])
                    return ob, of
                rope_mul(z3, 8, G[:], G, rt, ob[:].rearrange("p (h d) -> p h d", d=64), tA, tB, R=[PS[bank], ss8], W=[ob], scale_bcast=sc)
                return ob, None

            if A_PARTS <= 0:
                continue
            zmm(0, 512, 0)
            qb, _ = qknorm(0, gq, "dq", False)
            store_T(qb, 4, 5, lambda: dqT[:, :, grow].rearrange("h p c -> p h c"), R=[qb])
            if A_PARTS <= 1:
                continue
            zmm(512, 1024, 1)
            API = int(os.environ.get("A_PI", "7"))
            kb_, kf_ = qknorm(1, gk, "dk", pi is not None and (API & 1))
            store_T(kb_, 4, 5, lambda: dkT[:, :, grow].rearrange("h p c -> p h c"), R=[kb_])
            if pi is not None and (API & 1):
                k.dma(ndk[pi, l, lrows, :], kf_[:], R=[kf_])
            if A_PARTS <= 2:
                continue
            zmm(1024, 1536, 2)
            dvb = k.tile("dvb", [128, 512], BF16)
            k.cp("act", dvb[:], PS[2][:], R=[PS[2]], W=[dvb])
            k.dma(dvs[grow, :], dvb[:], R=[dvb])
            if pi is not None and (API & 2):
                dvf = k.tile("dvf", [128, 512], F32)
                k.cp("act", dvf[:], PS[2][:], R=[PS[2]], W=[dvf])
                k.dma(ndv[pi, l, lrows, :], dvf[:], R=[dvf])
            if A_PARTS <= 3:
                continue
            zmm(1536, 2048, 3)
            z = PS[3]
            rqf = k.tile("rqf", [128, 4, 64], F32); rkf = k.tile("rkf", [128, 4, 64], F32)
            rope_mul(z[:, 0:256].rearrange("p (h d) -> p h d", d=64), 4, ones64[:], ones64, rt, rqf[:], tA, tB, R=[z], W=[rqf])
            rope_mul(z[:, 256:512].rearrange("p (h d) -> p h d", d=64), 4, eighth[:], eighth, rt, rkf[:], tA, tB, R=[z], W=[rkf])
            rqb = k.tile("rqb", [128, 256], BF16); rkb = k.tile("rkb", [128, 256], BF16)
            k.cp("act", rqb[:], rqf[:].rearrange("p h d -> p (h d)"), R=[rqf], W=[rqb])
            k.cp("act", rkb[:], rkf[:].rearrange("p h d -> p (h d)"), R=[rkf], W=[rkb])
            rq2 = k.tile("rq2", [128, 4, 2, 64], BF16); kk = k.tile("kk", [128, 4, 2, 64], BF16)
            for d_ in range(2):
                k.tt("dve", rq2[:, :, d_, :], rqf[:], qdc[:, :, d_:d_ + 1].to_broadcast([128, 4, 64]), ALU.mult, R=[rqf, qdc], W=[rq2])
                k.tt("dve", kk[:, :, d_, :], rkf[:], kdc[:, :, d_:d_ + 1].to_broadcast([128, 4, 64]), ALU.mult, R=[rkf, kdc], W=[kk])
            store_T(rqb, 2, 5, lambda: rqT[:, :, grow].rearrange("h p c -> p h c"), R=[rqb])
            store_T(rkb, 2, 5, lambda: rkT[:, :, grow].rearrange("h p c -> p h c"), R=[rkb])
            store_T(T(rq2.h[:].rearrange("p h a d -> p (h a d)"), rq2.b), 4, 5, lambda: QQ[:, :, grow].rearrange("h p c -> p h c"), R=[rq2])
            k.dma(kks[grow, :], kk[:].rearrange("p h a d -> p (h a d)"), R=[kk])
            if A_PARTS <= 4:
                continue
            zmm(2048, 2560, 0)
            rvb = k.tile("rvb", [128, 512], BF16)
            k.cp("act", rvb[:], PS[0][:], R=[PS[0]], W=[rvb])
            k.dma(rvs[grow, :], rvb[:], R=[rvb])
            if A_PARTS <= 5:
                continue
            zmm(2560, 3072, 1)
            rgf = k.tile("rgf", [128, 512], F32)
            k.cp("act", rgf[:], PS[1][:], R=[PS[1]], W=[rgf])
            k.dma(rgs[grow, :], rgf[:], R=[rgf])
            if A_PARTS <= 6:
                continue
            zmm(3072, 3456, 2)
            z = PS[2]
            ssq = k.tile("ssq", [128, 1], F32); junk2 = k.tile("junk2", [128, 384], BF16)
            k.act(junk2[:], z[:, 0:384], AF.Square, R=[z], W=[junk2, ssq], accum_out=ssq[:])
            rstd(ssq, 1.0 / 384)
            qab = k.tile("qab", [128, 384], BF16)
            k.stt("dve", qab[:], z[:, 0:384], ssq[:, 0:1], gqa[:], ALU.mult, ALU.mult, R=[z, ssq, gqa], W=[qab])
            qaT = k.tile("qaT", [128, 3, 128], BF16)
            transposes([qab[:, i * 128:(i + 1) * 128] for i in range(3)], 5, qaT, R=[qab])
            mqn = k.tile("mqn", [128, 4, 128], BF16); mqr = k.tile("mqr", [128, 4, 64], BF16)
            for c in range(2):
                bk = PS[6 + c]
                for kc in range(3):
                    k.mm(bk[:, 0:384], qaT[:, kc, :], wqb[:, kc, c * 384:(c + 1) * 384], kc == 0, kc == 2, R=[qaT, wqb], W=[bk])
                sq = k.tile("sq", [128, 512], F32); ss2 = k.tile("ss2", [128, 2], F32)
                k.act(sq[:, 0:384], bk[:, 0:384], AF.Square, R=[bk], W=[sq])
                k.red("dve", ss2[:], sq[:, 0:384].rearrange("p (h d) -> p h d", d=192), R=[sq], W=[ss2])
                rstd(ss2, 1.0 / 192)
                tq = k.tile("tq", [128, 2, 192], F32); tqr = k.tile("tqr", [128, 2, 64], F32)
                k.tt("dve", tq[:], bk[:, 0:384].rearrange("p (h d) -> p h d", d=192), gmqn[:].unsqueeze(1).to_broadcast([128, 2, 192]), ALU.mult, R=[bk, gmqn], W=[tq])
                rope_mul(tq[:, :, 128:192], 2, ones64[:], ones64, rt, tqr[:], tA, tB, R=[tq], W=[tqr])
                k.tt("dve", mqn[:, 2 * c:2 * c + 2, :], tq[:, :, 0:128], ss2[:].unsqueeze(2).to_broadcast([128, 2, 128]), ALU.mult, R=[tq, ss2], W=[mqn])
                k.tt("dve", mqr[:, 2 * c:2 * c + 2, :], tqr[:], ss2[:].unsqueeze(2).to_broadcast([128, 2, 64]), ALU.mult, R=[tqr, ss2], W=[mqr])
            store_T(T(mqn.h[:].rearrange("p h d -> p (h d)"), mqn.b), 4, 5, lambda: mqTn[:, :, grow].rearrange("h p c -> p h c"), R=[mqn])
            store_T(T(mqr.h[:].rearrange("p h d -> p (h d)"), mqr.b), 2, 5, lambda: mqTr[:, :, grow].rearrange("h p c -> p h c"), R=[mqr])
            if A_PARTS <= 7:
                continue
            zmm(3456, 3776, 3)
            z = PS[3]
            ssk = k.tile("ssk", [128, 1], F32)
            k.act(junk2[:, 0:256], z[:, 0:256], AF.Square, R=[z], W=[junk2, ssk], accum_out=ssk[:])
            rstd(ssk, 1.0 / 256)
            ckvf = k.tile("ckvf", [128, 256], F32); krf = k.tile("krf", [128, 64], F32); ckvb = k.tile("ckvb", [128, 256], BF16)
            k.stt("dve", ckvf[:], z[:, 0:256], ssk[:, 0:1], gkva[:], ALU.mult, ALU.mult, R=[z, ssk, gkva], W=[ckvf])
            k.cp("act", krf[:], z[:, 256:320], R=[z], W=[krf])
            k.cp("act", ckvb[:], ckvf[:], R=[ckvf], W=[ckvb])
            if pi is not None and (API & 4):
                k.dma(nckv[pi, l, lrows, :], ckvf[:], R=[ckvf])
                k.dma(nkr[pi, l, lrows, :], krf[:], R=[krf])
            mla_kv(ckvb, krf, rt, t * 128)
        k.pop()
        k.reorder = False

    def phase_ret(l):
        for s in seqs:
            k.push()
            nch = s["nt"]; t0 = s["t0"]; pi = s["pi"]
            gn = k.sb([128, 128], F32, "gn"); bload(gn[:], Wt["ret_gn"][l], W=[gn])
            rv = k.sb([128, nch, 512], BF16, "rv")
            k.dma(rv[:], rvs[t0 * 128:(t0 + nch) * 128, :].rearrange("(c p) f -> p c f", p=128), W=[rv])
            U = k.sb([128, nch, 512], F32, "U"); RR = k.sb([128, nch, 512], BF16, "RR"); S = k.sb([128, 512], F32, "S")
            Ub = [k.buf() for _ in range(nch)]; RRf = [k.buf() for _ in range(nch)]; RRb = [k.buf() for _ in range(nch)]
            Sf = k.buf(); Sb = k.buf()
            for c in range(nch):
                kkt = k.tile(f"kkt{c % 2}", [128, 512], BF16)
                k.dma(kkt[:], kks[(t0 + c) * 128:(t0 + c + 1) * 128, :], W=[kkt])
                bank = PS[c % 2]
                for h in range(4):
                    hs = slice(h * 128, (h + 1) * 128)
                    k.mm(bank[:, hs], kkt[:, hs], rv[:, c, hs], True, True, R=[kkt, rv], W=[bank])
                k.cp("act", U[:, c, :], bank[:], R=[bank], W=[Ub[c]])
            RS = int(os.environ.get("R_STOP", "9"))
            if RS < 1:
                k.pop(); k.barrier(); continue
            if s["ctx"]:
                for d in range(2):
                    k.dma(S[64 * d:64 * d + 64, :].rearrange("k (h v) -> k h v", h=4), sret[l, d].rearrange("h k v -> k h v"), W=[Sf if d == 0 else Sb])
            else:
                k.memset("dve", S[0:64, :], 0.0, W=[Sf]); k.memset("pool", S[64:128, :], 0.0, W=[Sb])
            dec2 = decC[:].rearrange("p h e -> p (h e)")
            for c in range(nch):
                k.cp("dve", RR[0:64, c, :], S[0:64, :], R=[Sf], W=[RRf[c]])
                k.tt("dve", S[0:64, :], S[0:64, :], dec2[0:64, :], ALU.mult, R=[Sf, decC], W=[Sf])
                k.tt("dve", S[0:64, :], S[0:64, :], U[0:64, c, :], ALU.add, R=[Sf, Ub[c]], W=[Sf])
            for c in reversed(range(nch)):
                k.cp("act", RR[64:128, c, :], S[64:128, :], R=[Sb], W=[RRb[c]])
                k.tt("pool", S[64:128, :], S[64:128, :], dec2[64:128, :], ALU.mult, R=[Sb, decC], W=[Sb])
                k.tt("pool", S[64:128, :], S[64:128, :], U[64:128, c, :], ALU.add, R=[Sb, Ub[c]], W=[Sb])
            if RS < 2:
                k.pop(); k.barrier(); continue
            if pi is not None:
                k.dma(nsr[pi, l, 0].rearrange("h k v -> k h v"), S[0:64, :].rearrange("k (h v) -> k h v", h=4), R=[Sf])
                k.dma(nsr[pi, l, 1].rearrange("h k v -> k h v"), S[64:128, :].rearrange("k (h v) -> k h v", h=4), R=[Sb])
            if RS < 3:
                k.pop(); k.barrier(); continue
            for c in range(nch):
                t = t0 + c; cols = slice(t * 128, (t + 1) * 128)
                kT = k.tile(f"rkT{c % 2}", [128, 2, 128], BF16); qT = k.tile(f"rqT{c % 2}", [128, 2, 128], BF16)
                qq = k.tile(f"rqq{c % 2}", [128, 4, 128], BF16); rg = k.tile(f"rrg{c % 2}", [128, 512], F32)
                k.dma(kT[:], rkT[:, :, cols].rearrange("h p c -> p h c"), W=[kT])
                k.dma(qT[:], rqT[:, :, cols].rearrange("h p c -> p h c"), W=[qT])
                k.dma(qq[:], QQ[:, :, cols].rearrange("h p c -> p h c"), W=[qq])
                k.dma(rg[:], rgs[cols, :], W=[rg])
                bsr = [PS[2 - 2 * (c % 2)], PS[3 - 2 * (c % 2)]]; bo = PS[4 + c % 2]
                for h in range(4):
                    pp = slice(64 * (h % 2), 64 * (h % 2) + 64)
                    k.mm(bsr[h % 2][:, (h // 2) * 128:(h // 2 + 1) * 128], kT[pp, h // 2, :], qT[pp, h // 2, :], True, True, R=[kT, qT], W=[bsr[h % 2]])
                P = k.tile(f"rP{c % 2}", [128, 512], BF16)
                P4 = P[:].rearrange("p (a r i) -> p a r i", a=2, r=2)
                MT4 = MT[:].rearrange("p (a r) i -> p a r i", r=2)
                for r_ in range(2):
                    k.tt("dve", P4[:, :, r_, :], bsr[r_][:, 0:256].rearrange("p (a i) -> p a i", a=2), MT4[:, :, r_, :], ALU.mult, R=[bsr[r_], MT], W=[P])
                for h in range(4):
                    hs = slice(h * 128, (h + 1) * 128)
                    k.mm(bo[:, hs], P[:, hs], rv[:, c, hs], True, False, R=[P, rv], W=[bo])
                    k.mm(bo[:, hs], qq[:, h, :], RR[:, c, hs], False, True, R=[qq, RRf[c], RRb[c]], W=[bo])
                sq = k.tile("rsq", [128, 512], F32); ss4 = k.tile("rss4", [128, 4], F32)
                k.act(sq[:], bo[:], AF.Square, R=[bo], W=[sq])
                k.red("dve", ss4[:], sq[:].rearrange("p (h e) -> p h e", h=4), R=[sq], W=[ss4])
                rstd(ss4, 1.0 / 128)
                to = k.tile("rto", [128, 4, 128], F32)
                k.tt("dve", to[:], bo[:].rearrange("p (h e) -> p h e", h=4), ss4[:].unsqueeze(2).to_broadcast([128, 4, 128]), ALU.mult, R=[bo, ss4], W=[to])
                k.tt("dve", to[:], to[:], gn[:].unsqueeze(1).to_broadcast([128, 4, 128]), ALU.mult, R=[to, gn], W=[to])
                sg = k.tile("rsg", [128, 512], F32)
                k.act(sg[:], rg[:], AF.Silu, R=[rg], W=[sg])
                orb = k.tile("orb", [128, 512], BF16)
                k.tt("pool", orb[:], to[:].rearrange("p h e -> p (h e)"), sg[:], ALU.mult, R=[to, sg], W=[orb])
                store_T(orb, 4, 6, lambda: oT[1][:, :, cols].rearrange("h p c -> p h c"), R=[orb])
            k.pop()
            k.barrier()

    def attn_core(pairs_fn, V, nkt, QC, scale, R):
        nqt = QC // 128

        def st(kt):
            bank = PS[kt % 2]
            prs = pairs_fn(kt)
            for i, (a, b) in enumerate(prs):
                k.mm(bank[:, 0:QC], a, b, i == 0, i == len(prs) - 1, R=R, W=[bank])

        def pv(kt):
            bank = PS[kt % 2]
            PT = k.tile(f"PT{kt % 3}", [128, 512], BF16)
            k.act(PT[:, 0:QC], bank[:, 0:QC], AF.Exp, R=[bank], W=[PT], scale=scale)
            for qt in range(nqt):
                k.mm(PS[2 + qt][:, 0:129], PT[:, qt * 128:(qt + 1) * 128], V[:, kt, :], kt == 0, kt == nkt - 1, R=[PT, V], W=[PS[2 + qt]])
        st(0)
        for kt in range(nkt):
            if kt + 1 < nkt:
                st(kt + 1)
            pv(kt)

    def phase_attn(l, lam_init):
        k.push()
        gsub = k.sb([128, 128], F32, "gsub"); bload(gsub[:], Wt["diff_subln"][l], W=[gsub])
        k.ts("dve", gsub[:], gsub[:], 1.0 - lam_init, None, ALU.mult, R=[gsub], W=[gsub])
        for s in seqs:
            k.push()
            n = s["nt"] * 128; q0 = s["t0"] * 128
            nctx = PAST if s["ctx"] else 0
            nkt = (n + nctx) // 128
            QC = min(512, n); nqt = QC // 128
            V = k.sb([128, nkt, 129], BF16, "V"); kTa = k.sb([128, nkt * 128], BF16, "kTa"); kTb = k.sb([128, nkt * 128], BF16, "kTb")
            k.memset("dve", V[:], 1.0, W=[V])

            def load_keys(dst, src2d):
                k.dma(dst[:, 0:n], src2d[:, q0:q0 + n], W=[dst])
                if nctx:
                    k.dma(dst[:, n:n + nctx], src2d[:, NT:NT + nctx], W=[dst])

            def load_V(src, h):
                k.dma(V[:, 0:n // 128, 0:128], src[q0:q0 + n, h * 128:(h + 1) * 128].rearrange("(c p) e -> p c e", p=128), W=[V])
                if nctx:
                    k.dma(V[:, n // 128:nkt, 0:128], src[NT:NT + nctx, h * 128:(h + 1) * 128].rearrange("(c p) e -> p c e", p=128), W=[V])

            for h in range(4):
                load_keys(kTa, dkT[h]); load_V(dvs, h)
                for qc in range(n // QC):
                    qs = slice(q0 + qc * QC, q0 + (qc + 1) * QC)
                    qp = []
                    for m in range(2):
                        nm = f"qTd{m}_{qc % 2}"
                        fresh = nm not in k.cache
                        t_ = k.tile(nm, [128, QC], BF16)
                        if fresh:
                            k.memset("dve", t_[64 * (1 - m):64 * (1 - m) + 64, :], 0.0, W=[t_])
                        k.dma(t_[64 * m:64 * m + 64, :], dqT[h][64 * m:64 * m + 64, qs], W=[t_])
                        qp.append(t_)
                    Oa = k.tile("Oa", [128, nqt, 128], F32); ob = k.tile("odb", [128, nqt * 128], BF16)
                    for m in range(2):
                        pp = slice(64 * m, 64 * m + 64)
                        attn_core(lambda kt: [(kTa[:, kt * 128:(kt + 1) * 128], qp[m][:, :])], V, nkt, QC, 0.125, R=[kTa, qp[m]])
                        for qt in range(nqt):
                            O = PS[2 + qt]
                            rinv = k.tile("rinv", [128, 1], F32)
                            k.op("dve", lambda e, O=O, rinv=rinv: e.reciprocal(out=rinv[:], in_=O[:, 128:129]), R=[O], W=[rinv])
                            if m == 0:
                                k.ts("dve", Oa[:, qt, :], O[:, 0:128], rinv[:, 0:1], None, ALU.mult, R=[O, rinv], W=[Oa])
                            else:
                                k.tt("dve", rinv[:], rinv[:], neglam[:], ALU.mult, R=[rinv, neglam], W=[rinv])
                                od = k.tile("od", [128, 128], F32)
                                k.stt("dve", od[:], O[:, 0:128], rinv[:, 0:1], Oa[:, qt, :], ALU.mult, ALU.add, R=[O, rinv, Oa], W=[od])
                                ssd = k.tile("ssd", [128, 1], F32); junk = k.tile("ajunk", [128, 128], BF16)
                                k.act(junk[:], od[:], AF.Square, R=[od], W=[junk, ssd], accum_out=ssd[:])
                                rstd(ssd, 1.0 / 128)
                                k.stt("dve", ob[:, qt * 128:(qt + 1) * 128], od[:], ssd[:, 0:1], gsub[:], ALU.mult, ALU.mult, R=[od, ssd, gsub], W=[ob])
                    store_T(ob, nqt, 6, lambda: oT[0, h][:, qs].rearrange("p (i c) -> p i c", c=128), R=[ob])
            for h in range(4):
                load_keys(kTa, mkTn[h]); load_keys(kTb, mkTr[h // 2]); load_V(mvs, h)
                pp = slice(64 * (h % 2), 64 * (h % 2) + 64)
                for qc in range(n // QC):
                    qs = slice(q0 + qc * QC, q0 + (qc + 1) * QC)
                    qTn = k.tile(f"qTn{qc % 2}", [128, QC], BF16)
                    nm = f"qTr{h % 2}_{qc % 2}"
                    fresh = nm not in k.cache
                    qTr = k.tile(nm, [128, QC], BF16)
                    if fresh:
                        k.memset("dve", qTr[64 * (1 - h % 2):64 * (1 - h % 2) + 64, :], 0.0, W=[qTr])
                    k.dma(qTn[:], mqTn[h][:, qs], W=[qTn]); k.dma(qTr[pp, :], mqTr[h // 2][pp, qs], W=[qTr])
                    attn_core(lambda kt: [(kTa[:, kt * 128:(kt + 1) * 128], qTn[:, :]), (kTb[:, kt * 128:(kt + 1) * 128], qTr[:, :])],
                              V, nkt, QC, 192 ** -0.5, R=[kTa, kTb, qTn, qTr])
                    omb = k.tile("omb", [128, nqt * 128], BF16)
                    for qt in range(nqt):
                        O = PS[2 + qt]
                        rinv = k.tile("rinv", [128, 1], F32)
                        k.op("dve", lambda e, O=O, rinv=rinv: e.reciprocal(out=rinv[:], in_=O[:, 128:129]), R=[O], W=[rinv])
                        k.ts("dve", omb[:, qt * 128:(qt + 1) * 128], O[:, 0:128], rinv[:, 0:1], None, ALU.mult, R=[O, rinv], W=[omb])
                    store_T(omb, nqt, 6, lambda: oT[2, h][:, qs].rearrange("p (i c) -> p i c", c=128), R=[omb])
            k.pop()
            k.barrier()
        k.pop()

    def mod_tiles(l, which):
        out = {}
        for name, idx in which:
            out[name] = []
            for v in range(2):
                a = k.sb([128, D], F32, name)
                bload(a[:], modrow[v, idx * D:(idx + 1) * D], W=[a])
                out[name].append(a)
        return out

    def phase_merge(l, xsrc):
        k.push()
        k.reorder = True
        wg = k.sb([128, 8, 3072], BF16, "wg"); wload(wg, Wt["w_in"][l][:, NPH_A:D_IN], nchunk=6)
        wb = k.sb([128, 12, 1024], BF16, "wb")
        for i in range(3):
            wload(T(wb.h[:, 4 * i:4 * i + 4, :], wb.b), Wt["w_branch"][l, i])
        wo = k.sb([128, 8, 1024], BF16, "wo"); wload(wo, Wt["w_out"][l], nchunk=2)
        m = mod_tiles(l, [("G1", 2), ("B2", 3), ("A2", 4)])
        n2 = k.sb([128, D], F32, "n2"); bload(n2[:], Wt["norm2"][l], W=[n2])
        for v in range(2):
            a = m["A2"][v]
            k.stt("dve", a[:], a[:], 1.0, n2[:], ALU.add, ALU.mult, R=[a, n2], W=[a])

        def loads(t):
            cols = slice(t * 128, (t + 1) * 128)
            xt = k.tile(f"x{t % 2}", [128, D], F32); hT = k.tile(f"hT{t % 2}", [128, 8, 128], BF16)
            oTt = k.tile(f"oTt{t % 2}", [128, 12, 128], BF16)
            k.dma(xt[:], xsrc[cols, :], W=[xt]); k.dma(hT[:], hTs[t], W=[hT])
            for i in range(3):
                k.dma(oTt[:, 4 * i:4 * i + 4, :], oT[i][:, :, cols].rearrange("h p c -> p h c"), W=[oTt])
        loads(0)
        for t in range(NTT):
            if t + 1 < NTT:
                loads(t + 1)
            v = seq_of(t)["var"]; grow = slice(t * 128, (t + 1) * 128)
            xt = k.tile(f"x{t % 2}", [128, D], F32); hT = k.tile(f"hT{t % 2}", [128, 8, 128], BF16)
            oTt = k.tile(f"oTt{t % 2}", [128, 12, 128], BF16)
            mer = k.tile("mer", [128, D], F32)
            for nn in range(2):
                cs = slice(nn * 512, (nn + 1) * 512)
                for i in range(3):
                    j = nn * 3 + i
                    bg = PS[j % 2]; bo = PS[2 + j % 2]
                    for kc in range(8):
                        k.mm(bg[:], hT[:, kc, :], wg[:, kc, i * 1024 + nn * 512:i * 1024 + nn * 512 + 512], kc == 0, kc == 7, R=[hT, wg], W=[bg])
                    for kc in range(4):
                        k.mm(bo[:], oTt[:, i * 4 + kc, :], wb[:, i * 4 + kc, cs], kc == 0, kc == 3, R=[oTt, wb], W=[bo])
                    sig = k.tile(f"sig{j % 2}", [128, 512], F32)
                    k.act(sig[:], bg[:], AF.Sigmoid, R=[bg], W=[sig])
                    if i == 0:
                        k.tt("dve", mer[:, cs], sig[:], bo[:], ALU.mult, R=[sig, bo], W=[mer])
                    else:
                        tmpm = k.tile(f"tmpm{j % 2}", [128, 512], F32)
                        k.tt("dve", tmpm[:], sig[:], bo[:], ALU.mult, R=[sig, bo], W=[tmpm])
                        k.tt("pool", mer[:, cs], mer[:, cs], tmpm[:], ALU.add, R=[mer, tmpm], W=[mer])
            merb = k.tile("merb", [128, D], BF16)
            k.cp("act", merb[:], mer[:], R=[mer], W=[merb])
            mT = k.tile("mT", [128, 8, 128], BF16)
            transposes([merb[:, i * 128:(i + 1) * 128] for i in range(8)], 6, mT, R=[merb])
            x1 = k.tile("xout", [128, D], F32)
            for nn in range(2):
                cs = slice(nn * 512, (nn + 1) * 512)
                bo = PS[4 + nn]
                for kc in range(8):
                    k.mm(bo[:], mT[:, kc, :], wo[:, kc, cs], kc == 0, kc == 7, R=[mT, wo], W=[bo])
                tmpo = k.tile(f"tmpo{nn}", [128, 512], F32)
                k.tt("dve", tmpo[:], bo[:], m["G1"][v][:, cs], ALU.mult, R=[bo, m["G1"][v]], W=[tmpo])
                k.tt("pool", x1[:, cs], xt[:, cs], tmpo[:], ALU.add, R=[xt, tmpo], W=[x1])
            k.dma(xa[grow, :], x1[:], R=[x1])
            junk = k.tile("junk", [128, D], BF16); ssx = k.tile("ssx", [128, 1], F32)
            k.act(junk[:], x1[:], AF.Square, R=[x1], W=[junk, ssx], accum_out=ssx[:])
            rstd(ssx, 1.0 / D)
            tmp = k.tile("htmp", [128, D], F32); hb = k.tile("hb", [128, D], BF16)
            k.stt("dve", tmp[:], x1[:], ssx[:, 0:1], m["A2"][v][:], ALU.mult, ALU.mult, R=[x1, ssx, m["A2"][v]], W=[tmp])
            k.tt("pool", hb[:], tmp[:], m["B2"][v][:], ALU.add, R=[tmp, m["B2"][v]], W=[hb])
            h2T = k.tile("h2T", [128, 8, 128], BF16)
            transposes([hb[:, i * 128:(i + 1) * 128] for i in range(8)], 7, h2T, R=[hb])
            k.dma(h2Ts[t], h2T[:], R=[h2T])
        k.pop()
        k.reorder = False

    def phase_mlp(l, xdst):
        k.push()
        wu = k.sb([128, 8, 4096], BF16, "wu"); wload(wu, Wt["w_up"][l], nchunk=8)
        wd = k.sb([128, 32, 1024], BF16, "wd"); wload(wd, Wt["w_down"][l], nchunk=4)
        m = mod_tiles(l, [("G2", 5)])
        uT = k.sb([128, 32, 512], BF16, "uT"); uTb = [k.buf() for _ in range(32)]
        groups = []
        for s in seqs:
            ts_ = list(range(s["t0"], s["t0"] + s["nt"]))
            for i in range(0, len(ts_), 4):
                groups.append((s["var"], ts_[i:i + 4]))
        for gi, (v, tiles) in enumerate(groups):
            ncol = len(tiles) * 128
            h2 = k.tile("h2g", [128, 8, 512], BF16)
            for j, t in enumerate(tiles):
                k.dma(h2[:, :, j * 128:(j + 1) * 128], h2Ts[t], W=[h2])
            for f in range(32):
                bank = PS[f % 2]
                for kc in range(8):
                    k.mm(bank[:, 0:ncol], wu[:, kc, f * 128:(f + 1) * 128], h2[:, kc, 0:ncol], kc == 0, kc == 7, R=[wu, h2], W=[bank])
                r = k.tile(f"relu{f % 2}", [128, 512], F32)
                k.act(r[:, 0:ncol], bank[:, 0:ncol], AF.Relu, R=[bank], W=[r])
                k.tt("dve" if f % 2 else "pool", uT[:, f, 0:ncol], r[:, 0:ncol], r[:, 0:ncol], ALU.mult, R=[r], W=[uTb[f]])
            for j, t in enumerate(tiles):
                rows = slice(t * 128, (t + 1) * 128)
                x1 = k.tile(f"mx{j % 2}", [128, D], F32); xo = k.tile("mxo", [128, D], F32)
                k.dma(x1[:], xa[rows, :], W=[x1])
                for nn in range(2):
                    cs = slice(nn * 512, (nn + 1) * 512)
                    bank = PS[2 + (j * 2 + nn) % 4]
                    for f in range(32):
                        k.mm(bank[:], uT[:, f, j * 128:(j + 1) * 128], wd[:, f, cs], f == 0, f == 31, R=[uTb[f], wd], W=[bank])
                    tmp = k.tile(f"mtmp{nn}", [128, 512], F32)
                    k.tt("dve", tmp[:], bank[:], m["G2"][v][:, cs], ALU.mult, R=[bank, m["G2"][v]], W=[tmp])
                    k.tt("pool", xo[:, cs], x1[:, cs], tmp[:], ALU.add, R=[x1, tmp], W=[xo])
                k.dma(xdst[rows, :], xo[:], R=[xo])
        k.pop()

    nph = 0
    for l in range(L):
        lam_init = 0.8 - 0.6 * math.exp(-0.3 * l)
        phases = [lambda: layer_consts(l), lambda: phase_mod(l), lambda: phase_A(l, x_all if l == 0 else xb),
                  lambda: phase_ret(l), lambda: phase_attn(l, lam_init), lambda: phase_merge(l, x_all if l == 0 else xb),
                  lambda: phase_mlp(l, xb if l < L - 1 else y_all)]
        for ph in phases:
            if nph < stop:
                ph(); k.barrier()
            nph += 1
    k.emit()
    return k


def _rope_table(NS, NP):
    n_rows = NS // GRID_W
    row = np.repeat(np.arange(n_rows, dtype=np.float32), GRID_W)
    col = np.tile(np.arange(GRID_W, dtype=np.float32), n_rows)
    inv = (10000.0 ** (-np.arange(0, 32, 2, dtype=np.float32) / 32)).astype(np.float32)
    ar = (row[:, None] * inv[None, :]).astype(np.float32); ac = (col[:, None] * inv[None, :]).astype(np.float32)
    cr, sr, cc, sc = np.cos(ar), np.sin(ar), np.cos(ac), np.sin(ac)
    tab = np.concatenate([cr, cr, cc, cc, -sr, sr, -sc, sc], axis=1).astype(np.float32)
    ptab = np.concatenate([np.ones((2 * NP, 64), np.float32), np.zeros((2 * NP, 64), np.float32)], axis=1)
    return np.ascontiguousarray(np.concatenate([tab, ptab], axis=0))


def _ret_consts():
    C = 128
    j = np.arange(C, dtype=np.float32)[:, None]; i = np.arange(C, dtype=np.float32)[None, :]
    relf = np.maximum(i - j, 0.0); maskf = (i >= j).astype(np.float32)
    relb = np.maximum(j - i, 0.0); maskb = (j > i).astype(np.float32)
    retc = np.stack([relf, maskf, relb, maskb]).astype(np.float32)
    p = np.arange(C, dtype=np.float32)
    retcol = np.stack([p + 1.0, C - p, C - 1.0 - p, p], axis=1).astype(np.float32)
    return np.ascontiguousarray(retc), np.ascontiguousarray(retcol)


_CACHE = {}


def _run(inputs, NS, NP, PAST, dbg=(), stop=99):
    key = (NS, NP, PAST, tuple(dbg), stop)
    if key not in _CACHE:
        _CACHE[key] = build(NS, NP, PAST, dbg, stop)
    kb = _CACHE[key]
    f = lambda a: np.ascontiguousarray(np.asarray(a, dtype=np.float32))
    rope_tab = _rope_table(NS, NP); retc, retcol = _ret_consts()
    xp = f(inputs["x_prompt"]); xs = f(inputs["x_sample"])
    L = DEPTH
    in_maps = []
    for c in range(8):
        m = {
            "x_all": np.ascontiguousarray(np.concatenate([xs[c], xp[2 * c], xp[2 * c + 1]], axis=0)),
            "cvec": np.ascontiguousarray(np.stack([f(inputs["c"])[c], f(inputs["c_ctx"])])),
            "cdk": f(inputs["cache_diff_k"])[c].reshape(L, PAST, 512),
            "cdv": f(inputs["cache_diff_v"])[c].reshape(L, PAST, 512),
            "cckv": f(inputs["cache_mla_ckv"])[c], "ckr": f(inputs["cache_mla_krope"])[c],
            "sret": f(inputs["state_ret"])[c],
            "rope_tab": rope_tab, "retc": retc, "retcol": retcol,
        }
        for n, _ in W_SPECS:
            m[n] = f(inputs[n])
        in_maps.append({a: np.ascontiguousarray(b) for a, b in m.items()})
    res = run_bass_kernel_spmd(kb.nc, in_maps, core_ids=list(range(8)))
    return res.results


def kernel(**inputs):
    NS = inputs["x_sample"].shape[1]; NP = inputs["x_prompt"].shape[1]; PAST = inputs["cache_diff_k"].shape[2]
    B = inputs["x_prompt"].shape[0]
    r = _run(inputs, NS, NP, PAST)
    L = DEPTH
    y_prompt = np.zeros((B, NP, D), np.float32); y_sample = np.zeros((8, NS, D), np.float32)
    ndk = np.zeros((B, L, NP, 8, 64), np.float32); ndv = np.zeros((B, L, NP, 4, 128), np.float32)
    nckv = np.zeros((B, L, NP, 256), np.float32); nkr = np.zeros((B, L, NP, 64), np.float32)
    nsr = np.zeros((B, L, 2, 4, 64, 128), np.float32)
    for c in range(8):
        ya = r[c]["y_all"]
        y_sample[c] = ya[:NS]
        for p in range(2):
            b = 2 * c + p
            y_prompt[b] = ya[NS + p * NP: NS + (p + 1) * NP]
            ndk[b] = r[c]["ndk"][p].reshape(L, NP, 8, 64); ndv[b] = r[c]["ndv"][p].reshape(L, NP, 4, 128)
            nckv[b] = r[c]["nckv"][p]; nkr[b] = r[c]["nkr"][p]; nsr[b] = r[c]["nsr"][p]
    return (y_prompt, y_sample, ndk, ndv, nckv, nkr, nsr)
```

```python
import math
import os
import numpy as np
import concourse.bass as bass
import concourse.mybir as mybir
from concourse.bass_utils import run_bass_kernel_spmd

F32 = mybir.dt.float32
BF16 = mybir.dt.bfloat16
AF = mybir.ActivationFunctionType
ALU = mybir.AluOpType
AX = mybir.AxisListType

D = 1024
DEPTH = 2
EPS = 1e-6
D_IN = 6848
NPH_A = 3776
GRID_W = 64


class Buf:
    __slots__ = ("name", "w", "r")

    def __init__(self, name):
        self.name = name
        self.w = None
        self.r = []


class Op:
    __slots__ = ("eng", "fn", "deps", "dma", "sig", "need", "selfsig", "odeps", "reorder")

    def __init__(self, eng, fn, dma):
        self.eng = eng
        self.fn = fn
        self.deps = []
        self.dma = dma
        self.sig = None
        self.need = False
        self.selfsig = False
        self.odeps = []
        self.reorder = False


class T:
    def __init__(self, h, buf):
        self.h = h
        self.b = buf
        self.tok = None

    def __getitem__(self, k):
        return self.h[k]


ENGS = ("pe", "act", "dve", "pool", "sp")
NDMASEM = 16


class KB:
    def __init__(self):
        self.nc = bass.Bass("TRN2", target_bir_lowering=False)
        nc = self.nc
        self.e = {"pe": nc.tensor, "act": nc.scalar, "dve": nc.vector, "pool": nc.gpsimd, "sp": nc.sync}
        self.ops = {k: [] for k in ENGS}
        self.allops = []
        self.csem = {k: nc.alloc_semaphore("c_" + k) for k in ENGS}
        self.dsem = {q: [nc.alloc_semaphore(f"d_{q}{i}") for i in range(NDMASEM)] for q in ("sp", "pool", "act")}
        self.dcnt = {q: 0 for q in self.dsem}
        self.dlast = {}
        self.bar = None
        self.bar_seen = {k: None for k in ENGS}
        self.sb_off = 16512
        self.sb_mark = []
        self.nbuf = 0
        self.names = 0
        self.cache = {}
        self.reorder = False

    def buf(self, name="b"):
        self.nbuf += 1
        return Buf(f"{name}{self.nbuf}")

    def sb(self, shape, dt, name="t"):
        self.names += 1
        nbytes = int(np.prod(shape[1:])) * (4 if dt == F32 else 2)
        nbytes = (nbytes + 63) // 64 * 64
        h = self.nc.alloc_sbuf_tensor_at(f"{name}_{self.names}", list(shape), dt, offset=self.sb_off)
        self.sb_off += nbytes
        assert self.sb_off <= 229344, f"SBUF overflow {self.sb_off}"
        return T(h, self.buf(name))

    def push(self):
        self.sb_mark.append((self.sb_off, dict(self.cache)))

    def pop(self):
        self.sb_off, self.cache = self.sb_mark.pop()

    def tile(self, name, shape, dt):
        t = self.cache.get(name)
        if t is None:
            t = self.sb(shape, dt, name)
            self.cache[name] = t
        return t

    def dram(self, name, shape, dt, kind="Internal"):
        return self.nc.dram_tensor(name, list(shape), dt, kind=kind).ap()

    def op(self, eng, fn, R=(), W=(), dma=False):
        o = Op(eng, fn, dma)
        o.reorder = self.reorder
        deps = {}
        toks = [t.tok for t in R if isinstance(t, T) and t.tok is not None]
        if toks:
            W = list(W) + toks
        for t in R:
            b = t.b if isinstance(t, T) else t
            if b.w is not None:
                deps[id(b.w)] = (b.w, "raw")
        for t in W:
            b = t.b if isinstance(t, T) else t
            if b.w is not None:
                deps[id(b.w)] = (b.w, "waw")
            for r in b.r:
                if id(r) not in deps:
                    deps[id(r)] = (r, "war")
        for d, kind in deps.values():
            if d is o:
                continue
            if not d.dma and not dma and d.eng == eng:
                if eng == "pe":
                    o.odeps.append(d)
                    continue
            o.deps.append(d)
        if self.bar is not None and self.bar_seen[eng] is not self.bar:
            o.deps.append(self.bar)
            self.bar_seen[eng] = self.bar
        if dma:
            q = eng
            i = self.dcnt[q] % (2 if q == "pool" else NDMASEM)
            self.dcnt[q] += 1
            s = self.dsem[q][i]
            prev = self.dlast.get((q, i))
            if prev is not None:
                o.deps.append(prev[0])
                cnt = prev[1] + 1
            else:
                cnt = 1
            o.sig = (s, 16 * cnt)
            self.dlast[(q, i)] = (o, cnt)
            o.need = True
        for d in o.deps:
            d.need = True
        for t in R:
            b = t.b if isinstance(t, T) else t
            b.r.append(o)
        for t in W:
            b = t.b if isinstance(t, T) else t
            b.w = o
            b.r = []
        self.ops[eng].append(o)
        self.allops.append(o)
        return o

    def barrier(self):
        deps = []
        for k in ENGS:
            for o in reversed(self.ops[k]):
                if not o.dma:
                    deps.append(o)
                    break
        for (q, i), (o, c) in self.dlast.items():
            deps.append(o)
        b = Op("sp", lambda e: e.sem_inc(self.csem["sp"], 1), False)
        b.deps = [d for d in deps]
        for d in deps:
            d.need = True
        b.need = True
        b.selfsig = True
        self.ops["sp"].append(b)
        self.allops.append(b)
        self.bar = b

    def dma(self, out, in_, R=(), W=(), q="sp", **kw):
        return self.op(q, lambda e: e.dma_start(out=out, in_=in_, **kw), R, W, dma=True)

    def mm(self, out, lhsT, rhs, start, stop, R=(), W=()):
        return self.op("pe", lambda e: e.matmul(out, lhsT=lhsT, rhs=rhs, start=start, stop=stop), R, W)

    def tr(self, out, in_, ident, R=(), W=()):
        return self.op("pe", lambda e: e.transpose(out=out, in_=in_, identity=ident), R, W)

    def act(self, out, in_, func, R=(), W=(), **kw):
        return self.op("act", lambda e: e.activation(out=out, in_=in_, func=func, **kw), R, W)

    def tt(self, eng, out, in0, in1, op, R=(), W=()):
        return self.op(eng, lambda e: e.tensor_tensor(out=out, in0=in0, in1=in1, op=op), R, W)

    def ts(self, eng, out, in0, s1, s2, op0, op1=None, R=(), W=()):
        if op1 is None:
            return self.op(eng, lambda e: e.tensor_scalar(out=out, in0=in0, scalar1=s1, scalar2=None, op0=op0), R, W)
        return self.op(eng, lambda e: e.tensor_scalar(out=out, in0=in0, scalar1=s1, scalar2=s2, op0=op0, op1=op1), R, W)

    def stt(self, eng, out, in0, scalar, in1, op0, op1, R=(), W=()):
        return self.op(eng, lambda e: e.scalar_tensor_tensor(out=out, in0=in0, scalar=scalar, in1=in1, op0=op0, op1=op1), R, W)

    def cp(self, eng, out, in_, R=(), W=()):
        if eng == "act":
            return self.op("act", lambda e: e.activation(out=out, in_=in_, func=AF.Identity), R, W)
        return self.op(eng, lambda e: e.tensor_copy(out=out, in_=in_), R, W)

    def red(self, eng, out, in_, R=(), W=()):
        return self.op(eng, lambda e: e.tensor_reduce(out=out, in_=in_, axis=AX.X, op=ALU.add), R, W)

    def memset(self, eng, ap, val, W=()):
        return self.op(eng, lambda e: e.memset(ap, val), (), W)

    def sched_seg(self, ops):
        idx = {id(o): i for i, o in enumerate(ops)}
        fin = {}
        per = {kx: [o for o in ops if o.eng == kx] for kx in ENGS}
        free = {kx: 0.0 for kx in ENGS}
        DUR = {"pe": 0.3, "act": 0.6, "dve": 0.5, "pool": 0.6, "sp": 0.05}
        out = []
        Wn = 96
        nleft = len(ops)
        while nleft:
            best = None
            for kx in ENGS:
                lst = per[kx]
                fk = free[kx]
                for j in range(min(Wn, len(lst))):
                    o = lst[j]
                    t = fk
                    ok = True
                    for d in o.deps:
                        f = fin.get(id(d))
                        if f is None:
                            if id(d) in idx:
                                ok = False
                                break
                            continue
                        f += 0.2
                        if f > t:
                            t = f
                    if ok:
                        for d in o.odeps:
                            f = fin.get(id(d))
                            if f is None:
                                if id(d) in idx:
                                    ok = False
                                    break
                                continue
                            if f > t:
                                t = f
                    if not ok:
                        continue
                    key = (t, idx[id(o)])
                    if best is None or key < best[0]:
                        best = (key, kx, j, o, t)
                    if t <= fk:
                        break
            assert best is not None, "scheduler stuck"
            _, kx, j, o, t = best
            per[kx].pop(j)
            if o.dma:
                occ = 0.8 if kx == "pool" else 0.05
                fin[id(o)] = t + occ + 2.5
            else:
                occ = DUR[kx]
                fin[id(o)] = t + occ
            free[kx] = t + occ
            out.append(o)
            nleft -= 1
        return out

    def finalize(self):
        segs = []
        cur = []
        for o in self.allops:
            if o.selfsig:
                segs.append((cur, o)); cur = []
            else:
                cur.append(o)
        if cur:
            segs.append((cur, None))
        bars = set(id(b) for _, b in segs if b is not None)
        final = {kx: [] for kx in ENGS}
        lastdma = {}
        prevbar = None
        order_all = []
        for ops, bar in segs:
            for o in ops:
                o.deps = [d for d in o.deps if id(d) not in bars]
            order = self.sched_seg(ops)
            seen = set()
            for o in order:
                if o.eng not in seen:
                    seen.add(o.eng)
                    if prevbar is not None:
                        o.deps.append(prevbar)
                final[o.eng].append(o)
                order_all.append(o)
                if o.dma:
                    lastdma[o.sig[0].num] = o
            if bar is not None:
                deps = []
                for kx in ENGS:
                    for o in reversed(final[kx]):
                        if not o.dma:
                            deps.append(o)
                            break
                deps += list(lastdma.values())
                bar.deps = deps
                for d in deps:
                    d.need = True
                final["sp"].append(bar)
                order_all.append(bar)
                prevbar = bar
        self.ops = final
        self.allops = order_all

    def emit(self):
        self.finalize()
        cnt = {k: 0 for k in ENGS}
        for k_ in ENGS:
            for o in self.ops[k_]:
                if o.dma:
                    continue
                if o.need:
                    cnt[o.eng] += 1
                    o.sig = (self.csem[o.eng], cnt[o.eng])
        for k in ENGS:
            eng = self.e[k]
            waited = {}
            for o in self.ops[k]:
                for d in o.deps:
                    s, v = d.sig
                    key = s.num
                    if waited.get(key, 0) >= v:
                        continue
                    waited[key] = v
                    eng.wait_ge(s, v)
                ins = o.fn(eng)
                if o.dma:
                    ins.then_inc(o.sig[0], 16)
                elif o.need and not o.selfsig:
                    ins.then_inc(o.sig[0], 1)


def rstd_ops(k, ss, n_inv, mh, R=(), W=()):
    k.ts("pool", ss, ss, n_inv, EPS, ALU.mult, ALU.add, R=R, W=W)
    k.tt("pool", ss, ss, mh, ALU.pow, R=list(R) + list(W), W=W)


W_SPECS = [
    ("w_mod", [DEPTH, D, 6 * D]), ("b_mod", [DEPTH, 6 * D]), ("norm1", [DEPTH, D]), ("norm2", [DEPTH, D]),
    ("w_in", [DEPTH, D, D_IN]), ("diff_qn", [DEPTH, 64]), ("diff_kn", [DEPTH, 64]), ("diff_lambda", [DEPTH, 4, 64]),
    ("diff_subln", [DEPTH, 128]), ("ret_decay", [DEPTH, 2, 4]), ("ret_gn", [DEPTH, 128]),
    ("mla_qa_norm", [DEPTH, 384]), ("w_mla_qb", [DEPTH, 384, 768]), ("mla_kva_norm", [DEPTH, 256]),
    ("w_mla_kvb", [DEPTH, 256, 1024]), ("mla_qn", [DEPTH, 192]), ("mla_kn", [DEPTH, 192]),
    ("w_branch", [DEPTH, 3, 512, D]), ("w_out", [DEPTH, D, D]), ("w_up", [DEPTH, D, 4 * D]), ("w_down", [DEPTH, 4 * D, D]),
]


def build(NS, NP, PAST, dbg=(), stop=99):
    k = KB()
    nc = k.nc
    NT = NS + 2 * NP
    NTT = NT // 128
    NK = NT + PAST
    L = DEPTH
    ein = lambda n, s: k.dram(n, s, F32, kind="ExternalInput")
    eout = lambda n, s: k.dram(n, s, F32, kind="ExternalOutput")
    x_all = ein("x_all", [NT, D]); cvec = ein("cvec", [2, D])
    cdk = ein("cdk", [L, PAST, 512]); cdv = ein("cdv", [L, PAST, 512])
    cckv = ein("cckv", [L, PAST, 256]); ckr = ein("ckr", [L, PAST, 64]); sret = ein("sret", [L, 2, 4, 64, 128])
    Wt = {n: ein(n, s) for n, s in W_SPECS}
    rope_tab = ein("rope_tab", [NT, 128]); retc = ein("retc", [4, 128, 128]); retcol = ein("retcol", [128, 4])
    y_all = eout("y_all", [NT, D])
    ndk = eout("ndk", [2, L, NP, 512]); ndv = eout("ndv", [2, L, NP, 512])
    nckv = eout("nckv", [2, L, NP, 256]); nkr = eout("nkr", [2, L, NP, 64]); nsr = eout("nsr", [2, L, 2, 4, 64, 128])

    def scr(n, s, dt):
        return k.dram(n, s, dt, kind=("ExternalOutput" if n in dbg else "Internal"))
    modrow = scr("modrow", [2, 6 * D], F32)
    hTs = scr("hTs", [NTT, 128, 8, 128], BF16); h2Ts = scr("h2Ts", [NTT, 128, 8, 128], BF16)
    dqT = scr("dqT", [4, 128, NT], BF16); dkT = scr("dkT", [4, 128, NK], BF16); dvs = scr("dvs", [NK, 512], BF16)
    rqT = scr("rqT", [2, 128, NT], BF16); rkT = scr("rkT", [2, 128, NT], BF16); QQ = scr("QQ", [4, 128, NT], BF16)
    kks = scr("kks", [NT, 512], BF16); rvs = scr("rvs", [NT, 512], BF16); rgs = scr("rgs", [NT, 512], F32)
    mqTn = scr("mqTn", [4, 128, NT], BF16); mqTr = scr("mqTr", [2, 128, NT], BF16)
    mkTn = scr("mkTn", [4, 128, NK], BF16); mkTr = scr("mkTr", [2, 128, NK], BF16); mvs = scr("mvs", [NK, 512], BF16)
    oT = scr("oT", [3, 4, 128, NT], BF16)
    xa = scr("xa", [NT, D], F32); xb = scr("xb", [NT, D], F32)

    seqs = [dict(t0=0, nt=NS // 128, ctx=True, var=0, pi=None),
            dict(t0=NS // 128, nt=NP // 128, ctx=False, var=1, pi=0),
            dict(t0=(NS + NP) // 128, nt=NP // 128, ctx=False, var=1, pi=1)]

    def seq_of(t):
        for s in seqs:
            if s["t0"] <= t < s["t0"] + s["nt"]:
                return s

    PS = []
    for i in range(8):
        h = nc.alloc_psum_tensor(f"ps{i}", [128, 512], F32)
        PS.append(T(h, k.buf("ps")))
        PS[-1].tok = k.buf("pstok")
    psb = lambda i: PS[i][:].bitcast(BF16)

    identf = k.sb([128, 128], F32, "identf"); ident = k.sb([128, 128], BF16, "ident")
    mh = k.sb([128, 8], F32, "mh")
    k.memset("pool", identf[:], 0.0, W=[identf])
    k.op("pool", lambda e: e.affine_select(out=identf[:], in_=identf[:], pattern=[[-1, 128]], compare_op=ALU.not_equal,
                                           fill=1.0, base=0, channel_multiplier=1), R=[identf], W=[identf])
    k.cp("dve", ident[:], identf[:], R=[identf], W=[ident])
    k.memset("pool", mh[:], -0.5, W=[mh])
    rc = k.sb([128, 4, 128], F32, "retc")
    k.dma(rc[:], retc.rearrange("c p i -> p c i"), W=[rc])
    rcol = k.sb([128, 4], F32, "retcol")
    k.dma(rcol[:], retcol, W=[rcol])
    MT = k.sb([128, 4, 128], F32, "MT"); qdc = k.sb([128, 4, 2], F32, "qdc"); kdc = k.sb([128, 4, 2], F32, "kdc")
    decC = k.sb([128, 4, 128], F32, "decC"); lg = k.sb([128, 2, 4], F32, "lg"); neglam = k.sb([128, 1], F32, "neglam")

    def rstd(ss, n_inv, extraR=()):
        k.ts("pool", ss[:], ss[:], n_inv, EPS, ALU.mult, ALU.add, R=[ss] + list(extraR), W=[ss])
        k.tt("pool", ss[:], ss[:], mh[:, 0:ss.h.shape[1]] if len(ss.h.shape) == 2 else mh[:], ALU.pow, R=[ss, mh], W=[ss])

    def wload(dst, src_ap, nchunk=1):
        n = src_ap.shape[-1]
        step = max(256, ((n + nchunk - 1) // nchunk + 255) // 256 * 256)
        for c0 in range(0, n, step):
            c1 = min(n, c0 + step)
            k.dma(dst[:, :, c0:c1], src_ap[:, c0:c1].rearrange("(kc p) n -> p kc n", p=128), W=[dst], q="pool")

    def bload(dst_ap, src_ap, W):
        k.dma(dst_ap, src_ap.partition_broadcast(128), W=W)

    def transposes(src_aps, bank, dst, R, n_out_part=128):
        pb = psb(bank)
        n = len(src_aps)
        for i, a in enumerate(src_aps):
            k.tr(pb[:, i * 128:(i + 1) * 128], a, ident[:], R=list(R) + [ident], W=[PS[bank]])
        k.cp("dve", dst[:].rearrange("p n c -> p (n c)"), pb[:, 0:n * 128], R=[PS[bank]], W=[dst])

    def layer_consts(l):
        k.push()
        lam_init = 0.8 - 0.6 * math.exp(-0.3 * l)
        dl = k.sb([128, 4, 64], F32); pr = k.sb([128, 2, 64], F32); sm = k.sb([128, 2], F32)
        bload(dl[:].rearrange("p a d -> p (a d)"), Wt["diff_lambda"][l].rearrange("a d -> (a d)"), W=[dl])
        dl4 = dl[:].rearrange("p (a b) d -> p a b d", b=2)
        k.tt("dve", pr[:], dl4[:, :, 0, :], dl4[:, :, 1, :], ALU.mult, R=[dl], W=[pr])
        k.red("dve", sm[:], pr[:], R=[pr], W=[sm])
        k.act(sm[:], sm[:], AF.Exp, R=[sm], W=[sm])
        k.stt("dve", neglam[:], sm[:, 1:2], -lam_init, sm[:, 0:1], ALU.add, ALU.subtract, R=[sm], W=[neglam])
        bload(lg[:].rearrange("p a h -> p (a h)"), Wt["ret_decay"][l].rearrange("a h -> (a h)"), W=[lg])
        k.act(lg[:], lg[:], AF.Exp, R=[lg], W=[lg], scale=-1.0)
        k.act(lg[:], lg[:], AF.Ln, R=[lg], W=[lg], bias=1.0)
        k.ts("dve", lg[:], lg[:], -1.0, None, ALU.mult, R=[lg], W=[lg])
        tmp = k.sb([128, 4, 128], F32); tmp2 = k.sb([128, 4, 128], F32)
        for h in range(4):
            k.ts("dve", tmp[:, h, :], rc[:, 0, :], lg[:, 0, h:h + 1], None, ALU.mult, R=[rc, lg], W=[tmp])
            k.ts("dve", tmp2[:, h, :], rc[:, 2, :], lg[:, 1, h:h + 1], None, ALU.mult, R=[rc, lg], W=[tmp2])
        k.act(tmp[:], tmp[:], AF.Exp, R=[tmp], W=[tmp])
        k.act(tmp2[:], tmp2[:], AF.Exp, R=[tmp2], W=[tmp2])
        k.tt("dve", tmp[:], tmp[:], rc[:, 1:2, :].to_broadcast([128, 4, 128]), ALU.mult, R=[tmp, rc], W=[tmp])
        k.tt("dve", tmp2[:], tmp2[:], rc[:, 3:4, :].to_broadcast([128, 4, 128]), ALU.mult, R=[tmp2, rc], W=[tmp2])
        k.tt("dve", MT[:], tmp[:], tmp2[:], ALU.add, R=[tmp, tmp2], W=[MT])
        for d in range(2):
            k.ts("dve", qdc[:, :, d], lg[:, d, :], rcol[:, d:d + 1], None, ALU.mult, R=[lg, rcol], W=[qdc])
            k.ts("dve", kdc[:, :, d], lg[:, d, :], rcol[:, 2 + d:3 + d], None, ALU.mult, R=[lg, rcol], W=[kdc])
        k.act(qdc[:], qdc[:], AF.Exp, R=[qdc], W=[qdc])
        k.act(kdc[:], kdc[:], AF.Exp, R=[kdc], W=[kdc])
        dc = k.sb([128, 4], F32)
        k.act(dc[0:64, :], lg[0:64, 0, :], AF.Exp, R=[lg], W=[dc], scale=128.0)
        k.act(dc[64:128, :], lg[64:128, 1, :], AF.Exp, R=[lg], W=[dc], scale=128.0)
        k.cp("dve", decC[:], dc[:].unsqueeze(2).to_broadcast([128, 4, 128]), R=[dc], W=[decC])
        k.pop()
        return lam_init

    def phase_mod(l):
        k.push()
        cT = k.sb([128, 2, 8], F32); sT = k.sb([128, 2, 8], F32); rep = k.sb([128, 16, 128], BF16)
        for v in range(2):
            k.dma(cT[:, v, :], cvec[v].rearrange("(kc p) -> p kc", p=128), W=[cT], allow_slow_non_contiguous=True)
        k.act(sT[:], cT[:], AF.Silu, R=[cT], W=[sT])
        k.cp("dve", rep[:], sT[:].rearrange("p v c -> p (v c)").unsqueeze(2).to_broadcast([128, 16, 128]), R=[sT], W=[rep])
        wm = [k.sb([128, 8, 512], BF16) for _ in range(2)]
        bm = [k.sb([128, 512], F32) for _ in range(2)]
        res = [k.sb([128, 512], F32) for _ in range(2)]
        for n in range(12):
            w_, b_ = wm[n % 2], bm[n % 2]
            wload(w_, Wt["w_mod"][l][:, n * 512:(n + 1) * 512])
            bload(b_[:], Wt["b_mod"][l][n * 512:(n + 1) * 512], W=[b_])
            for v in range(2):
                bank = PS[v]
                for kc in range(8):
                    k.mm(bank[:], rep[:, v * 8 + kc, :], w_[:, kc, :], kc == 0, kc == 7, R=[rep, w_], W=[bank])
                r_ = res[v]
                k.tt("dve", r_[:], bank[:], b_[:], ALU.add, R=[bank, b_], W=[r_])
                k.dma(modrow[v:v + 1, n * 512:(n + 1) * 512], r_[0:1, :], R=[r_])
        k.pop()

    def rope_mul(zv, H, Gap, G, rt, out3, tmpA, tmpB, R, W, scale_bcast=None):
        gc = k.tile("rope_gc", [128, 64], F32); gs = k.tile("rope_gs", [128, 64], F32)
        k.tt("dve", gc[:], Gap, rt[:, 0:64], ALU.mult, R=[G, rt], W=[gc])
        g4 = Gap.rearrange("p (r h x) -> p r h x", r=2, h=2)
        s4 = rt[:, 64:128].rearrange("p (r h x) -> p r h x", r=2, h=2)
        gs4 = gs[:].rearrange("p (r h x) -> p r h x", r=2, h=2)
        k.tt("dve", gs4[:, :, 0, :], g4[:, :, 1, :], s4[:, :, 0, :], ALU.mult, R=[G, rt], W=[gs])
        k.tt("dve", gs4[:, :, 1, :], g4[:, :, 0, :], s4[:, :, 1, :], ALU.mult, R=[G, rt, gs], W=[gs])
        k.tt("dve", tmpA[:, 0:H, :], zv, gc[:].unsqueeze(1).to_broadcast([128, H, 64]), ALU.mult, R=list(R) + [gc], W=[tmpA])
        for r in range(2):
            for hf in range(2):
                zs = zv[:, :, r * 32 + (1 - hf) * 16: r * 32 + (1 - hf) * 16 + 16]
                o_ = tmpB[:, 0:H, r * 32 + hf * 16: r * 32 + hf * 16 + 16]
                g_ = gs[:, r * 32 + hf * 16: r * 32 + hf * 16 + 16].unsqueeze(1).to_broadcast([128, H, 16])
                k.tt("dve", o_, zs, g_, ALU.mult, R=list(R) + [gs], W=[tmpB])
        if scale_bcast is None:
            k.tt("pool", out3, tmpA[:, 0:H, :], tmpB[:, 0:H, :], ALU.add, R=[tmpA, tmpB], W=W)
        else:
            k.tt("pool", tmpA[:, 0:H, :], tmpA[:, 0:H, :], tmpB[:, 0:H, :], ALU.add, R=[tmpA, tmpB], W=[tmpA])
            k.tt("dve", out3, tmpA[:, 0:H, :], scale_bcast, ALU.mult, R=[tmpA] + list(R), W=W)

    def store_T(src, nblk, bank, dstT_ap_fn, R):
        tT = k.tile(f"stT{bank}_{nblk}", [128, nblk, 128], BF16)
        transposes([src[:, i * 128:(i + 1) * 128] for i in range(nblk)], bank, tT, R=[src])
        k.dma(dstT_ap_fn(), tT[:], R=[tT])

    def phase_A(l, xsrc):
        k.push()
        k.reorder = True
        wA = k.sb([128, 8, NPH_A], BF16, "wA"); wload(wA, Wt["w_in"][l][:, 0:NPH_A], nchunk=8)
        wqb = k.sb([128, 3, 768], BF16, "wqb"); wload(wqb, Wt["w_mla_qb"][l])
        wkvb = k.sb([128, 2, 1024], BF16, "wkvb"); wload(wkvb, Wt["w_mla_kvb"][l])
        A1, B1 = [], []
        n1 = k.sb([128, D], F32, "n1"); bload(n1[:], Wt["norm1"][l], W=[n1])
        for v in range(2):
            a = k.sb([128, D], F32, "A1"); b = k.sb([128, D], F32, "B1")
            bload(a[:], modrow[v, D:2 * D], W=[a]); bload(b[:], modrow[v, 0:D], W=[b])
            k.stt("dve", a[:], a[:], 1.0, n1[:], ALU.add, ALU.mult, R=[a, n1], W=[a])
            A1.append(a); B1.append(b)
        gq = k.sb([128, 64], F32, "gq"); bload(gq[:], Wt["diff_qn"][l], W=[gq])
        gk = k.sb([128, 64], F32, "gk"); bload(gk[:], Wt["diff_kn"][l], W=[gk])
        ones64 = k.sb([128, 64], F32, "ones"); k.memset("pool", ones64[:], 1.0, W=[ones64])
        eighth = k.sb([128, 64], F32, "eighth"); k.memset("pool", eighth[:], 0.125, W=[eighth])
        gqa = k.sb([128, 384], F32, "gqa"); bload(gqa[:], Wt["mla_qa_norm"][l], W=[gqa])
        gkva = k.sb([128, 256], F32, "gkva"); bload(gkva[:], Wt["mla_kva_norm"][l], W=[gkva])
        gmqn = k.sb([128, 192], F32, "gmqn"); bload(gmqn[:], Wt["mla_qn"][l], W=[gmqn])
        gmkn = k.sb([128, 192], F32, "gmkn"); bload(gmkn[:], Wt["mla_kn"][l], W=[gmkn])
        rt1 = k.sb([128, 128], F32, "rt1")
        k.memset("pool", rt1[:, 0:64], 1.0, W=[rt1]); k.memset("pool", rt1[:, 64:128], 0.0, W=[rt1])
        tA = k.sb([128, 8, 64], F32, "tA"); tB = k.sb([128, 8, 64], F32, "tB")

        def mla_kv(ckvb, krf, rt, col):
            ckvT = k.tile("ckvT", [128, 2, 128], BF16)
            transposes([ckvb[:, 0:128], ckvb[:, 128:256]], 5, ckvT, R=[ckvb])
            for c in range(2):
                for kc in range(2):
                    k.mm(PS[6 + c][:], ckvT[:, kc, :], wkvb[:, kc, c * 512:(c + 1) * 512], kc == 0, kc == 1, R=[ckvT, wkvb], W=[PS[6 + c]])
            sqk = k.tile("sqk", [128, 4, 128], F32); ssn = k.tile("ssn", [128, 4], F32)
            kv4 = [PS[6 + c][:].rearrange("p (h x d) -> p h x d", h=2, x=2) for c in range(2)]
            for c in range(2):
                k.act(sqk[:, 2 * c:2 * c + 2, :], kv4[c][:, :, 0, :], AF.Square, R=[PS[6 + c]], W=[sqk])
            k.red("dve", ssn[:], sqk[:], R=[sqk], W=[ssn])
            skr = k.tile("skr", [128, 1], F32); junk3 = k.tile("junk3", [128, 64], F32)
            k.act(junk3[:], krf[:], AF.Square, R=[krf], W=[junk3, skr], accum_out=skr[:])
            k.ts("dve", ssn[:], ssn[:], skr[:, 0:1], None, ALU.add, R=[ssn, skr], W=[ssn])
            rstd(ssn, 1.0 / 192)
            krg = k.tile("krg", [128, 1, 64], F32)
            rope_mul(krf[:].unsqueeze(1), 1, gmkn[:, 128:192], gmkn, rt, krg[:], tA, tB, R=[krf], W=[krg])
            mkn = k.tile("mkn", [128, 4, 128], BF16); mkr = k.tile("mkr", [128, 4, 64], BF16); mvb = k.tile("mvb", [128, 4, 128], BF16)
            for c in range(2):
                tn = k.tile("tn", [128, 2, 128], F32)
                k.tt("dve", tn[:], kv4[c][:, :, 0, :], gmkn[:, 0:128].unsqueeze(1).to_broadcast([128, 2, 128]), ALU.mult, R=[PS[6 + c], gmkn], W=[tn])
                k.tt("dve", mkn[:, 2 * c:2 * c + 2, :], tn[:], ssn[:, 2 * c:2 * c + 2].unsqueeze(2).to_broadcast([128, 2, 128]), ALU.mult, R=[tn, ssn], W=[mkn])
                k.cp("act", mvb[:, 2 * c:2 * c + 2, :], kv4[c][:, :, 1, :], R=[PS[6 + c]], W=[mvb])
            for h_ in range(4):
                k.ts("dve", mkr[:, h_, :], krg[:, 0, :], ssn[:, h_:h_ + 1], None, ALU.mult, R=[krg, ssn], W=[mkr])
            store_T(T(mkn.h[:].rearrange("p h d -> p (h d)"), mkn.b), 4, 5, lambda: mkTn[:, :, col:col + 128].rearrange("h p c -> p h c"), R=[mkn])
            store_T(T(mkr.h[:].rearrange("p h d -> p (h d)"), mkr.b), 2, 5, lambda: mkTr[:, :, col:col + 128].rearrange("h p c -> p h c"), R=[mkr])
            k.dma(mvs[col:col + 128, :], mvb[:].rearrange("p h d -> p (h d)"), R=[mvb])

        for ct in range(PAST // 128 if int(os.environ.get("A_CTX", "1")) else 0):
            col = NT + ct * 128
            rows = slice(ct * 128, (ct + 1) * 128)
            CP = int(os.environ.get("A_CTXP", "9"))
            ckb = k.tile(f"ckb{ct % 2}", [128, 512], BF16)
            k.dma(ckb[:], cdk[l, rows, :], W=[ckb], q="pool")
            if CP >= 1:
                store_T(ckb, 4, 5, lambda: dkT[:, :, col:col + 128].rearrange("h p c -> p h c"), R=[ckb])
            cvb = k.tile(f"cvb{ct % 2}", [128, 512], BF16)
            if CP >= 2:
                k.dma(cvb[:], cdv[l, rows, :], W=[cvb], q="pool")
                k.dma(dvs[col:col + 128, :], cvb[:], R=[cvb])
            cc = k.tile(f"ccb{ct % 2}", [128, 256], BF16)
            crf = k.tile(f"crf{ct % 2}", [128, 64], F32)
            if CP >= 3:
                k.dma(cc[:], cckv[l, rows, :], W=[cc], q="pool")
                k.dma(crf[:], ckr[l, rows, :], W=[crf])
            if CP >= 4:
                mla_kv(cc, crf, rt1, col)

        def loads(t):
            xt = k.tile(f"x{t % 2}", [128, D], F32); rt = k.tile(f"rt{t % 2}", [128, 128], F32)
            k.dma(xt[:], xsrc[t * 128:(t + 1) * 128, :], W=[xt])
            k.dma(rt[:], rope_tab[t * 128:(t + 1) * 128, :], W=[rt])

        loads(0)
        A_TILES = int(os.environ.get("A_TILES", "999")); A_PARTS = int(os.environ.get("A_PARTS", "99"))
        for t in range(min(NTT, A_TILES)):
            if t + 1 < NTT:
                loads(t + 1)
            s = seq_of(t); v = s["var"]; pi = s["pi"]
            lrows = slice((t - s["t0"]) * 128, (t - s["t0"] + 1) * 128)
            grow = slice(t * 128, (t + 1) * 128)
            xt = k.tile(f"x{t % 2}", [128, D], F32); rt = k.tile(f"rt{t % 2}", [128, 128], F32)
            junk = k.tile("junk", [128, D], BF16); ssx = k.tile("ssx", [128, 1], F32)
            k.act(junk[:], xt[:], AF.Square, R=[xt], W=[junk, ssx], accum_out=ssx[:])
            rstd(ssx, 1.0 / D)
            tmp = k.tile("htmp", [128, D], F32); hb = k.tile("hb", [128, D], BF16)
            k.stt("dve", tmp[:], xt[:], ssx[:, 0:1], A1[v][:], ALU.mult, ALU.mult, R=[xt, ssx, A1[v]], W=[tmp])
            k.tt("pool", hb[:], tmp[:], B1[v][:], ALU.add, R=[tmp, B1[v]], W=[hb])
            hT = k.tile(f"hT{t % 2}", [128, 8, 128], BF16)
            transposes([hb[:, i * 128:(i + 1) * 128] for i in range(8)], 4, hT, R=[hb])
            k.dma(hTs[t], hT[:], R=[hT])

            def zmm(c0, c1, bank):
                for kc in range(8):
                    k.mm(PS[bank][:, 0:c1 - c0], hT[:, kc, :], wA[:, kc, c0:c1], kc == 0, kc == 7, R=[hT, wA], W=[PS[bank]])

            def qknorm(bank, G, name, f32out):
                z3 = PS[bank][:].rearrange("p (h d) -> p h d", d=64)
                sq = k.tile("sq", [128, 512], F32); ss8 = k.tile("ss8", [128, 8], F32)
                k.act(sq[:], PS[bank][:], AF.Square, R=[PS[bank]], W=[sq])
                k.red("dve", ss8[:], sq[:].rearrange("p (h d) -> p h d", d=64), R=[sq], W=[ss8])
                rstd(ss8, 1.0 / 64)
                ob = k.tile(name + "b", [128, 512], BF16)
                sc = ss8[:].unsqueeze(2).to_broadcast([128, 8, 64])
                if f32out:
                    of = k.tile(name + "f", [128, 512], F32)
                    rope_mul(z3, 8, G[:], G, rt, of[:].rearrange("p (h d) -> p h d", d=64), tA, tB, R=[PS[bank], ss8], W=[of], scale_bcast=sc)
                    k.cp("act", ob[:], of[:], R=[of], W=[ob])
                    return ob, of
                rope_mul(z3, 8, G[:], G, rt, ob[:].rearrange("p (h d) -> p h d", d=64), tA, tB, R=[PS[bank], ss8], W=[ob], scale_bcast=sc)
                return ob, None

            if A_PARTS <= 0:
                continue
            zmm(0, 512, 0)
            qb, _ = qknorm(0, gq, "dq", False)
            store_T(qb, 4, 5, lambda: dqT[:, :, grow].rearrange("h p c -> p h c"), R=[qb])
            if A_PARTS <= 1:
                continue
            zmm(512, 1024, 1)
            API = int(os.environ.get("A_PI", "7"))
            kb_, kf_ = qknorm(1, gk, "dk", pi is not None and (API & 1))
            store_T(kb_, 4, 5, lambda: dkT[:, :, grow].rearrange("h p c -> p h c"), R=[kb_])
            if pi is not None and (API & 1):
                k.dma(ndk[pi, l, lrows, :], kf_[:], R=[kf_])
            if A_PARTS <= 2:
                continue
            zmm(1024, 1536, 2)
            dvb = k.tile("dvb", [128, 512], BF16)
            k.cp("act", dvb[:], PS[2][:], R=[PS[2]], W=[dvb])
            k.dma(dvs[grow, :], dvb[:], R=[dvb])
            if pi is not None and (API & 2):
                dvf = k.tile("dvf", [128, 512], F32)
                k.cp("act", dvf[:], PS[2][:], R=[PS[2]], W=[dvf])
                k.dma(ndv[pi, l, lrows, :], dvf[:], R=[dvf])
            if A_PARTS <= 3:
                continue
            zmm(1536, 2048, 3)
            z = PS[3]
            rqf = k.tile("rqf", [128, 4, 64], F32); rkf = k.tile("rkf", [128, 4, 64], F32)
            rope_mul(z[:, 0:256].rearrange("p (h d) -> p h d", d=64), 4, ones64[:], ones64, rt, rqf[:], tA, tB, R=[z], W=[rqf])
            rope_mul(z[:, 256:512].rearrange("p (h d) -> p h d", d=64), 4, eighth[:], eighth, rt, rkf[:], tA, tB, R=[z], W=[rkf])
            rqb = k.tile("rqb", [128, 256], BF16); rkb = k.tile("rkb", [128, 256], BF16)
            k.cp("act", rqb[:], rqf[:].rearrange("p h d -> p (h d)"), R=[rqf], W=[rqb])
            k.cp("act", rkb[:], rkf[:].rearrange("p h d -> p (h d)"), R=[rkf], W=[rkb])
            rq2 = k.tile("rq2", [128, 4, 2, 64], BF16); kk = k.tile("kk", [128, 4, 2, 64], BF16)
            for d_ in range(2):
                k.tt("dve", rq2[:, :, d_, :], rqf[:], qdc[:, :, d_:d_ + 1].to_broadcast([128, 4, 64]), ALU.mult, R=[rqf, qdc], W=[rq2])
                k.tt("dve", kk[:, :, d_, :], rkf[:], kdc[:, :, d_:d_ + 1].to_broadcast([128, 4, 64]), ALU.mult, R=[rkf, kdc], W=[kk])
            store_T(rqb, 2, 5, lambda: rqT[:, :, grow].rearrange("h p c -> p h c"), R=[rqb])
            store_T(rkb, 2, 5, lambda: rkT[:, :, grow].rearrange("h p c -> p h c"), R=[rkb])
            store_T(T(rq2.h[:].rearrange("p h a d -> p (h a d)"), rq2.b), 4, 5, lambda: QQ[:, :, grow].rearrange("h p c -> p h c"), R=[rq2])
            k.dma(kks[grow, :], kk[:].rearrange("p h a d -> p (h a d)"), R=[kk])
            if A_PARTS <= 4:
                continue
            zmm(2048, 2560, 0)
            rvb = k.tile("rvb", [128, 512], BF16)
            k.cp("act", rvb[:], PS[0][:], R=[PS[0]], W=[rvb])
            k.dma(rvs[grow, :], rvb[:], R=[rvb])
            if A_PARTS <= 5:
                continue
            zmm(2560, 3072, 1)
            rgf = k.tile("rgf", [128, 512], F32)
            k.cp("act", rgf[:], PS[1][:], R=[PS[1]], W=[rgf])
            k.dma(rgs[grow, :], rgf[:], R=[rgf])
            if A_PARTS <= 6:
                continue
            zmm(3072, 3456, 2)
            z = PS[2]
            ssq = k.tile("ssq", [128, 1], F32); junk2 = k.tile("junk2", [128, 384], BF16)
            k.act(junk2[:], z[:, 0:384], AF.Square, R=[z], W=[junk2, ssq], accum_out=ssq[:])
            rstd(ssq, 1.0 / 384)
            qab = k.tile("qab", [128, 384], BF16)
            k.stt("dve", qab[:], z[:, 0:384], ssq[:, 0:1], gqa[:], ALU.mult, ALU.mult, R=[z, ssq, gqa], W=[qab])
            qaT = k.tile("qaT", [128, 3, 128], BF16)
            transposes([qab[:, i * 128:(i + 1) * 128] for i in range(3)], 5, qaT, R=[qab])
            mqn = k.tile("mqn", [128, 4, 128], BF16); mqr = k.tile("mqr", [128, 4, 64], BF16)
            for c in range(2):
                bk = PS[6 + c]
                for kc in range(3):
                    k.mm(bk[:, 0:384], qaT[:, kc, :], wqb[:, kc, c * 384:(c + 1) * 384], kc == 0, kc == 2, R=[qaT, wqb], W=[bk])
                sq = k.tile("sq", [128, 512], F32); ss2 = k.tile("ss2", [128, 2], F32)
                k.act(sq[:, 0:384], bk[:, 0:384], AF.Square, R=[bk], W=[sq])
                k.red("dve", ss2[:], sq[:, 0:384].rearrange("p (h d) -> p h d", d=192), R=[sq], W=[ss2])
                rstd(ss2, 1.0 / 192)
                tq = k.tile("tq", [128, 2, 192], F32); tqr = k.tile("tqr", [128, 2, 64], F32)
                k.tt("dve", tq[:], bk[:, 0:384].rearrange("p (h d) -> p h d", d=192), gmqn[:].unsqueeze(1).to_broadcast([128, 2, 192]), ALU.mult, R=[bk, gmqn], W=[tq])
                rope_mul(tq[:, :, 128:192], 2, ones64[:], ones64, rt, tqr[:], tA, tB, R=[tq], W=[tqr])
                k.tt("dve", mqn[:, 2 * c:2 * c + 2, :], tq[:, :, 0:128], ss2[:].unsqueeze(2).to_broadcast([128, 2, 128]), ALU.mult, R=[tq, ss2], W=[mqn])
                k.tt("dve", mqr[:, 2 * c:2 * c + 2, :], tqr[:], ss2[:].unsqueeze(2).to_broadcast([128, 2, 64]), ALU.mult, R=[tqr, ss2], W=[mqr])
            store_T(T(mqn.h[:].rearrange("p h d -> p (h d)"), mqn.b), 4, 5, lambda: mqTn[:, :, grow].rearrange("h p c -> p h c"), R=[mqn])
            store_T(T(mqr.h[:].rearrange("p h d -> p (h d)"), mqr.b), 2, 5, lambda: mqTr[:, :, grow].rearrange("h p c -> p h c"), R=[mqr])
            if A_PARTS <= 7:
                continue
            zmm(3456, 3776, 3)
            z = PS[3]
            ssk = k.tile("ssk", [128, 1], F32)
            k.act(junk2[:, 0:256], z[:, 0:256], AF.Square, R=[z], W=[junk2, ssk], accum_out=ssk[:])
            rstd(ssk, 1.0 / 256)
            ckvf = k.tile("ckvf", [128, 256], F32); krf = k.tile("krf", [128, 64], F32); ckvb = k.tile("ckvb", [128, 256], BF16)
            k.stt("dve", ckvf[:], z[:, 0:256], ssk[:, 0:1], gkva[:], ALU.mult, ALU.mult, R=[z, ssk, gkva], W=[ckvf])
            k.cp("act", krf[:], z[:, 256:320], R=[z], W=[krf])
            k.cp("act", ckvb[:], ckvf[:], R=[ckvf], W=[ckvb])
            if pi is not None and (API & 4):
                k.dma(nckv[pi, l, lrows, :], ckvf[:], R=[ckvf])
                k.dma(nkr[pi, l, lrows, :], krf[:], R=[krf])
            mla_kv(ckvb, krf, rt, t * 128)
        k.pop()
        k.reorder = False

    def phase_ret(l):
        for s in seqs:
            k.push()
            nch = s["nt"]; t0 = s["t0"]; pi = s["pi"]
            gn = k.sb([128, 128], F32, "gn"); bload(gn[:], Wt["ret_gn"][l], W=[gn])
            rv = k.sb([128, nch, 512], BF16, "rv")
            k.dma(rv[:], rvs[t0 * 128:(t0 + nch) * 128, :].rearrange("(c p) f -> p c f", p=128), W=[rv])
            U = k.sb([128, nch, 512], F32, "U"); RR = k.sb([128, nch, 512], BF16, "RR"); S = k.sb([128, 512], F32, "S")
            Ub = [k.buf() for _ in range(nch)]; RRf = [k.buf() for _ in range(nch)]; RRb = [k.buf() for _ in range(nch)]
            Sf = k.buf(); Sb = k.buf()
            for c in range(nch):
                kkt = k.tile(f"kkt{c % 2}", [128, 512], BF16)
                k.dma(kkt[:], kks[(t0 + c) * 128:(t0 + c + 1) * 128, :], W=[kkt])
                bank = PS[c % 2]
                for h in range(4):
                    hs = slice(h * 128, (h + 1) * 128)
                    k.mm(bank[:, hs], kkt[:, hs], rv[:, c, hs], True, True, R=[kkt, rv], W=[bank])
                k.cp("act", U[:, c, :], bank[:], R=[bank], W=[Ub[c]])
            RS = int(os.environ.get("R_STOP", "9"))
            if RS < 1:
                k.pop(); k.barrier(); continue
            if s["ctx"]:
                for d in range(2):
                    k.dma(S[64 * d:64 * d + 64, :].rearrange("k (h v) -> k h v", h=4), sret[l, d].rearrange("h k v -> k h v"), W=[Sf if d == 0 else Sb])
            else:
                k.memset("dve", S[0:64, :], 0.0, W=[Sf]); k.memset("pool", S[64:128, :], 0.0, W=[Sb])
            dec2 = decC[:].rearrange("p h e -> p (h e)")
            for c in range(nch):
                k.cp("dve", RR[0:64, c, :], S[0:64, :], R=[Sf], W=[RRf[c]])
                k.tt("dve", S[0:64, :], S[0:64, :], dec2[0:64, :], ALU.mult, R=[Sf, decC], W=[Sf])
                k.tt("dve", S[0:64, :], S[0:64, :], U[0:64, c, :], ALU.add, R=[Sf, Ub[c]], W=[Sf])
            for c in reversed(range(nch)):
                k.cp("act", RR[64:128, c, :], S[64:128, :], R=[Sb], W=[RRb[c]])
                k.tt("pool", S[64:128, :], S[64:128, :], dec2[64:128, :], ALU.mult, R=[Sb, decC], W=[Sb])
                k.tt("pool", S[64:128, :], S[64:128, :], U[64:128, c, :], ALU.add, R=[Sb, Ub[c]], W=[Sb])
            if RS < 2:
                k.pop(); k.barrier(); continue
            if pi is not None:
                k.dma(nsr[pi, l, 0].rearrange("h k v -> k h v"), S[0:64, :].rearrange("k (h v) -> k h v", h=4), R=[Sf])
                k.dma(nsr[pi, l, 1].rearrange("h k v -> k h v"), S[64:128, :].rearrange("k (h v) -> k h v", h=4), R=[Sb])
            if RS < 3:
                k.pop(); k.barrier(); continue
            for c in range(nch):
                t = t0 + c; cols = slice(t * 128, (t + 1) * 128)
                kT = k.tile(f"rkT{c % 2}", [128, 2, 128], BF16); qT = k.tile(f"rqT{c % 2}", [128, 2, 128], BF16)
                qq = k.tile(f"rqq{c % 2}", [128, 4, 128], BF16); rg = k.tile(f"rrg{c % 2}", [128, 512], F32)
                k.dma(kT[:], rkT[:, :, cols].rearrange("h p c -> p h c"), W=[kT])
                k.dma(qT[:], rqT[:, :, cols].rearrange("h p c -> p h c"), W=[qT])
                k.dma(qq[:], QQ[:, :, cols].rearrange("h p c -> p h c"), W=[qq])
                k.dma(rg[:], rgs[cols, :], W=[rg])
                bsr = [PS[2 - 2 * (c % 2)], PS[3 - 2 * (c % 2)]]; bo = PS[4 + c % 2]
                for h in range(4):
                    pp = slice(64 * (h % 2), 64 * (h % 2) + 64)
                    k.mm(bsr[h % 2][:, (h // 2) * 128:(h // 2 + 1) * 128], kT[pp, h // 2, :], qT[pp, h // 2, :], True, True, R=[kT, qT], W=[bsr[h % 2]])
                P = k.tile(f"rP{c % 2}", [128, 512], BF16)
                P4 = P[:].rearrange("p (a r i) -> p a r i", a=2, r=2)
                MT4 = MT[:].rearrange("p (a r) i -> p a r i", r=2)
                for r_ in range(2):
                    k.tt("dve", P4[:, :, r_, :], bsr[r_][:, 0:256].rearrange("p (a i) -> p a i", a=2), MT4[:, :, r_, :], ALU.mult, R=[bsr[r_], MT], W=[P])
                for h in range(4):
                    hs = slice(h * 128, (h + 1) * 128)
                    k.mm(bo[:, hs], P[:, hs], rv[:, c, hs], True, False, R=[P, rv], W=[bo])
                    k.mm(bo[:, hs], qq[:, h, :], RR[:, c, hs], False, True, R=[qq, RRf[c], RRb[c]], W=[bo])
                sq = k.tile("rsq", [128, 512], F32); ss4 = k.tile("rss4", [128, 4], F32)
                k.act(sq[:], bo[:], AF.Square, R=[bo], W=[sq])
                k.red("dve", ss4[:], sq[:].rearrange("p (h e) -> p h e", h=4), R=[sq], W=[ss4])
                rstd(ss4, 1.0 / 128)
                to = k.tile("rto", [128, 4, 128], F32)
                k.tt("dve", to[:], bo[:].rearrange("p (h e) -> p h e", h=4), ss4[:].unsqueeze(2).to_broadcast([128, 4, 128]), ALU.mult, R=[bo, ss4], W=[to])
                k.tt("dve", to[:], to[:], gn[:].unsqueeze(1).to_broadcast([128, 4, 128]), ALU.mult, R=[to, gn], W=[to])
                sg = k.tile("rsg", [128, 512], F32)
                k.act(sg[:], rg[:], AF.Silu, R=[rg], W=[sg])
                orb = k.tile("orb", [128, 512], BF16)
                k.tt("pool", orb[:], to[:].rearrange("p h e -> p (h e)"), sg[:], ALU.mult, R=[to, sg], W=[orb])
                store_T(orb, 4, 6, lambda: oT[1][:, :, cols].rearrange("h p c -> p h c"), R=[orb])
            k.pop()
            k.barrier()

    def attn_core(pairs_fn, V, nkt, QC, scale, R):
        nqt = QC // 128

        def st(kt):
            bank = PS[kt % 2]
            prs = pairs_fn(kt)
            for i, (a, b) in enumerate(prs):
                k.mm(bank[:, 0:QC], a, b, i == 0, i == len(prs) - 1, R=R, W=[bank])

        def pv(kt):
            bank = PS[kt % 2]
            PT = k.tile(f"PT{kt % 3}", [128, 512], BF16)
            k.act(PT[:, 0:QC], bank[:, 0:QC], AF.Exp, R=[bank], W=[PT], scale=scale)
            for qt in range(nqt):
                k.mm(PS[2 + qt][:, 0:129], PT[:, qt * 128:(qt + 1) * 128], V[:, kt, :], kt == 0, kt == nkt - 1, R=[PT, V], W=[PS[2 + qt]])
        st(0)
        for kt in range(nkt):
            if kt + 1 < nkt:
                st(kt + 1)
            pv(kt)

    def phase_attn(l, lam_init):
        k.push()
        gsub = k.sb([128, 128], F32, "gsub"); bload(gsub[:], Wt["diff_subln"][l], W=[gsub])
        k.ts("dve", gsub[:], gsub[:], 1.0 - lam_init, None, ALU.mult, R=[gsub], W=[gsub])
        for s in seqs:
            k.push()
            n = s["nt"] * 128; q0 = s["t0"] * 128
            nctx = PAST if s["ctx"] else 0
            nkt = (n + nctx) // 128
            QC = min(512, n); nqt = QC // 128
            V = k.sb([128, nkt, 129], BF16, "V"); kTa = k.sb([128, nkt * 128], BF16, "kTa"); kTb = k.sb([128, nkt * 128], BF16, "kTb")
            k.memset("dve", V[:], 1.0, W=[V])

            def load_keys(dst, src2d):
                k.dma(dst[:, 0:n], src2d[:, q0:q0 + n], W=[dst])
                if nctx:
                    k.dma(dst[:, n:n + nctx], src2d[:, NT:NT + nctx], W=[dst])

            def load_V(src, h):
                k.dma(V[:, 0:n // 128, 0:128], src[q0:q0 + n, h * 128:(h + 1) * 128].rearrange("(c p) e -> p c e", p=128), W=[V])
                if nctx:
                    k.dma(V[:, n // 128:nkt, 0:128], src[NT:NT + nctx, h * 128:(h + 1) * 128].rearrange("(c p) e -> p c e", p=128), W=[V])

            for h in range(4):
                load_keys(kTa, dkT[h]); load_V(dvs, h)
                for qc in range(n // QC):
                    qs = slice(q0 + qc * QC, q0 + (qc + 1) * QC)
                    qp = []
                    for m in range(2):
                        nm = f"qTd{m}_{qc % 2}"
                        fresh = nm not in k.cache
                        t_ = k.tile(nm, [128, QC], BF16)
                        if fresh:
                            k.memset("dve", t_[64 * (1 - m):64 * (1 - m) + 64, :], 0.0, W=[t_])
                        k.dma(t_[64 * m:64 * m + 64, :], dqT[h][64 * m:64 * m + 64, qs], W=[t_])
                        qp.append(t_)
                    Oa = k.tile("Oa", [128, nqt, 128], F32); ob = k.tile("odb", [128, nqt * 128], BF16)
                    for m in range(2):
                        pp = slice(64 * m, 64 * m + 64)
                        attn_core(lambda kt: [(kTa[:, kt * 128:(kt + 1) * 128], qp[m][:, :])], V, nkt, QC, 0.125, R=[kTa, qp[m]])
                        for qt in range(nqt):
                            O = PS[2 + qt]
                            rinv = k.tile("rinv", [128, 1], F32)
                            k.op("dve", lambda e, O=O, rinv=rinv: e.reciprocal(out=rinv[:], in_=O[:, 128:129]), R=[O], W=[rinv])
                            if m == 0:
                                k.ts("dve", Oa[:, qt, :], O[:, 0:128], rinv[:, 0:1], None, ALU.mult, R=[O, rinv], W=[Oa])
                            else:
                                k.tt("dve", rinv[:], rinv[:], neglam[:], ALU.mult, R=[rinv, neglam], W=[rinv])
                                od = k.tile("od", [128, 128], F32)
                                k.stt("dve", od[:], O[:, 0:128], rinv[:, 0:1], Oa[:, qt, :], ALU.mult, ALU.add, R=[O, rinv, Oa], W=[od])
                                ssd = k.tile("ssd", [128, 1], F32); junk = k.tile("ajunk", [128, 128], BF16)
                                k.act(junk[:], od[:], AF.Square, R=[od], W=[junk, ssd], accum_out=ssd[:])
                                rstd(ssd, 1.0 / 128)
                                k.stt("dve", ob[:, qt * 128:(qt + 1) * 128], od[:], ssd[:, 0:1], gsub[:], ALU.mult, ALU.mult, R=[od, ssd, gsub], W=[ob])
                    store_T(ob, nqt, 6, lambda: oT[0, h][:, qs].rearrange("p (i c) -> p i c", c=128), R=[ob])
            for h in range(4):
                load_keys(kTa, mkTn[h]); load_keys(kTb, mkTr[h // 2]); load_V(mvs, h)
                pp = slice(64 * (h % 2), 64 * (h % 2) + 64)
                for qc in range(n // QC):
                    qs = slice(q0 + qc * QC, q0 + (qc + 1) * QC)
                    qTn = k.tile(f"qTn{qc % 2}", [128, QC], BF16)
                    nm = f"qTr{h % 2}_{qc % 2}"
                    fresh = nm not in k.cache
                    qTr = k.tile(nm, [128, QC], BF16)
                    if fresh:
                        k.memset("dve", qTr[64 * (1 - h % 2):64 * (1 - h % 2) + 64, :], 0.0, W=[qTr])
                    k.dma(qTn[:], mqTn[h][:, qs], W=[qTn]); k.dma(qTr[pp, :], mqTr[h // 2][pp, qs], W=[qTr])
                    attn_core(lambda kt: [(kTa[:, kt * 128:(kt + 1) * 128], qTn[:, :]), (kTb[:, kt * 128:(kt + 1) * 128], qTr[:, :])],
                              V, nkt, QC, 192 ** -0.5, R=[kTa, kTb, qTn, qTr])
                    omb = k.tile("omb", [128, nqt * 128], BF16)
                    for qt in range(nqt):
                        O = PS[2 + qt]
                        rinv = k.tile("rinv", [128, 1], F32)
                        k.op("dve", lambda e, O=O, rinv=rinv: e.reciprocal(out=rinv[:], in_=O[:, 128:129]), R=[O], W=[rinv])
                        k.ts("dve", omb[:, qt * 128:(qt + 1) * 128], O[:, 0:128], rinv[:, 0:1], None, ALU.mult, R=[O, rinv], W=[omb])
                    store_T(omb, nqt, 6, lambda: oT[2, h][:, qs].rearrange("p (i c) -> p i c", c=128), R=[omb])
            k.pop()
            k.barrier()
        k.pop()

    def mod_tiles(l, which):
        out = {}
        for name, idx in which:
            out[name] = []
            for v in range(2):
                a = k.sb([128, D], F32, name)
                bload(a[:], modrow[v, idx * D:(idx + 1) * D], W=[a])
                out[name].append(a)
        return out

    def phase_merge(l, xsrc):
        k.push()
        k.reorder = True
        wg = k.sb([128, 8, 3072], BF16, "wg"); wload(wg, Wt["w_in"][l][:, NPH_A:D_IN], nchunk=6)
        wb = k.sb([128, 12, 1024], BF16, "wb")
        for i in range(3):
            wload(T(wb.h[:, 4 * i:4 * i + 4, :], wb.b), Wt["w_branch"][l, i])
        wo = k.sb([128, 8, 1024], BF16, "wo"); wload(wo, Wt["w_out"][l], nchunk=2)
        m = mod_tiles(l, [("G1", 2), ("B2", 3), ("A2", 4)])
        n2 = k.sb([128, D], F32, "n2"); bload(n2[:], Wt["norm2"][l], W=[n2])
        for v in range(2):
            a = m["A2"][v]
            k.stt("dve", a[:], a[:], 1.0, n2[:], ALU.add, ALU.mult, R=[a, n2], W=[a])

        def loads(t):
            cols = slice(t * 128, (t + 1) * 128)
            xt = k.tile(f"x{t % 2}", [128, D], F32); hT = k.tile(f"hT{t % 2}", [128, 8, 128], BF16)
            oTt = k.tile(f"oTt{t % 2}", [128, 12, 128], BF16)
            k.dma(xt[:], xsrc[cols, :], W=[xt]); k.dma(hT[:], hTs[t], W=[hT])
            for i in range(3):
                k.dma(oTt[:, 4 * i:4 * i + 4, :], oT[i][:, :, cols].rearrange("h p c -> p h c"), W=[oTt])
        loads(0)
        for t in range(NTT):
            if t + 1 < NTT:
                loads(t + 1)
            v = seq_of(t)["var"]; grow = slice(t * 128, (t + 1) * 128)
            xt = k.tile(f"x{t % 2}", [128, D], F32); hT = k.tile(f"hT{t % 2}", [128, 8, 128], BF16)
            oTt = k.tile(f"oTt{t % 2}", [128, 12, 128], BF16)
            mer = k.tile("mer", [128, D], F32)
            for nn in range(2):
                cs = slice(nn * 512, (nn + 1) * 512)
                for i in range(3):
                    j = nn * 3 + i
                    bg = PS[j % 2]; bo = PS[2 + j % 2]
                    for kc in range(8):
                        k.mm(bg[:], hT[:, kc, :], wg[:, kc, i * 1024 + nn * 512:i * 1024 + nn * 512 + 512], kc == 0, kc == 7, R=[hT, wg], W=[bg])
                    for kc in range(4):
                        k.mm(bo[:], oTt[:, i * 4 + kc, :], wb[:, i * 4 + kc, cs], kc == 0, kc == 3, R=[oTt, wb], W=[bo])
                    sig = k.tile(f"sig{j % 2}", [128, 512], F32)
                    k.act(sig[:], bg[:], AF.Sigmoid, R=[bg], W=[sig])
                    if i == 0:
                        k.tt("dve", mer[:, cs], sig[:], bo[:], ALU.mult, R=[sig, bo], W=[mer])
                    else:
                        tmpm = k.tile(f"tmpm{j % 2}", [128, 512], F32)
                        k.tt("dve", tmpm[:], sig[:], bo[:], ALU.mult, R=[sig, bo], W=[tmpm])
                        k.tt("pool", mer[:, cs], mer[:, cs], tmpm[:], ALU.add, R=[mer, tmpm], W=[mer])
            merb = k.tile("merb", [128, D], BF16)
            k.cp("act", merb[:], mer[:], R=[mer], W=[merb])
            mT = k.tile("mT", [128, 8, 128], BF16)
            transposes([merb[:, i * 128:(i + 1) * 128] for i in range(8)], 6, mT, R=[merb])
            x1 = k.tile("xout", [128, D], F32)
            for nn in range(2):
                cs = slice(nn * 512, (nn + 1) * 512)
                bo = PS[4 + nn]
                for kc in range(8):
                    k.mm(bo[:], mT[:, kc, :], wo[:, kc, cs], kc == 0, kc == 7, R=[mT, wo], W=[bo])
                tmpo = k.tile(f"tmpo{nn}", [128, 512], F32)
                k.tt("dve", tmpo[:], bo[:], m["G1"][v][:, cs], ALU.mult, R=[bo, m["G1"][v]], W=[tmpo])
                k.tt("pool", x1[:, cs], xt[:, cs], tmpo[:], ALU.add, R=[xt, tmpo], W=[x1])
            k.dma(xa[grow, :], x1[:], R=[x1])
            junk = k.tile("junk", [128, D], BF16); ssx = k.tile("ssx", [128, 1], F32)
            k.act(junk[:], x1[:], AF.Square, R=[x1], W=[junk, ssx], accum_out=ssx[:])
            rstd(ssx, 1.0 / D)
            tmp = k.tile("htmp", [128, D], F32); hb = k.tile("hb", [128, D], BF16)
            k.stt("dve", tmp[:], x1[:], ssx[:, 0:1], m["A2"][v][:], ALU.mult, ALU.mult, R=[x1, ssx, m["A2"][v]], W=[tmp])
            k.tt("pool", hb[:], tmp[:], m["B2"][v][:], ALU.add, R=[tmp, m["B2"][v]], W=[hb])
            h2T = k.tile("h2T", [128, 8, 128], BF16)
            transposes([hb[:, i * 128:(i + 1) * 128] for i in range(8)], 7, h2T, R=[hb])
            k.dma(h2Ts[t], h2T[:], R=[h2T])
        k.pop()
        k.reorder = False

    def phase_mlp(l, xdst):
        k.push()
        wu = k.sb([128, 8, 4096], BF16, "wu"); wload(wu, Wt["w_up"][l], nchunk=8)
        wd = k.sb([128, 32, 1024], BF16, "wd"); wload(wd, Wt["w_down"][l], nchunk=4)
        m = mod_tiles(l, [("G2", 5)])
        uT = k.sb([128, 32, 512], BF16, "uT"); uTb = [k.buf() for _ in range(32)]
        groups = []
        for s in seqs:
            ts_ = list(range(s["t0"], s["t0"] + s["nt"]))
            for i in range(0, len(ts_), 4):
                groups.append((s["var"], ts_[i:i + 4]))
        for gi, (v, tiles) in enumerate(groups):
            ncol = len(tiles) * 128
            h2 = k.tile("h2g", [128, 8, 512], BF16)
            for j, t in enumerate(tiles):
                k.dma(h2[:, :, j * 128:(j + 1) * 128], h2Ts[t], W=[h2])
            for f in range(32):
                bank = PS[f % 2]
                for kc in range(8):
                    k.mm(bank[:, 0:ncol], wu[:, kc, f * 128:(f + 1) * 128], h2[:, kc, 0:ncol], kc == 0, kc == 7, R=[wu, h2], W=[bank])
                r = k.tile(f"relu{f % 2}", [128, 512], F32)
                k.act(r[:, 0:ncol], bank[:, 0:ncol], AF.Relu, R=[bank], W=[r])
                k.tt("dve" if f % 2 else "pool", uT[:, f, 0:ncol], r[:, 0:ncol], r[:, 0:ncol], ALU.mult, R=[r], W=[uTb[f]])
            for j, t in enumerate(tiles):
                rows = slice(t * 128, (t + 1) * 128)
                x1 = k.tile(f"mx{j % 2}", [128, D], F32); xo = k.tile("mxo", [128, D], F32)
                k.dma(x1[:], xa[rows, :], W=[x1])
                for nn in range(2):
                    cs = slice(nn * 512, (nn + 1) * 512)
                    bank = PS[2 + (j * 2 + nn) % 4]
                    for f in range(32):
                        k.mm(bank[:], uT[:, f, j * 128:(j + 1) * 128], wd[:, f, cs], f == 0, f == 31, R=[uTb[f], wd], W=[bank])
                    tmp = k.tile(f"mtmp{nn}", [128, 512], F32)
                    k.tt("dve", tmp[:], bank[:], m["G2"][v][:, cs], ALU.mult, R=[bank, m["G2"][v]], W=[tmp])
                    k.tt("pool", xo[:, cs], x1[:, cs], tmp[:], ALU.add, R=[x1, tmp], W=[xo])
                k.dma(xdst[rows, :], xo[:], R=[xo])
        k.pop()

    nph = 0
    for l in range(L):
        lam_init = 0.8 - 0.6 * math.exp(-0.3 * l)
        phases = [lambda: layer_consts(l), lambda: phase_mod(l), lambda: phase_A(l, x_all if l == 0 else xb),
                  lambda: phase_ret(l), lambda: phase_attn(l, lam_init), lambda: phase_merge(l, x_all if l == 0 else xb),
                  lambda: phase_mlp(l, xb if l < L - 1 else y_all)]
        for ph in phases:
            if nph < stop:
                ph(); k.barrier()
            nph += 1
    k.emit()
    return k


def _rope_table(NS, NP):
    n_rows = NS // GRID_W
    row = np.repeat(np.arange(n_rows, dtype=np.float32), GRID_W)
    col = np.tile(np.arange(GRID_W, dtype=np.float32), n_rows)
    inv = (10000.0 ** (-np.arange(0, 32, 2, dtype=np.float32) / 32)).astype(np.float32)
    ar = (row[:, None] * inv[None, :]).astype(np.float32); ac = (col[:, None] * inv[None, :]).astype(np.float32)
    cr, sr, cc, sc = np.cos(ar), np.sin(ar), np.cos(ac), np.sin(ac)
    tab = np.concatenate([cr, cr, cc, cc, -sr, sr, -sc, sc], axis=1).astype(np.float32)
    ptab = np.concatenate([np.ones((2 * NP, 64), np.float32), np.zeros((2 * NP, 64), np.float32)], axis=1)
    return np.ascontiguousarray(np.concatenate([tab, ptab], axis=0))


def _ret_consts():
    C = 128
    j = np.arange(C, dtype=np.float32)[:, None]; i = np.arange(C, dtype=np.float32)[None, :]
    relf = np.maximum(i - j, 0.0); maskf = (i >= j).astype(np.float32)
    relb = np.maximum(j - i, 0.0); maskb = (j > i).astype(np.float32)
    retc = np.stack([relf, maskf, relb, maskb]).astype(np.float32)
    p = np.arange(C, dtype=np.float32)
    retcol = np.stack([p + 1.0, C - p, C - 1.0 - p, p], axis=1).astype(np.float32)
    return np.ascontiguousarray(retc), np.ascontiguousarray(retcol)


_CACHE = {}


def _run(inputs, NS, NP, PAST, dbg=(), stop=99):
    key = (NS, NP, PAST, tuple(dbg), stop)
    if key not in _CACHE:
        _CACHE[key] = build(NS, NP, PAST, dbg, stop)
    kb = _CACHE[key]
    f = lambda a: np.ascontiguousarray(np.asarray(a, dtype=np.float32))
    rope_tab = _rope_table(NS, NP); retc, retcol = _ret_consts()
    xp = f(inputs["x_prompt"]); xs = f(inputs["x_sample"])
    L = DEPTH
    in_maps = []
    for c in range(8):
        m = {
            "x_all": np.ascontiguousarray(np.concatenate([xs[c], xp[2 * c], xp[2 * c + 1]], axis=0)),
            "cvec": np.ascontiguousarray(np.stack([f(inputs["c"])[c], f(inputs["c_ctx"])])),
            "cdk": f(inputs["cache_diff_k"])[c].reshape(L, PAST, 512),
            "cdv": f(inputs["cache_diff_v"])[c].reshape(L, PAST, 512),
            "cckv": f(inputs["cache_mla_ckv"])[c], "ckr": f(inputs["cache_mla_krope"])[c],
            "sret": f(inputs["state_ret"])[c],
            "rope_tab": rope_tab, "retc": retc, "retcol": retcol,
        }
        for n, _ in W_SPECS:
            m[n] = f(inputs[n])
        in_maps.append({a: np.ascontiguousarray(b) for a, b in m.items()})
    res = run_bass_kernel_spmd(kb.nc, in_maps, core_ids=list(range(8)))
    return res.results


def kernel(**inputs):
    NS = inputs["x_sample"].shape[1]; NP = inputs["x_prompt"].shape[1]; PAST = inputs["cache_diff_k"].shape[2]
    B = inputs["x_prompt"].shape[0]
    r = _run(inputs, NS, NP, PAST)
    L = DEPTH
    y_prompt = np.zeros((B, NP, D), np.float32); y_sample = np.zeros((8, NS, D), np.float32)
    ndk = np.zeros((B, L, NP, 8, 64), np.float32); ndv = np.zeros((B, L, NP, 4, 128), np.float32)
    nckv = np.zeros((B, L, NP, 256), np.float32); nkr = np.zeros((B, L, NP, 64), np.float32)
    nsr = np.zeros((B, L, 2, 4, 64, 128), np.float32)
    for c in range(8):
        ya = r[c]["y_all"]
        y_sample[c] = ya[:NS]
        for p in range(2):
            b = 2 * c + p
            y_prompt[b] = ya[NS + p * NP: NS + (p + 1) * NP]
            ndk[b] = r[c]["ndk"][p].reshape(L, NP, 8, 64); ndv[b] = r[c]["ndv"][p].reshape(L, NP, 4, 128)
            nckv[b] = r[c]["nckv"][p]; nkr[b] = r[c]["nkr"][p]; nsr[b] = r[c]["nsr"][p]
    return (y_prompt, y_sample, ndk, ndv, nckv, nkr, nsr)
```

```python
import math
import os
import numpy as np
import concourse.bass as bass
import concourse.mybir as mybir
from concourse.bass_utils import run_bass_kernel_spmd

F32 = mybir.dt.float32
BF16 = mybir.dt.bfloat16
AF = mybir.ActivationFunctionType
ALU = mybir.AluOpType
AX = mybir.AxisListType

D = 1024
DEPTH = 2
EPS = 1e-6
D_IN = 6848
NPH_A = 3776
GRID_W = 64


class Buf:
    __slots__ = ("name", "w", "r")

    def __init__(self, name):
        self.name = name
        self.w = None
        self.r = []


class Op:
    __slots__ = ("eng", "fn", "deps", "dma", "sig", "need", "selfsig", "odeps", "reorder")

    def __init__(self, eng, fn, dma):
        self.eng = eng
        self.fn = fn
        self.deps = []
        self.dma = dma
        self.sig = None
        self.need = False
        self.selfsig = False
        self.odeps = []
        self.reorder = False


class T:
    def __init__(self, h, buf):
        self.h = h
        self.b = buf
        self.tok = None

    def __getitem__(self, k):
        return self.h[k]


ENGS = ("pe", "act", "dve", "pool", "sp")
NDMASEM = 16


class KB:
    def __init__(self):
        self.nc = bass.Bass("TRN2", target_bir_lowering=False)
        nc = self.nc
        self.e = {"pe": nc.tensor, "act": nc.scalar, "dve": nc.vector, "pool": nc.gpsimd, "sp": nc.sync}
        self.ops = {k: [] for k in ENGS}
        self.allops = []
        self.csem = {k: nc.alloc_semaphore("c_" + k) for k in ENGS}
        self.dsem = {q: [nc.alloc_semaphore(f"d_{q}{i}") for i in range(NDMASEM)] for q in ("sp", "pool", "act")}
        self.dcnt = {q: 0 for q in self.dsem}
        self.dlast = {}
        self.bar = None
        self.bar_seen = {k: None for k in ENGS}
        self.sb_off = 16512
        self.sb_mark = []
        self.nbuf = 0
        self.names = 0
        self.cache = {}
        self.reorder = False

    def buf(self, name="b"):
        self.nbuf += 1
        return Buf(f"{name}{self.nbuf}")

    def sb(self, shape, dt, name="t"):
        self.names += 1
        nbytes = int(np.prod(shape[1:])) * (4 if dt == F32 else 2)
        nbytes = (nbytes + 63) // 64 * 64
        h = self.nc.alloc_sbuf_tensor_at(f"{name}_{self.names}", list(shape), dt, offset=self.sb_off)
        self.sb_off += nbytes
        assert self.sb_off <= 229344, f"SBUF overflow {self.sb_off}"
        return T(h, self.buf(name))

    def push(self):
        self.sb_mark.append((self.sb_off, dict(self.cache)))

    def pop(self):
        self.sb_off, self.cache = self.sb_mark.pop()

    def tile(self, name, shape, dt):
        t = self.cache.get(name)
        if t is None:
            t = self.sb(shape, dt, name)
            self.cache[name] = t
        return t

    def dram(self, name, shape, dt, kind="Internal"):
        return self.nc.dram_tensor(name, list(shape), dt, kind=kind).ap()

    def op(self, eng, fn, R=(), W=(), dma=False):
        o = Op(eng, fn, dma)
        o.reorder = self.reorder
        deps = {}
        toks = [t.tok for t in R if isinstance(t, T) and t.tok is not None]
        if toks:
            W = list(W) + toks
        for t in R:
            b = t.b if isinstance(t, T) else t
            if b.w is not None:
                deps[id(b.w)] = (b.w, "raw")
        for t in W:
            b = t.b if isinstance(t, T) else t
            if b.w is not None:
                deps[id(b.w)] = (b.w, "waw")
            for r in b.r:
                if id(r) not in deps:
                    deps[id(r)] = (r, "war")
        for d, kind in deps.values():
            if d is o:
                continue
            if not d.dma and not dma and d.eng == eng:
                if eng == "pe":
                    o.odeps.append(d)
                    continue
            o.deps.append(d)
        if self.bar is not None and self.bar_seen[eng] is not self.bar:
            o.deps.append(self.bar)
            self.bar_seen[eng] = self.bar
        if dma:
            q = eng
            i = self.dcnt[q] % (2 if q == "pool" else NDMASEM)
            self.dcnt[q] += 1
            s = self.dsem[q][i]
            prev = self.dlast.get((q, i))
            if prev is not None:
                o.deps.append(prev[0])
                cnt = prev[1] + 1
            else:
                cnt = 1
            o.sig = (s, 16 * cnt)
            self.dlast[(q, i)] = (o, cnt)
            o.need = True
        for d in o.deps:
            d.need = True
        for t in R:
            b = t.b if isinstance(t, T) else t
            b.r.append(o)
        for t in W:
            b = t.b if isinstance(t, T) else t
            b.w = o
            b.r = []
        self.ops[eng].append(o)
        self.allops.append(o)
        return o

    def barrier(self):
        deps = []
        for k in ENGS:
            for o in reversed(self.ops[k]):
                if not o.dma:
                    deps.append(o)
                    break
        for (q, i), (o, c) in self.dlast.items():
            deps.append(o)
        b = Op("sp", lambda e: e.sem_inc(self.csem["sp"], 1), False)
        b.deps = [d for d in deps]
        for d in deps:
            d.need = True
        b.need = True
        b.selfsig = True
        self.ops["sp"].append(b)
        self.allops.append(b)
        self.bar = b

    def dma(self, out, in_, R=(), W=(), q="sp", **kw):
        return self.op(q, lambda e: e.dma_start(out=out, in_=in_, **kw), R, W, dma=True)

    def mm(self, out, lhsT, rhs, start, stop, R=(), W=()):
        return self.op("pe", lambda e: e.matmul(out, lhsT=lhsT, rhs=rhs, start=start, stop=stop), R, W)

    def tr(self, out, in_, ident, R=(), W=()):
        return self.op("pe", lambda e: e.transpose(out=out, in_=in_, identity=ident), R, W)

    def act(self, out, in_, func, R=(), W=(), **kw):
        return self.op("act", lambda e: e.activation(out=out, in_=in_, func=func, **kw), R, W)

    def tt(self, eng, out, in0, in1, op, R=(), W=()):
        return self.op(eng, lambda e: e.tensor_tensor(out=out, in0=in0, in1=in1, op=op), R, W)

    def ts(self, eng, out, in0, s1, s2, op0, op1=None, R=(), W=()):
        if op1 is None:
            return self.op(eng, lambda e: e.tensor_scalar(out=out, in0=in0, scalar1=s1, scalar2=None, op0=op0), R, W)
        return self.op(eng, lambda e: e.tensor_scalar(out=out, in0=in0, scalar1=s1, scalar2=s2, op0=op0, op1=op1), R, W)

    def stt(self, eng, out, in0, scalar, in1, op0, op1, R=(), W=()):
        return self.op(eng, lambda e: e.scalar_tensor_tensor(out=out, in0=in0, scalar=scalar, in1=in1, op0=op0, op1=op1), R, W)

    def cp(self, eng, out, in_, R=(), W=()):
        if eng == "act":
            return self.op("act", lambda e: e.activation(out=out, in_=in_, func=AF.Identity), R, W)
        return self.op(eng, lambda e: e.tensor_copy(out=out, in_=in_), R, W)

    def red(self, eng, out, in_, R=(), W=()):
        return self.op(eng, lambda e: e.tensor_reduce(out=out, in_=in_, axis=AX.X, op=ALU.add), R, W)

    def memset(self, eng, ap, val, W=()):
        return self.op(eng, lambda e: e.memset(ap, val), (), W)

    def sched_seg(self, ops):
        idx = {id(o): i for i, o in enumerate(ops)}
        fin = {}
        per = {kx: [o for o in ops if o.eng == kx] for kx in ENGS}
        free = {kx: 0.0 for kx in ENGS}
        DUR = {"pe": 0.3, "act": 0.6, "dve": 0.5, "pool": 0.6, "sp": 0.05}
        out = []
        Wn = 96
        nleft = len(ops)
        while nleft:
            best = None
            for kx in ENGS:
                lst = per[kx]
                fk = free[kx]
                for j in range(min(Wn, len(lst))):
                    o = lst[j]
                    t = fk
                    ok = True
                    for d in o.deps:
                        f = fin.get(id(d))
                        if f is None:
                            if id(d) in idx:
                                ok = False
                                break
                            continue
                        f += 0.2
                        if f > t:
                            t = f
                    if ok:
                        for d in o.odeps:
                            f = fin.get(id(d))
                            if f is None:
                                if id(d) in idx:
                                    ok = False
                                    break
                                continue
                            if f > t:
                                t = f
                    if not ok:
                        continue
                    key = (t, idx[id(o)])
                    if best is None or key < best[0]:
                        best = (key, kx, j, o, t)
                    if t <= fk:
                        break
            assert best is not None, "scheduler stuck"
            _, kx, j, o, t = best
            per[kx].pop(j)
            if o.dma:
                occ = 0.8 if kx == "pool" else 0.05
                fin[id(o)] = t + occ + 2.5
            else:
                occ = DUR[kx]
                fin[id(o)] = t + occ
            free[kx] = t + occ
            out.append(o)
            nleft -= 1
        return out

    def finalize(self):
        segs = []
        cur = []
        for o in self.allops:
            if o.selfsig:
                segs.append((cur, o)); cur = []
            else:
                cur.append(o)
        if cur:
            segs.append((cur, None))
        bars = set(id(b) for _, b in segs if b is not None)
        final = {kx: [] for kx in ENGS}
        lastdma = {}
        prevbar = None
        order_all = []
        for ops, bar in segs:
            for o in ops:
                o.deps = [d for d in o.deps if id(d) not in bars]
            order = self.sched_seg(ops)
            seen = set()
            for o in order:
                if o.eng not in seen:
                    seen.add(o.eng)
                    if prevbar is not None:
                        o.deps.append(prevbar)
                final[o.eng].append(o)
                order_all.append(o)
                if o.dma:
                    lastdma[o.sig[0].num] = o
            if bar is not None:
                deps = []
                for kx in ENGS:
                    for o in reversed(final[kx]):
                        if not o.dma:
                            deps.append(o)
                            break
                deps += list(lastdma.values())
                bar.deps = deps
                for d in deps:
                    d.need = True
                final["sp"].append(bar)
                order_all.append(bar)
                prevbar = bar
        self.ops = final
        self.allops = order_all

    def emit(self):
        self.finalize()
        cnt = {k: 0 for k in ENGS}
        for k_ in ENGS:
            for o in self.ops[k_]:
                if o.dma:
                    continue
                if o.need:
                    cnt[o.eng] += 1
                    o.sig = (self.csem[o.eng], cnt[o.eng])
        for k in ENGS:
            eng = self.e[k]
            waited = {}
            for o in self.ops[k]:
                for d in o.deps:
                    s, v = d.sig
                    key = s.num
                    if waited.get(key, 0) >= v:
                        continue
                    waited[key] = v
                    eng.wait_ge(s, v)
                ins = o.fn(eng)
                if o.dma:
                    ins.then_inc(o.sig[0], 16)
                elif o.need and not o.selfsig:
                    ins.then_inc(o.sig[0], 1)


def rstd_ops(k, ss, n_inv, mh, R=(), W=()):
    k.ts("pool", ss, ss, n_inv, EPS, ALU.mult, ALU.add, R=R, W=W)
    k.tt("pool", ss, ss, mh, ALU.pow, R=list(R) + list(W), W=W)


W_SPECS = [
    ("w_mod", [DEPTH, D, 6 * D]), ("b_mod", [DEPTH, 6 * D]), ("norm1", [DEPTH, D]), ("norm2", [DEPTH, D]),
    ("w_in", [DEPTH, D, D_IN]), ("diff_qn", [DEPTH, 64]), ("diff_kn", [DEPTH, 64]), ("diff_lambda", [DEPTH, 4, 64]),
    ("diff_subln", [DEPTH, 128]), ("ret_decay", [DEPTH, 2, 4]), ("ret_gn", [DEPTH, 128]),
    ("mla_qa_norm", [DEPTH, 384]), ("w_mla_qb", [DEPTH, 384, 768]), ("mla_kva_norm", [DEPTH, 256]),
    ("w_mla_kvb", [DEPTH, 256, 1024]), ("mla_qn", [DEPTH, 192]), ("mla_kn", [DEPTH, 192]),
    ("w_branch", [DEPTH, 3, 512, D]), ("w_out", [DEPTH, D, D]), ("w_up", [DEPTH, D, 4 * D]), ("w_down", [DEPTH, 4 * D, D]),
]


def build(NS, NP, PAST, dbg=(), stop=99):
    k = KB()
    nc = k.nc
    NT = NS + 2 * NP
    NTT = NT // 128
    NK = NT + PAST
    L = DEPTH
    ein = lambda n, s: k.dram(n, s, F32, kind="ExternalInput")
    eout = lambda n, s: k.dram(n, s, F32, kind="ExternalOutput")
    x_all = ein("x_all", [NT, D]); cvec = ein("cvec", [2, D])
    cdk = ein("cdk", [L, PAST, 512]); cdv = ein("cdv", [L, PAST, 512])
    cckv = ein("cckv", [L, PAST, 256]); ckr = ein("ckr", [L, PAST, 64]); sret = ein("sret", [L, 2, 4, 64, 128])
    Wt = {n: ein(n, s) for n, s in W_SPECS}
    rope_tab = ein("rope_tab", [NT, 128]); retc = ein("retc", [4, 128, 128]); retcol = ein("retcol", [128, 4])
    y_all = eout("y_all", [NT, D])
    ndk = eout("ndk", [2, L, NP, 512]); ndv = eout("ndv", [2, L, NP, 512])
    nckv = eout("nckv", [2, L, NP, 256]); nkr = eout("nkr", [2, L, NP, 64]); nsr = eout("nsr", [2, L, 2, 4, 64, 128])

    def scr(n, s, dt):
        return k.dram(n, s, dt, kind=("ExternalOutput" if n in dbg else "Internal"))
    modrow = scr("modrow", [2, 6 * D], F32)
    hTs = scr("hTs", [NTT, 128, 8, 128], BF16); h2Ts = scr("h2Ts", [NTT, 128, 8, 128], BF16)
    dqT = scr("dqT", [4, 128, NT], BF16); dkT = scr("dkT", [4, 128, NK], BF16); dvs = scr("dvs", [NK, 512], BF16)
    rqT = scr("rqT", [2, 128, NT], BF16); rkT = scr("rkT", [2, 128, NT], BF16); QQ = scr("QQ", [4, 128, NT], BF16)
    kks = scr("kks", [NT, 512], BF16); rvs = scr("rvs", [NT, 512], BF16); rgs = scr("rgs", [NT, 512], F32)
    mqTn = scr("mqTn", [4, 128, NT], BF16); mqTr = scr("mqTr", [2, 128, NT], BF16)
    mkTn = scr("mkTn", [4, 128, NK], BF16); mkTr = scr("mkTr", [2, 128, NK], BF16); mvs = scr("mvs", [NK, 512], BF16)
    oT = scr("oT", [3, 4, 128, NT], BF16)
    xa = scr("xa", [NT, D], F32); xb = scr("xb", [NT, D], F32)

    seqs = [dict(t0=0, nt=NS // 128, ctx=True, var=0, pi=None),
            dict(t0=NS // 128, nt=NP // 128, ctx=False, var=1, pi=0),
            dict(t0=(NS + NP) // 128, nt=NP // 128, ctx=False, var=1, pi=1)]

    def seq_of(t):
        for s in seqs:
            if s["t0"] <= t < s["t0"] + s["nt"]:
                return s

    PS = []
    for i in range(8):
        h = nc.alloc_psum_tensor(f"ps{i}", [128, 512], F32)
        PS.append(T(h, k.buf("ps")))
        PS[-1].tok = k.buf("pstok")
    psb = lambda i: PS[i][:].bitcast(BF16)

    identf = k.sb([128, 128], F32, "identf"); ident = k.sb([128, 128], BF16, "ident")
    mh = k.sb([128, 8], F32, "mh")
    k.memset("pool", identf[:], 0.0, W=[identf])
    k.op("pool", lambda e: e.affine_select(out=identf[:], in_=identf[:], pattern=[[-1, 128]], compare_op=ALU.not_equal,
                                           fill=1.0, base=0, channel_multiplier=1), R=[identf], W=[identf])
    k.cp("dve", ident[:], identf[:], R=[identf], W=[ident])
    k.memset("pool", mh[:], -0.5, W=[mh])
    rc = k.sb([128, 4, 128], F32, "retc")
    k.dma(rc[:], retc.rearrange("c p i -> p c i"), W=[rc])
    rcol = k.sb([128, 4], F32, "retcol")
    k.dma(rcol[:], retcol, W=[rcol])
    MT = k.sb([128, 4, 128], F32, "MT"); qdc = k.sb([128, 4, 2], F32, "qdc"); kdc = k.sb([128, 4, 2], F32, "kdc")
    decC = k.sb([128, 4, 128], F32, "decC"); lg = k.sb([128, 2, 4], F32, "lg"); neglam = k.sb([128, 1], F32, "neglam")

    def rstd(ss, n_inv, extraR=()):
        k.ts("pool", ss[:], ss[:], n_inv, EPS, ALU.mult, ALU.add, R=[ss] + list(extraR), W=[ss])
        k.tt("pool", ss[:], ss[:], mh[:, 0:ss.h.shape[1]] if len(ss.h.shape) == 2 else mh[:], ALU.pow, R=[ss, mh], W=[ss])

    def wload(dst, src_ap, nchunk=1):
        n = src_ap.shape[-1]
        step = max(256, ((n + nchunk - 1) // nchunk + 255) // 256 * 256)
        for c0 in range(0, n, step):
            c1 = min(n, c0 + step)
            k.dma(dst[:, :, c0:c1], src_ap[:, c0:c1].rearrange("(kc p) n -> p kc n", p=128), W=[dst], q="pool")

    def bload(dst_ap, src_ap, W):
        k.dma(dst_ap, src_ap.partition_broadcast(128), W=W)

    def transposes(src_aps, bank, dst, R, n_out_part=128):
        pb = psb(bank)
        n = len(src_aps)
        for i, a in enumerate(src_aps):
            k.tr(pb[:, i * 128:(i + 1) * 128], a, ident[:], R=list(R) + [ident], W=[PS[bank]])
        k.cp("dve", dst[:].rearrange("p n c -> p (n c)"), pb[:, 0:n * 128], R=[PS[bank]], W=[dst])

    def layer_consts(l):
        k.push()
        lam_init = 0.8 - 0.6 * math.exp(-0.3 * l)
        dl = k.sb([128, 4, 64], F32); pr = k.sb([128, 2, 64], F32); sm = k.sb([128, 2], F32)
        bload(dl[:].rearrange("p a d -> p (a d)"), Wt["diff_lambda"][l].rearrange("a d -> (a d)"), W=[dl])
        dl4 = dl[:].rearrange("p (a b) d -> p a b d", b=2)
        k.tt("dve", pr[:], dl4[:, :, 0, :], dl4[:, :, 1, :], ALU.mult, R=[dl], W=[pr])
        k.red("dve", sm[:], pr[:], R=[pr], W=[sm])
        k.act(sm[:], sm[:], AF.Exp, R=[sm], W=[sm])
        k.stt("dve", neglam[:], sm[:, 1:2], -lam_init, sm[:, 0:1], ALU.add, ALU.subtract, R=[sm], W=[neglam])
        bload(lg[:].rearrange("p a h -> p (a h)"), Wt["ret_decay"][l].rearrange("a h -> (a h)"), W=[lg])
        k.act(lg[:], lg[:], AF.Exp, R=[lg], W=[lg], scale=-1.0)
        k.act(lg[:], lg[:], AF.Ln, R=[lg], W=[lg], bias=1.0)
        k.ts("dve", lg[:], lg[:], -1.0, None, ALU.mult, R=[lg], W=[lg])
        tmp = k.sb([128, 4, 128], F32); tmp2 = k.sb([128, 4, 128], F32)
        for h in range(4):
            k.ts("dve", tmp[:, h, :], rc[:, 0, :], lg[:, 0, h:h + 1], None, ALU.mult, R=[rc, lg], W=[tmp])
            k.ts("dve", tmp2[:, h, :], rc[:, 2, :], lg[:, 1, h:h + 1], None, ALU.mult, R=[rc, lg], W=[tmp2])
        k.act(tmp[:], tmp[:], AF.Exp, R=[tmp], W=[tmp])
        k.act(tmp2[:], tmp2[:], AF.Exp, R=[tmp2], W=[tmp2])
        k.tt("dve", tmp[:], tmp[:], rc[:, 1:2, :].to_broadcast([128, 4, 128]), ALU.mult, R=[tmp, rc], W=[tmp])
        k.tt("dve", tmp2[:], tmp2[:], rc[:, 3:4, :].to_broadcast([128, 4, 128]), ALU.mult, R=[tmp2, rc], W=[tmp2])
        k.tt("dve", MT[:], tmp[:], tmp2[:], ALU.add, R=[tmp, tmp2], W=[MT])
        for d in range(2):
            k.ts("dve", qdc[:, :, d], lg[:, d, :], rcol[:, d:d + 1], None, ALU.mult, R=[lg, rcol], W=[qdc])
            k.ts("dve", kdc[:, :, d], lg[:, d, :], rcol[:, 2 + d:3 + d], None, ALU.mult, R=[lg, rcol], W=[kdc])
        k.act(qdc[:], qdc[:], AF.Exp, R=[qdc], W=[qdc])
        k.act(kdc[:], kdc[:], AF.Exp, R=[kdc], W=[kdc])
        dc = k.sb([128, 4], F32)
        k.act(dc[0:64, :], lg[0:64, 0, :], AF.Exp, R=[lg], W=[dc], scale=128.0)
        k.act(dc[64:128, :], lg[64:128, 1, :], AF.Exp, R=[lg], W=[dc], scale=128.0)
        k.cp("dve", decC[:], dc[:].unsqueeze(2).to_broadcast([128, 4, 128]), R=[dc], W=[decC])
        k.pop()
        return lam_init

    def phase_mod(l):
        k.push()
        cT = k.sb([128, 2, 8], F32); sT = k.sb([128, 2, 8], F32); rep = k.sb([128, 16, 128], BF16)
        for v in range(2):
            k.dma(cT[:, v, :], cvec[v].rearrange("(kc p) -> p kc", p=128), W=[cT], allow_slow_non_contiguous=True)
        k.act(sT[:], cT[:], AF.Silu, R=[cT], W=[sT])
        k.cp("dve", rep[:], sT[:].rearrange("p v c -> p (v c)").unsqueeze(2).to_broadcast([128, 16, 128]), R=[sT], W=[rep])
        wm = [k.sb([128, 8, 512], BF16) for _ in range(2)]
        bm = [k.sb([128, 512], F32) for _ in range(2)]
        res = [k.sb([128, 512], F32) for _ in range(2)]
        for n in range(12):
            w_, b_ = wm[n % 2], bm[n % 2]
            wload(w_, Wt["w_mod"][l][:, n * 512:(n + 1) * 512])
            bload(b_[:], Wt["b_mod"][l][n * 512:(n + 1) * 512], W=[b_])
            for v in range(2):
                bank = PS[v]
                for kc in range(8):
                    k.mm(bank[:], rep[:, v * 8 + kc, :], w_[:, kc, :], kc == 0, kc == 7, R=[rep, w_], W=[bank])
                r_ = res[v]
                k.tt("dve", r_[:], bank[:], b_[:], ALU.add, R=[bank, b_], W=[r_])
                k.dma(modrow[v:v + 1, n * 512:(n + 1) * 512], r_[0:1, :], R=[r_])
        k.pop()

    def rope_mul(zv, H, Gap, G, rt, out3, tmpA, tmpB, R, W, scale_bcast=None):
        gc = k.tile("rope_gc", [128, 64], F32); gs = k.tile("rope_gs", [128, 64], F32)
        k.tt("dve", gc[:], Gap, rt[:, 0:64], ALU.mult, R=[G, rt], W=[gc])
        g4 = Gap.rearrange("p (r h x) -> p r h x", r=2, h=2)
        s4 = rt[:, 64:128].rearrange("p (r h x) -> p r h x", r=2, h=2)
        gs4 = gs[:].rearrange("p (r h x) -> p r h x", r=2, h=2)
        k.tt("dve", gs4[:, :, 0, :], g4[:, :, 1, :], s4[:, :, 0, :], ALU.mult, R=[G, rt], W=[gs])
        k.tt("dve", gs4[:, :, 1, :], g4[:, :, 0, :], s4[:, :, 1, :], ALU.mult, R=[G, rt, gs], W=[gs])
        k.tt("dve", tmpA[:, 0:H, :], zv, gc[:].unsqueeze(1).to_broadcast([128, H, 64]), ALU.mult, R=list(R) + [gc], W=[tmpA])
        z5 = zv.rearrange("p h (r f x) -> p h r f x", r=2, f=2)
        o5 = tmpB[:, 0:H, :].rearrange("p h (r f x) -> p h r f x", r=2, f=2)
        g4 = gs[:].rearrange("p (r f x) -> p r f x", r=2, f=2)
        for hf in range(2):
            k.tt("dve", o5[:, :, :, hf, :], z5[:, :, :, 1 - hf, :], g4[:, :, hf, :].unsqueeze(1).to_broadcast([128, H, 2, 16]),
                 ALU.mult, R=list(R) + [gs], W=[tmpB])
        if scale_bcast is None:
            k.tt("pool", out3, tmpA[:, 0:H, :], tmpB[:, 0:H, :], ALU.add, R=[tmpA, tmpB], W=W)
        else:
            k.tt("pool", tmpA[:, 0:H, :], tmpA[:, 0:H, :], tmpB[:, 0:H, :], ALU.add, R=[tmpA, tmpB], W=[tmpA])
            k.tt("dve", out3, tmpA[:, 0:H, :], scale_bcast, ALU.mult, R=[tmpA] + list(R), W=W)

    def store_T(src, nblk, bank, dstT_ap_fn, R):
        tT = k.tile(f"stT{bank}_{nblk}", [128, nblk, 128], BF16)
        transposes([src[:, i * 128:(i + 1) * 128] for i in range(nblk)], bank, tT, R=[src])
        k.dma(dstT_ap_fn(), tT[:], R=[tT])

    def phase_A(l, xsrc):
        k.push()
        k.reorder = True
        wA = k.sb([128, 8, NPH_A], BF16, "wA"); wload(wA, Wt["w_in"][l][:, 0:NPH_A], nchunk=8)
        wqb = k.sb([128, 3, 768], BF16, "wqb"); wload(wqb, Wt["w_mla_qb"][l])
        wkvb = k.sb([128, 2, 1024], BF16, "wkvb"); wload(wkvb, Wt["w_mla_kvb"][l])
        A1, B1 = [], []
        n1 = k.sb([128, D], F32, "n1"); bload(n1[:], Wt["norm1"][l], W=[n1])
        for v in range(2):
            a = k.sb([128, D], F32, "A1"); b = k.sb([128, D], F32, "B1")
            bload(a[:], modrow[v, D:2 * D], W=[a]); bload(b[:], modrow[v, 0:D], W=[b])
            k.stt("dve", a[:], a[:], 1.0, n1[:], ALU.add, ALU.mult, R=[a, n1], W=[a])
            A1.append(a); B1.append(b)
        gq = k.sb([128, 64], F32, "gq"); bload(gq[:], Wt["diff_qn"][l], W=[gq])
        gk = k.sb([128, 64], F32, "gk"); bload(gk[:], Wt["diff_kn"][l], W=[gk])
        ones64 = k.sb([128, 64], F32, "ones"); k.memset("pool", ones64[:], 1.0, W=[ones64])
        eighth = k.sb([128, 64], F32, "eighth"); k.memset("pool", eighth[:], 0.125, W=[eighth])
        gqa = k.sb([128, 384], F32, "gqa"); bload(gqa[:], Wt["mla_qa_norm"][l], W=[gqa])
        gkva = k.sb([128, 256], F32, "gkva"); bload(gkva[:], Wt["mla_kva_norm"][l], W=[gkva])
        gmqn = k.sb([128, 192], F32, "gmqn"); bload(gmqn[:], Wt["mla_qn"][l], W=[gmqn])
        gmkn = k.sb([128, 192], F32, "gmkn"); bload(gmkn[:], Wt["mla_kn"][l], W=[gmkn])
        rt1 = k.sb([128, 128], F32, "rt1")
        k.memset("pool", rt1[:, 0:64], 1.0, W=[rt1]); k.memset("pool", rt1[:, 64:128], 0.0, W=[rt1])
        tA = k.sb([128, 8, 64], F32, "tA"); tB = k.sb([128, 8, 64], F32, "tB")

        def mla_kv(ckvb, krf, rt, col):
            ckvT = k.tile("ckvT", [128, 2, 128], BF16)
            transposes([ckvb[:, 0:128], ckvb[:, 128:256]], 5, ckvT, R=[ckvb])
            for c in range(2):
                for kc in range(2):
                    k.mm(PS[6 + c][:], ckvT[:, kc, :], wkvb[:, kc, c * 512:(c + 1) * 512], kc == 0, kc == 1, R=[ckvT, wkvb], W=[PS[6 + c]])
            sqk = k.tile("sqk", [128, 4, 128], F32); ssn = k.tile("ssn", [128, 4], F32)
            kv4 = [PS[6 + c][:].rearrange("p (h x d) -> p h x d", h=2, x=2) for c in range(2)]
            for c in range(2):
                k.act(sqk[:, 2 * c:2 * c + 2, :], kv4[c][:, :, 0, :], AF.Square, R=[PS[6 + c]], W=[sqk])
            k.red("dve", ssn[:], sqk[:], R=[sqk], W=[ssn])
            skr = k.tile("skr", [128, 1], F32); junk3 = k.tile("junk3", [128, 64], F32)
            k.act(junk3[:], krf[:], AF.Square, R=[krf], W=[junk3, skr], accum_out=skr[:])
            k.ts("dve", ssn[:], ssn[:], skr[:, 0:1], None, ALU.add, R=[ssn, skr], W=[ssn])
            rstd(ssn, 1.0 / 192)
            krg = k.tile("krg", [128, 1, 64], F32)
            rope_mul(krf[:].unsqueeze(1), 1, gmkn[:, 128:192], gmkn, rt, krg[:], tA, tB, R=[krf], W=[krg])
            mkn = k.tile("mkn", [128, 4, 128], BF16); mkr = k.tile("mkr", [128, 4, 64], BF16); mvb = k.tile("mvb", [128, 4, 128], BF16)
            for c in range(2):
                tn = k.tile("tn", [128, 2, 128], F32)
                k.tt("dve", tn[:], kv4[c][:, :, 0, :], gmkn[:, 0:128].unsqueeze(1).to_broadcast([128, 2, 128]), ALU.mult, R=[PS[6 + c], gmkn], W=[tn])
                k.tt("dve", mkn[:, 2 * c:2 * c + 2, :], tn[:], ssn[:, 2 * c:2 * c + 2].unsqueeze(2).to_broadcast([128, 2, 128]), ALU.mult, R=[tn, ssn], W=[mkn])
                k.cp("act", mvb[:, 2 * c:2 * c + 2, :], kv4[c][:, :, 1, :], R=[PS[6 + c]], W=[mvb])
            for h_ in range(4):
                k.ts("dve", mkr[:, h_, :], krg[:, 0, :], ssn[:, h_:h_ + 1], None, ALU.mult, R=[krg, ssn], W=[mkr])
            store_T(T(mkn.h[:].rearrange("p h d -> p (h d)"), mkn.b), 4, 5, lambda: mkTn[:, :, col:col + 128].rearrange("h p c -> p h c"), R=[mkn])
            store_T(T(mkr.h[:].rearrange("p h d -> p (h d)"), mkr.b), 2, 5, lambda: mkTr[:, :, col:col + 128].rearrange("h p c -> p h c"), R=[mkr])
            k.dma(mvs[col:col + 128, :], mvb[:].rearrange("p h d -> p (h d)"), R=[mvb])

        for ct in range(PAST // 128 if int(os.environ.get("A_CTX", "1")) else 0):
            col = NT + ct * 128
            rows = slice(ct * 128, (ct + 1) * 128)
            CP = int(os.environ.get("A_CTXP", "9"))
            ckb = k.tile(f"ckb{ct % 2}", [128, 512], BF16)
            k.dma(ckb[:], cdk[l, rows, :], W=[ckb], q="pool")
            if CP >= 1:
                store_T(ckb, 4, 5, lambda: dkT[:, :, col:col + 128].rearrange("h p c -> p h c"), R=[ckb])
            cvb = k.tile(f"cvb{ct % 2}", [128, 512], BF16)
            if CP >= 2:
                k.dma(cvb[:], cdv[l, rows, :], W=[cvb], q="pool")
                k.dma(dvs[col:col + 128, :], cvb[:], R=[cvb])
            cc = k.tile(f"ccb{ct % 2}", [128, 256], BF16)
            crf = k.tile(f"crf{ct % 2}", [128, 64], F32)
            if CP >= 3:
                k.dma(cc[:], cckv[l, rows, :], W=[cc], q="pool")
                k.dma(crf[:], ckr[l, rows, :], W=[crf])
            if CP >= 4:
                mla_kv(cc, crf, rt1, col)

        def loads(t):
            xt = k.tile(f"x{t % 2}", [128, D], F32); rt = k.tile(f"rt{t % 2}", [128, 128], F32)
            k.dma(xt[:], xsrc[t * 128:(t + 1) * 128, :], W=[xt])
            k.dma(rt[:], rope_tab[t * 128:(t + 1) * 128, :], W=[rt])

        loads(0)
        A_TILES = int(os.environ.get("A_TILES", "999")); A_PARTS = int(os.environ.get("A_PARTS", "99"))
        for t in range(min(NTT, A_TILES)):
            if t + 1 < NTT:
                loads(t + 1)
            s = seq_of(t); v = s["var"]; pi = s["pi"]
            lrows = slice((t - s["t0"]) * 128, (t - s["t0"] + 1) * 128)
            grow = slice(t * 128, (t + 1) * 128)
            xt = k.tile(f"x{t % 2}", [128, D], F32); rt = k.tile(f"rt{t % 2}", [128, 128], F32)
            junk = k.tile("junk", [128, D], BF16); ssx = k.tile("ssx", [128, 1], F32)
            k.act(junk[:], xt[:], AF.Square, R=[xt], W=[junk, ssx], accum_out=ssx[:])
            rstd(ssx, 1.0 / D)
            tmp = k.tile("htmp", [128, D], F32); hb = k.tile("hb", [128, D], BF16)
            k.stt("dve", tmp[:], xt[:], ssx[:, 0:1], A1[v][:], ALU.mult, ALU.mult, R=[xt, ssx, A1[v]], W=[tmp])
            k.tt("pool", hb[:], tmp[:], B1[v][:], ALU.add, R=[tmp, B1[v]], W=[hb])
            hT = k.tile(f"hT{t % 2}", [128, 8, 128], BF16)
            transposes([hb[:, i * 128:(i + 1) * 128] for i in range(8)], 4, hT, R=[hb])
            k.dma(hTs[t], hT[:], R=[hT])

            def zmm(c0, c1, bank):
                for kc in range(8):
                    k.mm(PS[bank][:, 0:c1 - c0], hT[:, kc, :], wA[:, kc, c0:c1], kc == 0, kc == 7, R=[hT, wA], W=[PS[bank]])

            def qknorm(bank, G, name, f32out):
                z3 = PS[bank][:].rearrange("p (h d) -> p h d", d=64)
                sq = k.tile("sq", [128, 512], F32); ss8 = k.tile("ss8", [128, 8], F32)
                k.act(sq[:], PS[bank][:], AF.Square, R=[PS[bank]], W=[sq])
                k.red("dve", ss8[:], sq[:].rearrange("p (h d) -> p h d", d=64), R=[sq], W=[ss8])
                rstd(ss8, 1.0 / 64)
                ob = k.tile(name + "b", [128, 512], BF16)
                sc = ss8[:].unsqueeze(2).to_broadcast([128, 8, 64])
                if f32out:
                    of = k.tile(name + "f", [128, 512], F32)
                    rope_mul(z3, 8, G[:], G, rt, of[:].rearrange("p (h d) -> p h d", d=64), tA, tB, R=[PS[bank], ss8], W=[of], scale_bcast=sc)
                    k.cp("act", ob[:], of[:], R=[of], W=[ob])
                    return ob, of
                rope_mul(z3, 8, G[:], G, rt, ob[:].rearrange("p (h d) -> p h d", d=64), tA, tB, R=[PS[bank], ss8], W=[ob], scale_bcast=sc)
                return ob, None

            if A_PARTS <= 0:
                continue
            zmm(0, 512, 0)
            qb, _ = qknorm(0, gq, "dq", False)
            store_T(qb, 4, 5, lambda: dqT[:, :, grow].rearrange("h p c -> p h c"), R=[qb])
            if A_PARTS <= 1:
                continue
            zmm(512, 1024, 1)
            API = int(os.environ.get("A_PI", "7"))
            kb_, kf_ = qknorm(1, gk, "dk", pi is not None and (API & 1))
            store_T(kb_, 4, 5, lambda: dkT[:, :, grow].rearrange("h p c -> p h c"), R=[kb_])
            if pi is not None and (API & 1):
                k.dma(ndk[pi, l, lrows, :], kf_[:], R=[kf_])
            if A_PARTS <= 2:
                continue
            zmm(1024, 1536, 2)
            dvb = k.tile("dvb", [128, 512], BF16)
            k.cp("act", dvb[:], PS[2][:], R=[PS[2]], W=[dvb])
            k.dma(dvs[grow, :], dvb[:], R=[dvb])
            if pi is not None and (API & 2):
                dvf = k.tile("dvf", [128, 512], F32)
                k.cp("act", dvf[:], PS[2][:], R=[PS[2]], W=[dvf])
                k.dma(ndv[pi, l, lrows, :], dvf[:], R=[dvf])
            if A_PARTS <= 3:
                continue
            zmm(1536, 2048, 3)
            z = PS[3]
            rqf = k.tile("rqf", [128, 4, 64], F32); rkf = k.tile("rkf", [128, 4, 64], F32)
            rope_mul(z[:, 0:256].rearrange("p (h d) -> p h d", d=64), 4, ones64[:], ones64, rt, rqf[:], tA, tB, R=[z], W=[rqf])
            rope_mul(z[:, 256:512].rearrange("p (h d) -> p h d", d=64), 4, eighth[:], eighth, rt, rkf[:], tA, tB, R=[z], W=[rkf])
            rqb = k.tile("rqb", [128, 256], BF16); rkb = k.tile("rkb", [128, 256], BF16)
            k.cp("act", rqb[:], rqf[:].rearrange("p h d -> p (h d)"), R=[rqf], W=[rqb])
            k.cp("act", rkb[:], rkf[:].rearrange("p h d -> p (h d)"), R=[rkf], W=[rkb])
            rq2 = k.tile("rq2", [128, 4, 2, 64], BF16); kk = k.tile("kk", [128, 4, 2, 64], BF16)
            for d_ in range(2):
                k.tt("dve", rq2[:, :, d_, :], rqf[:], qdc[:, :, d_:d_ + 1].to_broadcast([128, 4, 64]), ALU.mult, R=[rqf, qdc], W=[rq2])
                k.tt("dve", kk[:, :, d_, :], rkf[:], kdc[:, :, d_:d_ + 1].to_broadcast([128, 4, 64]), ALU.mult, R=[rkf, kdc], W=[kk])
            store_T(rqb, 2, 5, lambda: rqT[:, :, grow].rearrange("h p c -> p h c"), R=[rqb])
            store_T(rkb, 2, 5, lambda: rkT[:, :, grow].rearrange("h p c -> p h c"), R=[rkb])
            store_T(T(rq2.h[:].rearrange("p h a d -> p (h a d)"), rq2.b), 4, 5, lambda: QQ[:, :, grow].rearrange("h p c -> p h c"), R=[rq2])
            k.dma(kks[grow, :], kk[:].rearrange("p h a d -> p (h a d)"), R=[kk])
            if A_PARTS <= 4:
                continue
            zmm(2048, 2560, 0)
            rvb = k.tile("rvb", [128, 512], BF16)
            k.cp("act", rvb[:], PS[0][:], R=[PS[0]], W=[rvb])
            k.dma(rvs[grow, :], rvb[:], R=[rvb])
            if A_PARTS <= 5:
                continue
            zmm(2560, 3072, 1)
            rgf = k.tile("rgf", [128, 512], F32)
            k.cp("act", rgf[:], PS[1][:], R=[PS[1]], W=[rgf])
            k.dma(rgs[grow, :], rgf[:], R=[rgf])
            if A_PARTS <= 6:
                continue
            zmm(3072, 3456, 2)
            z = PS[2]
            ssq = k.tile("ssq", [128, 1], F32); junk2 = k.tile("junk2", [128, 384], BF16)
            k.act(junk2[:], z[:, 0:384], AF.Square, R=[z], W=[junk2, ssq], accum_out=ssq[:])
            rstd(ssq, 1.0 / 384)
            qab = k.tile("qab", [128, 384], BF16)
            k.stt("dve", qab[:], z[:, 0:384], ssq[:, 0:1], gqa[:], ALU.mult, ALU.mult, R=[z, ssq, gqa], W=[qab])
            qaT = k.tile("qaT", [128, 3, 128], BF16)
            transposes([qab[:, i * 128:(i + 1) * 128] for i in range(3)], 5, qaT, R=[qab])
            mqn = k.tile("mqn", [128, 4, 128], BF16); mqr = k.tile("mqr", [128, 4, 64], BF16)
            for c in range(2):
                bk = PS[6 + c]
                for kc in range(3):
                    k.mm(bk[:, 0:384], qaT[:, kc, :], wqb[:, kc, c * 384:(c + 1) * 384], kc == 0, kc == 2, R=[qaT, wqb], W=[bk])
                sq = k.tile("sq", [128, 512], F32); ss2 = k.tile("ss2", [128, 2], F32)
                k.act(sq[:, 0:384], bk[:, 0:384], AF.Square, R=[bk], W=[sq])
                k.red("dve", ss2[:], sq[:, 0:384].rearrange("p (h d) -> p h d", d=192), R=[sq], W=[ss2])
                rstd(ss2, 1.0 / 192)
                tq = k.tile("tq", [128, 2, 192], F32); tqr = k.tile("tqr", [128, 2, 64], F32)
                k.tt("dve", tq[:], bk[:, 0:384].rearrange("p (h d) -> p h d", d=192), gmqn[:].unsqueeze(1).to_broadcast([128, 2, 192]), ALU.mult, R=[bk, gmqn], W=[tq])
                rope_mul(tq[:, :, 128:192], 2, ones64[:], ones64, rt, tqr[:], tA, tB, R=[tq], W=[tqr])
                k.tt("dve", mqn[:, 2 * c:2 * c + 2, :], tq[:, :, 0:128], ss2[:].unsqueeze(2).to_broadcast([128, 2, 128]), ALU.mult, R=[tq, ss2], W=[mqn])
                k.tt("dve", mqr[:, 2 * c:2 * c + 2, :], tqr[:], ss2[:].unsqueeze(2).to_broadcast([128, 2, 64]), ALU.mult, R=[tqr, ss2], W=[mqr])
            store_T(T(mqn.h[:].rearrange("p h d -> p (h d)"), mqn.b), 4, 5, lambda: mqTn[:, :, grow].rearrange("h p c -> p h c"), R=[mqn])
            store_T(T(mqr.h[:].rearrange("p h d -> p (h d)"), mqr.b), 2, 5, lambda: mqTr[:, :, grow].rearrange("h p c -> p h c"), R=[mqr])
            if A_PARTS <= 7:
                continue
            zmm(3456, 3776, 3)
            z = PS[3]
            ssk = k.tile("ssk", [128, 1], F32)
            k.act(junk2[:, 0:256], z[:, 0:256], AF.Square, R=[z], W=[junk2, ssk], accum_out=ssk[:])
            rstd(ssk, 1.0 / 256)
            ckvf = k.tile("ckvf", [128, 256], F32); krf = k.tile("krf", [128, 64], F32); ckvb = k.tile("ckvb", [128, 256], BF16)
            k.stt("dve", ckvf[:], z[:, 0:256], ssk[:, 0:1], gkva[:], ALU.mult, ALU.mult, R=[z, ssk, gkva], W=[ckvf])
            k.cp("act", krf[:], z[:, 256:320], R=[z], W=[krf])
            k.cp("act", ckvb[:], ckvf[:], R=[ckvf], W=[ckvb])
            if pi is not None and (API & 4):
                k.dma(nckv[pi, l, lrows, :], ckvf[:], R=[ckvf])
                k.dma(nkr[pi, l, lrows, :], krf[:], R=[krf])
            mla_kv(ckvb, krf, rt, t * 128)
        k.pop()
        k.reorder = False

    def phase_ret(l):
        for s in seqs:
            k.push()
            nch = s["nt"]; t0 = s["t0"]; pi = s["pi"]
            gn = k.sb([128, 128], F32, "gn"); bload(gn[:], Wt["ret_gn"][l], W=[gn])
            rv = k.sb([128, nch, 512], BF16, "rv")
            k.dma(rv[:], rvs[t0 * 128:(t0 + nch) * 128, :].rearrange("(c p) f -> p c f", p=128), W=[rv])
            U = k.sb([128, nch, 512], F32, "U"); RR = k.sb([128, nch, 512], BF16, "RR"); S = k.sb([128, 512], F32, "S")
            Ub = [k.buf() for _ in range(nch)]; RRf = [k.buf() for _ in range(nch)]; RRb = [k.buf() for _ in range(nch)]
            Sf = k.buf(); Sb = k.buf()
            for c in range(nch):
                kkt = k.tile(f"kkt{c % 2}", [128, 512], BF16)
                k.dma(kkt[:], kks[(t0 + c) * 128:(t0 + c + 1) * 128, :], W=[kkt])
                bank = PS[c % 2]
                for h in range(4):
                    hs = slice(h * 128, (h + 1) * 128)
                    k.mm(bank[:, hs], kkt[:, hs], rv[:, c, hs], True, True, R=[kkt, rv], W=[bank])
                k.cp("act", U[:, c, :], bank[:], R=[bank], W=[Ub[c]])
            RS = int(os.environ.get("R_STOP", "9"))
            if RS < 1:
                k.pop(); k.barrier(); continue
            if s["ctx"]:
                for d in range(2):
                    k.dma(S[64 * d:64 * d + 64, :].rearrange("k (h v) -> k h v", h=4), sret[l, d].rearrange("h k v -> k h v"), W=[Sf if d == 0 else Sb])
            else:
                k.memset("dve", S[0:64, :], 0.0, W=[Sf]); k.memset("pool", S[64:128, :], 0.0, W=[Sb])
            dec2 = decC[:].rearrange("p h e -> p (h e)")
            for c in range(nch):
                k.cp("dve", RR[0:64, c, :], S[0:64, :], R=[Sf], W=[RRf[c]])
                k.tt("dve", S[0:64, :], S[0:64, :], dec2[0:64, :], ALU.mult, R=[Sf, decC], W=[Sf])
                k.tt("dve", S[0:64, :], S[0:64, :], U[0:64, c, :], ALU.add, R=[Sf, Ub[c]], W=[Sf])
            for c in reversed(range(nch)):
                k.cp("act", RR[64:128, c, :], S[64:128, :], R=[Sb], W=[RRb[c]])
                k.tt("pool", S[64:128, :], S[64:128, :], dec2[64:128, :], ALU.mult, R=[Sb, decC], W=[Sb])
                k.tt("pool", S[64:128, :], S[64:128, :], U[64:128, c, :], ALU.add, R=[Sb, Ub[c]], W=[Sb])
            if RS < 2:
                k.pop(); k.barrier(); continue
            if pi is not None:
                k.dma(nsr[pi, l, 0].rearrange("h k v -> k h v"), S[0:64, :].rearrange("k (h v) -> k h v", h=4), R=[Sf])
                k.dma(nsr[pi, l, 1].rearrange("h k v -> k h v"), S[64:128, :].rearrange("k (h v) -> k h v", h=4), R=[Sb])
            if RS < 3:
                k.pop(); k.barrier(); continue
            for c in range(nch):
                t = t0 + c; cols = slice(t * 128, (t + 1) * 128)
                kT = k.tile(f"rkT{c % 2}", [128, 2, 128], BF16); qT = k.tile(f"rqT{c % 2}", [128, 2, 128], BF16)
                qq = k.tile(f"rqq{c % 2}", [128, 4, 128], BF16); rg = k.tile(f"rrg{c % 2}", [128, 512], F32)
                k.dma(kT[:], rkT[:, :, cols].rearrange("h p c -> p h c"), W=[kT])
                k.dma(qT[:], rqT[:, :, cols].rearrange("h p c -> p h c"), W=[qT])
                k.dma(qq[:], QQ[:, :, cols].rearrange("h p c -> p h c"), W=[qq])
                k.dma(rg[:], rgs[cols, :], W=[rg])
                bsr = [PS[2 - 2 * (c % 2)], PS[3 - 2 * (c % 2)]]; bo = PS[4 + c % 2]
                for h in range(4):
                    pp = slice(64 * (h % 2), 64 * (h % 2) + 64)
                    k.mm(bsr[h % 2][:, (h // 2) * 128:(h // 2 + 1) * 128], kT[pp, h // 2, :], qT[pp, h // 2, :], True, True, R=[kT, qT], W=[bsr[h % 2]])
                P = k.tile(f"rP{c % 2}", [128, 512], BF16)
                P4 = P[:].rearrange("p (a r i) -> p a r i", a=2, r=2)
                MT4 = MT[:].rearrange("p (a r) i -> p a r i", r=2)
                for r_ in range(2):
                    k.tt("dve", P4[:, :, r_, :], bsr[r_][:, 0:256].rearrange("p (a i) -> p a i", a=2), MT4[:, :, r_, :], ALU.mult, R=[bsr[r_], MT], W=[P])
                for h in range(4):
                    hs = slice(h * 128, (h + 1) * 128)
                    k.mm(bo[:, hs], P[:, hs], rv[:, c, hs], True, False, R=[P, rv], W=[bo])
                    k.mm(bo[:, hs], qq[:, h, :], RR[:, c, hs], False, True, R=[qq, RRf[c], RRb[c]], W=[bo])
                sq = k.tile("rsq", [128, 512], F32); ss4 = k.tile("rss4", [128, 4], F32)
                k.act(sq[:], bo[:], AF.Square, R=[bo], W=[sq])
                k.red("dve", ss4[:], sq[:].rearrange("p (h e) -> p h e", h=4), R=[sq], W=[ss4])
                rstd(ss4, 1.0 / 128)
                to = k.tile("rto", [128, 4, 128], F32)
                k.tt("dve", to[:], bo[:].rearrange("p (h e) -> p h e", h=4), ss4[:].unsqueeze(2).to_broadcast([128, 4, 128]), ALU.mult, R=[bo, ss4], W=[to])
                k.tt("dve", to[:], to[:], gn[:].unsqueeze(1).to_broadcast([128, 4, 128]), ALU.mult, R=[to, gn], W=[to])
                sg = k.tile("rsg", [128, 512], F32)
                k.act(sg[:], rg[:], AF.Silu, R=[rg], W=[sg])
                orb = k.tile("orb", [128, 512], BF16)
                k.tt("pool", orb[:], to[:].rearrange("p h e -> p (h e)"), sg[:], ALU.mult, R=[to, sg], W=[orb])
                store_T(orb, 4, 6, lambda: oT[1][:, :, cols].rearrange("h p c -> p h c"), R=[orb])
            k.pop()
            k.barrier()

    def attn_core(pairs_fn, V, nkt, QC, scale, R):
        nqt = QC // 128

        def st(kt):
            bank = PS[kt % 2]
            prs = pairs_fn(kt)
            for i, (a, b) in enumerate(prs):
                k.mm(bank[:, 0:QC], a, b, i == 0, i == len(prs) - 1, R=R, W=[bank])

        def pv(kt):
            bank = PS[kt % 2]
            PT = k.tile(f"PT{kt % 3}", [128, 512], BF16)
            k.act(PT[:, 0:QC], bank[:, 0:QC], AF.Exp, R=[bank], W=[PT], scale=scale)
            for qt in range(nqt):
                k.mm(PS[2 + qt][:, 0:129], PT[:, qt * 128:(qt + 1) * 128], V[:, kt, :], kt == 0, kt == nkt - 1, R=[PT, V], W=[PS[2 + qt]])
        st(0)
        for kt in range(nkt):
            if kt + 1 < nkt:
                st(kt + 1)
            pv(kt)

    def phase_attn(l, lam_init):
        k.push()
        gsub = k.sb([128, 128], F32, "gsub"); bload(gsub[:], Wt["diff_subln"][l], W=[gsub])
        k.ts("dve", gsub[:], gsub[:], 1.0 - lam_init, None, ALU.mult, R=[gsub], W=[gsub])
        for s in seqs:
            k.push()
            n = s["nt"] * 128; q0 = s["t0"] * 128
            nctx = PAST if s["ctx"] else 0
            nkt = (n + nctx) // 128
            QC = min(512, n); nqt = QC // 128
            V = k.sb([128, nkt, 129], BF16, "V"); kTa = k.sb([128, nkt * 128], BF16, "kTa"); kTb = k.sb([128, nkt * 128], BF16, "kTb")
            k.memset("dve", V[:], 1.0, W=[V])

            def load_keys(dst, src2d):
                k.dma(dst[:, 0:n], src2d[:, q0:q0 + n], W=[dst])
                if nctx:
                    k.dma(dst[:, n:n + nctx], src2d[:, NT:NT + nctx], W=[dst])

            def load_V(src, h):
                k.dma(V[:, 0:n // 128, 0:128], src[q0:q0 + n, h * 128:(h + 1) * 128].rearrange("(c p) e -> p c e", p=128), W=[V])
                if nctx:
                    k.dma(V[:, n // 128:nkt, 0:128], src[NT:NT + nctx, h * 128:(h + 1) * 128].rearrange("(c p) e -> p c e", p=128), W=[V])

            for h in range(4):
                load_keys(kTa, dkT[h]); load_V(dvs, h)
                for qc in range(n // QC):
                    qs = slice(q0 + qc * QC, q0 + (qc + 1) * QC)
                    qp = []
                    for m in range(2):
                        nm = f"qTd{m}_{qc % 2}"
                        fresh = nm not in k.cache
                        t_ = k.tile(nm, [128, QC], BF16)
                        if fresh:
                            k.memset("dve", t_[64 * (1 - m):64 * (1 - m) + 64, :], 0.0, W=[t_])
                        k.dma(t_[64 * m:64 * m + 64, :], dqT[h][64 * m:64 * m + 64, qs], W=[t_])
                        qp.append(t_)
                    Oa = k.tile("Oa", [128, nqt, 128], F32); ob = k.tile("odb", [128, nqt * 128], BF16)
                    for m in range(2):
                        pp = slice(64 * m, 64 * m + 64)
                        attn_core(lambda kt: [(kTa[:, kt * 128:(kt + 1) * 128], qp[m][:, :])], V, nkt, QC, 0.125, R=[kTa, qp[m]])
                        for qt in range(nqt):
                            O = PS[2 + qt]
                            rinv = k.tile("rinv", [128, 1], F32)
                            k.op("dve", lambda e, O=O, rinv=rinv: e.reciprocal(out=rinv[:], in_=O[:, 128:129]), R=[O], W=[rinv])
                            if m == 0:
                                k.ts("dve", Oa[:, qt, :], O[:, 0:128], rinv[:, 0:1], None, ALU.mult, R=[O, rinv], W=[Oa])
                            else:
                                k.tt("dve", rinv[:], rinv[:], neglam[:], ALU.mult, R=[rinv, neglam], W=[rinv])
                                od = k.tile("od", [128, 128], F32)
                                k.stt("dve", od[:], O[:, 0:128], rinv[:, 0:1], Oa[:, qt, :], ALU.mult, ALU.add, R=[O, rinv, Oa], W=[od])
                                ssd = k.tile("ssd", [128, 1], F32); junk = k.tile("ajunk", [128, 128], BF16)
                                k.act(junk[:], od[:], AF.Square, R=[od], W=[junk, ssd], accum_out=ssd[:])
                                rstd(ssd, 1.0 / 128)
                                k.stt("dve", ob[:, qt * 128:(qt + 1) * 128], od[:], ssd[:, 0:1], gsub[:], ALU.mult, ALU.mult, R=[od, ssd, gsub], W=[ob])
                    store_T(ob, nqt, 6, lambda: oT[0, h][:, qs].rearrange("p (i c) -> p i c", c=128), R=[ob])
            for h in range(4):
                load_keys(kTa, mkTn[h]); load_keys(kTb, mkTr[h // 2]); load_V(mvs, h)
                pp = slice(64 * (h % 2), 64 * (h % 2) + 64)
                for qc in range(n // QC):
                    qs = slice(q0 + qc * QC, q0 + (qc + 1) * QC)
                    qTn = k.tile(f"qTn{qc % 2}", [128, QC], BF16)
                    nm = f"qTr{h % 2}_{qc % 2}"
                    fresh = nm not in k.cache
                    qTr = k.tile(nm, [128, QC], BF16)
                    if fresh:
                        k.memset("dve", qTr[64 * (1 - h % 2):64 * (1 - h % 2) + 64, :], 0.0, W=[qTr])
                    k.dma(qTn[:], mqTn[h][:, qs], W=[qTn]); k.dma(qTr[pp, :], mqTr[h // 2][pp, qs], W=[qTr])
                    attn_core(lambda kt: [(kTa[:, kt * 128:(kt + 1) * 128], qTn[:, :]), (kTb[:, kt * 128:(kt + 1) * 128], qTr[:, :])],
                              V, nkt, QC, 192 ** -0.5, R=[kTa, kTb, qTn, qTr])
                    omb = k.tile("omb", [128, nqt * 128], BF16)
                    for qt in range(nqt):
                        O = PS[2 + qt]
                        rinv = k.tile("rinv", [128, 1], F32)
                        k.op("dve", lambda e, O=O, rinv=rinv: e.reciprocal(out=rinv[:], in_=O[:, 128:129]), R=[O], W=[rinv])
                        k.ts("dve", omb[:, qt * 128:(qt + 1) * 128], O[:, 0:128], rinv[:, 0:1], None, ALU.mult, R=[O, rinv], W=[omb])
                    store_T(omb, nqt, 6, lambda: oT[2, h][:, qs].rearrange("p (i c) -> p i c", c=128), R=[omb])
            k.pop()
            k.barrier()
        k.pop()

    def mod_tiles(l, which):
        out = {}
        for name, idx in which:
            out[name] = []
            for v in range(2):
                a = k.sb([128, D], F32, name)
                bload(a[:], modrow[v, idx * D:(idx + 1) * D], W=[a])
                out[name].append(a)
        return out

    def phase_merge(l, xsrc):
        k.push()
        k.reorder = True
        wg = k.sb([128, 8, 3072], BF16, "wg"); wload(wg, Wt["w_in"][l][:, NPH_A:D_IN], nchunk=6)
        wb = k.sb([128, 12, 1024], BF16, "wb")
        for i in range(3):
            wload(T(wb.h[:, 4 * i:4 * i + 4, :], wb.b), Wt["w_branch"][l, i])
        wo = k.sb([128, 8, 1024], BF16, "wo"); wload(wo, Wt["w_out"][l], nchunk=2)
        m = mod_tiles(l, [("G1", 2), ("B2", 3), ("A2", 4)])
        n2 = k.sb([128, D], F32, "n2"); bload(n2[:], Wt["norm2"][l], W=[n2])
        for v in range(2):
            a = m["A2"][v]
            k.stt("dve", a[:], a[:], 1.0, n2[:], ALU.add, ALU.mult, R=[a, n2], W=[a])

        def loads(t):
            cols = slice(t * 128, (t + 1) * 128)
            xt = k.tile(f"x{t % 2}", [128, D], F32); hT = k.tile(f"hT{t % 2}", [128, 8, 128], BF16)
            oTt = k.tile(f"oTt{t % 2}", [128, 12, 128], BF16)
            k.dma(xt[:], xsrc[cols, :], W=[xt]); k.dma(hT[:], hTs[t], W=[hT])
            for i in range(3):
                k.dma(oTt[:, 4 * i:4 * i + 4, :], oT[i][:, :, cols].rearrange("h p c -> p h c"), W=[oTt])
        loads(0)
        for t in range(NTT):
            if t + 1 < NTT:
                loads(t + 1)
            v = seq_of(t)["var"]; grow = slice(t * 128, (t + 1) * 128)
            xt = k.tile(f"x{t % 2}", [128, D], F32); hT = k.tile(f"hT{t % 2}", [128, 8, 128], BF16)
            oTt = k.tile(f"oTt{t % 2}", [128, 12, 128], BF16)
            mer = k.tile("mer", [128, D], F32)
            for nn in range(2):
                cs = slice(nn * 512, (nn + 1) * 512)
                for i in range(3):
                    j = nn * 3 + i
                    bg = PS[j % 2]; bo = PS[2 + j % 2]
                    for kc in range(8):
                        k.mm(bg[:], hT[:, kc, :], wg[:, kc, i * 1024 + nn * 512:i * 1024 + nn * 512 + 512], kc == 0, kc == 7, R=[hT, wg], W=[bg])
                    for kc in range(4):
                        k.mm(bo[:], oTt[:, i * 4 + kc, :], wb[:, i * 4 + kc, cs], kc == 0, kc == 3, R=[oTt, wb], W=[bo])
                    sig = k.tile(f"sig{j % 2}", [128, 512], F32)
                    k.act(sig[:], bg[:], AF.Sigmoid, R=[bg], W=[sig])
                    if i == 0:
                        k.tt("dve", mer[:, cs], sig[:], bo[:], ALU.mult, R=[sig, bo], W=[mer])
                    else:
                        tmpm = k.tile(f"tmpm{j % 2}", [128, 512], F32)
                        k.tt("dve", tmpm[:], sig[:], bo[:], ALU.mult, R=[sig, bo], W=[tmpm])
                        k.tt("pool", mer[:, cs], mer[:, cs], tmpm[:], ALU.add, R=[mer, tmpm], W=[mer])
            merb = k.tile("merb", [128, D], BF16)
            k.cp("act", merb[:], mer[:], R=[mer], W=[merb])
            mT = k.tile("mT", [128, 8, 128], BF16)
            transposes([merb[:, i * 128:(i + 1) * 128] for i in range(8)], 6, mT, R=[merb])
            x1 = k.tile("xout", [128, D], F32)
            for nn in range(2):
                cs = slice(nn * 512, (nn + 1) * 512)
                bo = PS[4 + nn]
                for kc in range(8):
                    k.mm(bo[:], mT[:, kc, :], wo[:, kc, cs], kc == 0, kc == 7, R=[mT, wo], W=[bo])
                tmpo = k.tile(f"tmpo{nn}", [128, 512], F32)
                k.tt("dve", tmpo[:], bo[:], m["G1"][v][:, cs], ALU.mult, R=[bo, m["G1"][v]], W=[tmpo])
                k.tt("pool", x1[:, cs], xt[:, cs], tmpo[:], ALU.add, R=[xt, tmpo], W=[x1])
            k.dma(xa[grow, :], x1[:], R=[x1])
            junk = k.tile("junk", [128, D], BF16); ssx = k.tile("ssx", [128, 1], F32)
            k.act(junk[:], x1[:], AF.Square, R=[x1], W=[junk, ssx], accum_out=ssx[:])
            rstd(ssx, 1.0 / D)
            tmp = k.tile("htmp", [128, D], F32); hb = k.tile("hb", [128, D], BF16)
            k.stt("dve", tmp[:], x1[:], ssx[:, 0:1], m["A2"][v][:], ALU.mult, ALU.mult, R=[x1, ssx, m["A2"][v]], W=[tmp])
            k.tt("pool", hb[:], tmp[:], m["B2"][v][:], ALU.add, R=[tmp, m["B2"][v]], W=[hb])
            h2T = k.tile("h2T", [128, 8, 128], BF16)
            transposes([hb[:, i * 128:(i + 1) * 128] for i in range(8)], 7, h2T, R=[hb])
            k.dma(h2Ts[t], h2T[:], R=[h2T])
        k.pop()
        k.reorder = False

    def phase_mlp(l, xdst):
        k.push()
        wu = k.sb([128, 8, 4096], BF16, "wu"); wload(wu, Wt["w_up"][l], nchunk=8)
        wd = k.sb([128, 32, 1024], BF16, "wd"); wload(wd, Wt["w_down"][l], nchunk=4)
        m = mod_tiles(l, [("G2", 5)])
        uT = k.sb([128, 32, 512], BF16, "uT"); uTb = [k.buf() for _ in range(32)]
        groups = []
        for s in seqs:
            ts_ = list(range(s["t0"], s["t0"] + s["nt"]))
            for i in range(0, len(ts_), 4):
                groups.append((s["var"], ts_[i:i + 4]))
        for gi, (v, tiles) in enumerate(groups):
            ncol = len(tiles) * 128
            h2 = k.tile("h2g", [128, 8, 512], BF16)
            for j, t in enumerate(tiles):
                k.dma(h2[:, :, j * 128:(j + 1) * 128], h2Ts[t], W=[h2])
            for f in range(32):
                bank = PS[f % 2]
                for kc in range(8):
                    k.mm(bank[:, 0:ncol], wu[:, kc, f * 128:(f + 1) * 128], h2[:, kc, 0:ncol], kc == 0, kc == 7, R=[wu, h2], W=[bank])
                r = k.tile(f"relu{f % 2}", [128, 512], F32)
                k.act(r[:, 0:ncol], bank[:, 0:ncol], AF.Relu, R=[bank], W=[r])
                k.tt("dve" if f % 2 else "pool", uT[:, f, 0:ncol], r[:, 0:ncol], r[:, 0:ncol], ALU.mult, R=[r], W=[uTb[f]])
            for j, t in enumerate(tiles):
                rows = slice(t * 128, (t + 1) * 128)
                x1 = k.tile(f"mx{j % 2}", [128, D], F32); xo = k.tile("mxo", [128, D], F32)
                k.dma(x1[:], xa[rows, :], W=[x1])
                for nn in range(2):
                    cs = slice(nn * 512, (nn + 1) * 512)
                    bank = PS[2 + (j * 2 + nn) % 4]
                    for f in range(32):
                        k.mm(bank[:], uT[:, f, j * 128:(j + 1) * 128], wd[:, f, cs], f == 0, f == 31, R=[uTb[f], wd], W=[bank])
                    tmp = k.tile(f"mtmp{nn}", [128, 512], F32)
                    k.tt("dve", tmp[:], bank[:], m["G2"][v][:, cs], ALU.mult, R=[bank, m["G2"][v]], W=[tmp])
                    k.tt("pool", xo[:, cs], x1[:, cs], tmp[:], ALU.add, R=[x1, tmp], W=[xo])
                k.dma(xdst[rows, :], xo[:], R=[xo])
        k.pop()

    nph = 0
    for l in range(L):
        lam_init = 0.8 - 0.6 * math.exp(-0.3 * l)
        phases = [lambda: layer_consts(l), lambda: phase_mod(l), lambda: phase_A(l, x_all if l == 0 else xb),
                  lambda: phase_ret(l), lambda: phase_attn(l, lam_init), lambda: phase_merge(l, x_all if l == 0 else xb),
                  lambda: phase_mlp(l, xb if l < L - 1 else y_all)]
        for ph in phases:
            if nph < stop:
                ph(); k.barrier()
            nph += 1
    k.emit()
    return k


def _rope_table(NS, NP):
    n_rows = NS // GRID_W
    row = np.repeat(np.arange(n_rows, dtype=np.float32), GRID_W)
    col = np.tile(np.arange(GRID_W, dtype=np.float32), n_rows)
    inv = (10000.0 ** (-np.arange(0, 32, 2, dtype=np.float32) / 32)).astype(np.float32)
    ar = (row[:, None] * inv[None, :]).astype(np.float32); ac = (col[:, None] * inv[None, :]).astype(np.float32)
    cr, sr, cc, sc = np.cos(ar), np.sin(ar), np.cos(ac), np.sin(ac)
    tab = np.concatenate([cr, cr, cc, cc, -sr, sr, -sc, sc], axis=1).astype(np.float32)
    ptab = np.concatenate([np.ones((2 * NP, 64), np.float32), np.zeros((2 * NP, 64), np.float32)], axis=1)
    return np.ascontiguousarray(np.concatenate([tab, ptab], axis=0))


def _ret_consts():
    C = 128
    j = np.arange(C, dtype=np.float32)[:, None]; i = np.arange(C, dtype=np.float32)[None, :]
    relf = np.maximum(i - j, 0.0); maskf = (i >= j).astype(np.float32)
    relb = np.maximum(j - i, 0.0); maskb = (j > i).astype(np.float32)
    retc = np.stack([relf, maskf, relb, maskb]).astype(np.float32)
    p = np.arange(C, dtype=np.float32)
    retcol = np.stack([p + 1.0, C - p, C - 1.0 - p, p], axis=1).astype(np.float32)
    return np.ascontiguousarray(retc), np.ascontiguousarray(retcol)


_CACHE = {}


def _run(inputs, NS, NP, PAST, dbg=(), stop=99):
    key = (NS, NP, PAST, tuple(dbg), stop)
    if key not in _CACHE:
        _CACHE[key] = build(NS, NP, PAST, dbg, stop)
    kb = _CACHE[key]
    f = lambda a: np.ascontiguousarray(np.asarray(a, dtype=np.float32))
    rope_tab = _rope_table(NS, NP); retc, retcol = _ret_consts()
    xp = f(inputs["x_prompt"]); xs = f(inputs["x_sample"])
    L = DEPTH
    in_maps = []
    for c in range(8):
        m = {
            "x_all": np.ascontiguousarray(np.concatenate([xs[c], xp[2 * c], xp[2 * c + 1]], axis=0)),
            "cvec": np.ascontiguousarray(np.stack([f(inputs["c"])[c], f(inputs["c_ctx"])])),
            "cdk": f(inputs["cache_diff_k"])[c].reshape(L, PAST, 512),
            "cdv": f(inputs["cache_diff_v"])[c].reshape(L, PAST, 512),
            "cckv": f(inputs["cache_mla_ckv"])[c], "ckr": f(inputs["cache_mla_krope"])[c],
            "sret": f(inputs["state_ret"])[c],
            "rope_tab": rope_tab, "retc": retc, "retcol": retcol,
        }
        for n, _ in W_SPECS:
            m[n] = f(inputs[n])
        in_maps.append({a: np.ascontiguousarray(b) for a, b in m.items()})
    res = run_bass_kernel_spmd(kb.nc, in_maps, core_ids=list(range(8)))
    return res.results


def kernel(**inputs):
    NS = inputs["x_sample"].shape[1]; NP = inputs["x_prompt"].shape[1]; PAST = inputs["cache_diff_k"].shape[2]
    B = inputs["x_prompt"].shape[0]
    r = _run(inputs, NS, NP, PAST)
    L = DEPTH
    y_prompt = np.zeros((B, NP, D), np.float32); y_sample = np.zeros((8, NS, D), np.float32)
    ndk = np.zeros((B, L, NP, 8, 64), np.float32); ndv = np.zeros((B, L, NP, 4, 128), np.float32)
    nckv = np.zeros((B, L, NP, 256), np.float32); nkr = np.zeros((B, L, NP, 64), np.float32)
    nsr = np.zeros((B, L, 2, 4, 64, 128), np.float32)
    for c in range(8):
        ya = r[c]["y_all"]
        y_sample[c] = ya[:NS]
        for p in range(2):
            b = 2 * c + p
            y_prompt[b] = ya[NS + p * NP: NS + (p + 1) * NP]
            ndk[b] = r[c]["ndk"][p].reshape(L, NP, 8, 64); ndv[b] = r[c]["ndv"][p].reshape(L, NP, 4, 128)
            nckv[b] = r[c]["nckv"][p]; nkr[b] = r[c]["nkr"][p]; nsr[b] = r[c]["nsr"][p]
    return (y_prompt, y_sample, ndk, ndv, nckv, nkr, nsr)
```

```python
import math
import os
import numpy as np
import concourse.bass as bass
import concourse.mybir as mybir
from concourse.bass_utils import run_bass_kernel_spmd

F32 = mybir.dt.float32
BF16 = mybir.dt.bfloat16
AF = mybir.ActivationFunctionType
ALU = mybir.AluOpType
AX = mybir.AxisListType

D = 1024
DEPTH = 2
EPS = 1e-6
D_IN = 6848
NPH_A = 3776
GRID_W = 64


class Buf:
    __slots__ = ("name", "w", "r")

    def __init__(self, name):
        self.name = name
        self.w = None
        self.r = []


class Op:
    __slots__ = ("eng", "fn", "deps", "dma", "sig", "need", "selfsig", "odeps", "reorder")

    def __init__(self, eng, fn, dma):
        self.eng = eng
        self.fn = fn
        self.deps = []
        self.dma = dma
        self.sig = None
        self.need = False
        self.selfsig = False
        self.odeps = []
        self.reorder = False


class T:
    def __init__(self, h, buf):
        self.h = h
        self.b = buf
        self.tok = None

    def __getitem__(self, k):
        return self.h[k]


ENGS = ("pe", "act", "dve", "pool", "sp")
NDMASEM = 16


class KB:
    def __init__(self):
        self.nc = bass.Bass("TRN2", target_bir_lowering=False)
        nc = self.nc
        self.e = {"pe": nc.tensor, "act": nc.scalar, "dve": nc.vector, "pool": nc.gpsimd, "sp": nc.sync}
        self.ops = {k: [] for k in ENGS}
        self.allops = []
        self.csem = {k: nc.alloc_semaphore("c_" + k) for k in ENGS}
        self.dsem = {q: [nc.alloc_semaphore(f"d_{q}{i}") for i in range(NDMASEM)] for q in ("sp", "pool", "act")}
        self.dcnt = {q: 0 for q in self.dsem}
        self.dlast = {}
        self.bar = None
        self.bar_seen = {k: None for k in ENGS}
        self.sb_off = 16512
        self.sb_mark = []
        self.nbuf = 0
        self.names = 0
        self.cache = {}
        self.reorder = False

    def buf(self, name="b"):
        self.nbuf += 1
        return Buf(f"{name}{self.nbuf}")

    def sb(self, shape, dt, name="t"):
        self.names += 1
        nbytes = int(np.prod(shape[1:])) * (4 if dt == F32 else 2)
        nbytes = (nbytes + 63) // 64 * 64
        h = self.nc.alloc_sbuf_tensor_at(f"{name}_{self.names}", list(shape), dt, offset=self.sb_off)
        self.sb_off += nbytes
        assert self.sb_off <= 229344, f"SBUF overflow {self.sb_off}"
        return T(h, self.buf(name))

    def push(self):
        self.sb_mark.append((self.sb_off, dict(self.cache)))

    def pop(self):
        self.sb_off, self.cache = self.sb_mark.pop()

    def tile(self, name, shape, dt):
        t = self.cache.get(name)
        if t is None:
            t = self.sb(shape, dt, name)
            self.cache[name] = t
        return t

    def dram(self, name, shape, dt, kind="Internal"):
        return self.nc.dram_tensor(name, list(shape), dt, kind=kind).ap()

    def op(self, eng, fn, R=(), W=(), dma=False):
        o = Op(eng, fn, dma)
        o.reorder = self.reorder
        deps = {}
        toks = [t.tok for t in R if isinstance(t, T) and t.tok is not None]
        if toks:
            W = list(W) + toks
        for t in R:
            b = t.b if isinstance(t, T) else t
            if b.w is not None:
                deps[id(b.w)] = (b.w, "raw")
        for t in W:
            b = t.b if isinstance(t, T) else t
            if b.w is not None:
                deps[id(b.w)] = (b.w, "waw")
            for r in b.r:
                if id(r) not in deps:
                    deps[id(r)] = (r, "war")
        for d, kind in deps.values():
            if d is o:
                continue
            if not d.dma and not dma and d.eng == eng:
                if eng == "pe":
                    o.odeps.append(d)
                    continue
            o.deps.append(d)
        if self.bar is not None and self.bar_seen[eng] is not self.bar:
            o.deps.append(self.bar)
            self.bar_seen[eng] = self.bar
        if dma:
            q = eng
            i = self.dcnt[q] % (4 if q == "pool" else NDMASEM)
            self.dcnt[q] += 1
            s = self.dsem[q][i]
            prev = self.dlast.get((q, i))
            if prev is not None:
                o.deps.append(prev[0])
                cnt = prev[1] + 1
            else:
                cnt = 1
            o.sig = (s, 16 * cnt)
            self.dlast[(q, i)] = (o, cnt)
            o.need = True
        for d in o.deps:
            d.need = True
        for t in R:
            b = t.b if isinstance(t, T) else t
            b.r.append(o)
        for t in W:
            b = t.b if isinstance(t, T) else t
            b.w = o
            b.r = []
        self.ops[eng].append(o)
        self.allops.append(o)
        return o

    def barrier(self):
        deps = []
        for k in ENGS:
            for o in reversed(self.ops[k]):
                if not o.dma:
                    deps.append(o)
                    break
        for (q, i), (o, c) in self.dlast.items():
            deps.append(o)
        b = Op("sp", lambda e: e.sem_inc(self.csem["sp"], 1), False)
        b.deps = [d for d in deps]
        for d in deps:
            d.need = True
        b.need = True
        b.selfsig = True
        self.ops["sp"].append(b)
        self.allops.append(b)
        self.bar = b

    def dma(self, out, in_, R=(), W=(), q="sp", **kw):
        return self.op(q, lambda e: e.dma_start(out=out, in_=in_, **kw), R, W, dma=True)

    def mm(self, out, lhsT, rhs, start, stop, R=(), W=()):
        return self.op("pe", lambda e: e.matmul(out, lhsT=lhsT, rhs=rhs, start=start, stop=stop), R, W)

    def tr(self, out, in_, ident, R=(), W=()):
        return self.op("pe", lambda e: e.transpose(out=out, in_=in_, identity=ident), R, W)

    def act(self, out, in_, func, R=(), W=(), **kw):
        return self.op("act", lambda e: e.activation(out=out, in_=in_, func=func, **kw), R, W)

    def tt(self, eng, out, in0, in1, op, R=(), W=()):
        return self.op(eng, lambda e: e.tensor_tensor(out=out, in0=in0, in1=in1, op=op), R, W)

    def ts(self, eng, out, in0, s1, s2, op0, op1=None, R=(), W=()):
        if op1 is None:
            return self.op(eng, lambda e: e.tensor_scalar(out=out, in0=in0, scalar1=s1, scalar2=None, op0=op0), R, W)
        return self.op(eng, lambda e: e.tensor_scalar(out=out, in0=in0, scalar1=s1, scalar2=s2, op0=op0, op1=op1), R, W)

    def stt(self, eng, out, in0, scalar, in1, op0, op1, R=(), W=()):
        return self.op(eng, lambda e: e.scalar_tensor_tensor(out=out, in0=in0, scalar=scalar, in1=in1, op0=op0, op1=op1), R, W)

    def cp(self, eng, out, in_, R=(), W=()):
        if eng == "act":
            return self.op("act", lambda e: e.activation(out=out, in_=in_, func=AF.Identity), R, W)
        return self.op(eng, lambda e: e.tensor_copy(out=out, in_=in_), R, W)

    def red(self, eng, out, in_, R=(), W=()):
        return self.op(eng, lambda e: e.tensor_reduce(out=out, in_=in_, axis=AX.X, op=ALU.add), R, W)

    def memset(self, eng, ap, val, W=()):
        return self.op(eng, lambda e: e.memset(ap, val), (), W)

    def sched_seg(self, ops):
        idx = {id(o): i for i, o in enumerate(ops)}
        fin = {}
        per = {kx: [o for o in ops if o.eng == kx] for kx in ENGS}
        free = {kx: 0.0 for kx in ENGS}
        DUR = {"pe": 0.3, "act": 0.6, "dve": 0.5, "pool": 0.6, "sp": 0.05}
        out = []
        Wn = 96
        nleft = len(ops)
        while nleft:
            best = None
            for kx in ENGS:
                lst = per[kx]
                fk = free[kx]
                for j in range(min(Wn, len(lst))):
                    o = lst[j]
                    t = fk
                    ok = True
                    for d in o.deps:
                        f = fin.get(id(d))
                        if f is None:
                            if id(d) in idx:
                                ok = False
                                break
                            continue
                        f += 0.2
                        if f > t:
                            t = f
                    if ok:
                        for d in o.odeps:
                            f = fin.get(id(d))
                            if f is None:
                                if id(d) in idx:
                                    ok = False
                                    break
                                continue
                            if f > t:
                                t = f
                    if not ok:
                        continue
                    key = (t, idx[id(o)])
                    if best is None or key < best[0]:
                        best = (key, kx, j, o, t)
                    if t <= fk:
                        break
            assert best is not None, "scheduler stuck"
            _, kx, j, o, t = best
            per[kx].pop(j)
            if o.dma:
                occ = 0.8 if kx == "pool" else 0.05
                fin[id(o)] = t + occ + 2.5
            else:
                occ = DUR[kx]
                fin[id(o)] = t + occ
            free[kx] = t + occ
            out.append(o)
            nleft -= 1
        return out

    def finalize(self):
        segs = []
        cur = []
        for o in self.allops:
            if o.selfsig:
                segs.append((cur, o)); cur = []
            else:
                cur.append(o)
        if cur:
            segs.append((cur, None))
        bars = set(id(b) for _, b in segs if b is not None)
        final = {kx: [] for kx in ENGS}
        lastdma = {}
        prevbar = None
        order_all = []
        for ops, bar in segs:
            for o in ops:
                o.deps = [d for d in o.deps if id(d) not in bars]
            order = self.sched_seg(ops)
            seen = set()
            for o in order:
                if o.eng not in seen:
                    seen.add(o.eng)
                    if prevbar is not None:
                        o.deps.append(prevbar)
                final[o.eng].append(o)
                order_all.append(o)
                if o.dma:
                    lastdma[o.sig[0].num] = o
            if bar is not None:
                deps = []
                for kx in ENGS:
                    for o in reversed(final[kx]):
                        if not o.dma:
                            deps.append(o)
                            break
                deps += list(lastdma.values())
                bar.deps = deps
                for d in deps:
                    d.need = True
                final["sp"].append(bar)
                order_all.append(bar)
                prevbar = bar
        self.ops = final
        self.allops = order_all

    def emit(self):
        self.finalize()
        cnt = {k: 0 for k in ENGS}
        for k_ in ENGS:
            for o in self.ops[k_]:
                if o.dma:
                    continue
                if o.need:
                    cnt[o.eng] += 1
                    o.sig = (self.csem[o.eng], cnt[o.eng])
        for k in ENGS:
            eng = self.e[k]
            waited = {}
            for o in self.ops[k]:
                for d in o.deps:
                    s, v = d.sig
                    key = s.num
                    if waited.get(key, 0) >= v:
                        continue
                    waited[key] = v
                    eng.wait_ge(s, v)
                ins = o.fn(eng)
                if o.dma:
                    ins.then_inc(o.sig[0], 16)
                elif o.need and not o.selfsig:
                    ins.then_inc(o.sig[0], 1)


def rstd_ops(k, ss, n_inv, mh, R=(), W=()):
    k.ts("pool", ss, ss, n_inv, EPS, ALU.mult, ALU.add, R=R, W=W)
    k.tt("pool", ss, ss, mh, ALU.pow, R=list(R) + list(W), W=W)


W_SPECS = [
    ("w_mod", [DEPTH, D, 6 * D]), ("b_mod", [DEPTH, 6 * D]), ("norm1", [DEPTH, D]), ("norm2", [DEPTH, D]),
    ("w_in", [DEPTH, D, D_IN]), ("diff_qn", [DEPTH, 64]), ("diff_kn", [DEPTH, 64]), ("diff_lambda", [DEPTH, 4, 64]),
    ("diff_subln", [DEPTH, 128]), ("ret_decay", [DEPTH, 2, 4]), ("ret_gn", [DEPTH, 128]),
    ("mla_qa_norm", [DEPTH, 384]), ("w_mla_qb", [DEPTH, 384, 768]), ("mla_kva_norm", [DEPTH, 256]),
    ("w_mla_kvb", [DEPTH, 256, 1024]), ("mla_qn", [DEPTH, 192]), ("mla_kn", [DEPTH, 192]),
    ("w_branch", [DEPTH, 3, 512, D]), ("w_out", [DEPTH, D, D]), ("w_up", [DEPTH, D, 4 * D]), ("w_down", [DEPTH, 4 * D, D]),
]


def build(NS, NP, PAST, dbg=(), stop=99):
    k = KB()
    nc = k.nc
    NT = NS + 2 * NP
    NTT = NT // 128
    NK = NT + PAST
    L = DEPTH
    ein = lambda n, s: k.dram(n, s, F32, kind="ExternalInput")
    eout = lambda n, s: k.dram(n, s, F32, kind="ExternalOutput")
    x_all = ein("x_all", [NT, D]); cvec = ein("cvec", [2, D])
    cdk = ein("cdk", [L, PAST, 512]); cdv = ein("cdv", [L, PAST, 512])
    cckv = ein("cckv", [L, PAST, 256]); ckr = ein("ckr", [L, PAST, 64]); sret = ein("sret", [L, 2, 4, 64, 128])
    Wt = {n: ein(n, s) for n, s in W_SPECS}
    rope_tab = ein("rope_tab", [NT, 128]); retc = ein("retc", [4, 128, 128]); retcol = ein("retcol", [128, 4])
    y_all = eout("y_all", [NT, D])
    ndk = eout("ndk", [2, L, NP, 512]); ndv = eout("ndv", [2, L, NP, 512])
    nckv = eout("nckv", [2, L, NP, 256]); nkr = eout("nkr", [2, L, NP, 64]); nsr = eout("nsr", [2, L, 2, 4, 64, 128])

    def scr(n, s, dt):
        return k.dram(n, s, dt, kind=("ExternalOutput" if n in dbg else "Internal"))
    modrow = scr("modrow", [2, 6 * D], F32)
    hTs = scr("hTs", [NTT, 128, 8, 128], BF16); h2Ts = scr("h2Ts", [NTT, 128, 8, 128], BF16)
    dqT = scr("dqT", [4, 128, NT], BF16); dkT = scr("dkT", [4, 128, NK], BF16); dvs = scr("dvs", [NK, 512], BF16)
    rqT = scr("rqT", [2, 128, NT], BF16); rkT = scr("rkT", [2, 128, NT], BF16); QQ = scr("QQ", [4, 128, NT], BF16)
    kks = scr("kks", [NT, 512], BF16); rvs = scr("rvs", [NT, 512], BF16); rgs = scr("rgs", [NT, 512], F32)
    mqTn = scr("mqTn", [4, 128, NT], BF16); mqTr = scr("mqTr", [2, 128, NT], BF16)
    mkTn = scr("mkTn", [4, 128, NK], BF16); mkTr = scr("mkTr", [2, 128, NK], BF16); mvs = scr("mvs", [NK, 512], BF16)
    oT = scr("oT", [3, 4, 128, NT], BF16)
    xa = scr("xa", [NT, D], F32); xb = scr("xb", [NT, D], F32)

    seqs = [dict(t0=0, nt=NS // 128, ctx=True, var=0, pi=None),
            dict(t0=NS // 128, nt=NP // 128, ctx=False, var=1, pi=0),
            dict(t0=(NS + NP) // 128, nt=NP // 128, ctx=False, var=1, pi=1)]

    def seq_of(t):
        for s in seqs:
            if s["t0"] <= t < s["t0"] + s["nt"]:
                return s

    PS = []
    for i in range(8):
        h = nc.alloc_psum_tensor(f"ps{i}", [128, 512], F32)
        PS.append(T(h, k.buf("ps")))
        PS[-1].tok = k.buf("pstok")
    psb = lambda i: PS[i][:].bitcast(BF16)

    identf = k.sb([128, 128], F32, "identf"); ident = k.sb([128, 128], BF16, "ident")
    mh = k.sb([128, 8], F32, "mh")
    k.memset("pool", identf[:], 0.0, W=[identf])
    k.op("pool", lambda e: e.affine_select(out=identf[:], in_=identf[:], pattern=[[-1, 128]], compare_op=ALU.not_equal,
                                           fill=1.0, base=0, channel_multiplier=1), R=[identf], W=[identf])
    k.cp("dve", ident[:], identf[:], R=[identf], W=[ident])
    k.memset("pool", mh[:], -0.5, W=[mh])
    rc = k.sb([128, 4, 128], F32, "retc")
    k.dma(rc[:], retc.rearrange("c p i -> p c i"), W=[rc])
    rcol = k.sb([128, 4], F32, "retcol")
    k.dma(rcol[:], retcol, W=[rcol])
    MT = k.sb([128, 4, 128], F32, "MT"); qdc = k.sb([128, 4, 2], F32, "qdc"); kdc = k.sb([128, 4, 2], F32, "kdc")
    decC = k.sb([128, 4, 128], F32, "decC"); lg = k.sb([128, 2, 4], F32, "lg"); neglam = k.sb([128, 1], F32, "neglam")

    def rstd(ss, n_inv, extraR=()):
        k.ts("pool", ss[:], ss[:], n_inv, EPS, ALU.mult, ALU.add, R=[ss] + list(extraR), W=[ss])
        k.tt("pool", ss[:], ss[:], mh[:, 0:ss.h.shape[1]] if len(ss.h.shape) == 2 else mh[:], ALU.pow, R=[ss, mh], W=[ss])

    def wload(dst, src_ap, nchunk=1):
        n = src_ap.shape[-1]
        step = max(256, ((n + nchunk - 1) // nchunk + 255) // 256 * 256)
        for c0 in range(0, n, step):
            c1 = min(n, c0 + step)
            k.dma(dst[:, :, c0:c1], src_ap[:, c0:c1].rearrange("(kc p) n -> p kc n", p=128), W=[dst], q="pool")

    def bload(dst_ap, src_ap, W):
        k.dma(dst_ap, src_ap.partition_broadcast(128), W=W)

    def transposes(src_aps, bank, dst, R, n_out_part=128):
        pb = psb(bank)
        n = len(src_aps)
        for i, a in enumerate(src_aps):
            k.tr(pb[:, i * 128:(i + 1) * 128], a, ident[:], R=list(R) + [ident], W=[PS[bank]])
        k.cp("dve", dst[:].rearrange("p n c -> p (n c)"), pb[:, 0:n * 128], R=[PS[bank]], W=[dst])

    def layer_consts(l):
        k.push()
        lam_init = 0.8 - 0.6 * math.exp(-0.3 * l)
        dl = k.sb([128, 4, 64], F32); pr = k.sb([128, 2, 64], F32); sm = k.sb([128, 2], F32)
        bload(dl[:].rearrange("p a d -> p (a d)"), Wt["diff_lambda"][l].rearrange("a d -> (a d)"), W=[dl])
        dl4 = dl[:].rearrange("p (a b) d -> p a b d", b=2)
        k.tt("dve", pr[:], dl4[:, :, 0, :], dl4[:, :, 1, :], ALU.mult, R=[dl], W=[pr])
        k.red("dve", sm[:], pr[:], R=[pr], W=[sm])
        k.act(sm[:], sm[:], AF.Exp, R=[sm], W=[sm])
        k.stt("dve", neglam[:], sm[:, 1:2], -lam_init, sm[:, 0:1], ALU.add, ALU.subtract, R=[sm], W=[neglam])
        bload(lg[:].rearrange("p a h -> p (a h)"), Wt["ret_decay"][l].rearrange("a h -> (a h)"), W=[lg])
        k.act(lg[:], lg[:], AF.Exp, R=[lg], W=[lg], scale=-1.0)
        k.act(lg[:], lg[:], AF.Ln, R=[lg], W=[lg], bias=1.0)
        k.ts("dve", lg[:], lg[:], -1.0, None, ALU.mult, R=[lg], W=[lg])
        tmp = k.sb([128, 4, 128], F32); tmp2 = k.sb([128, 4, 128], F32)
        for h in range(4):
            k.ts("dve", tmp[:, h, :], rc[:, 0, :], lg[:, 0, h:h + 1], None, ALU.mult, R=[rc, lg], W=[tmp])
            k.ts("dve", tmp2[:, h, :], rc[:, 2, :], lg[:, 1, h:h + 1], None, ALU.mult, R=[rc, lg], W=[tmp2])
        k.act(tmp[:], tmp[:], AF.Exp, R=[tmp], W=[tmp])
        k.act(tmp2[:], tmp2[:], AF.Exp, R=[tmp2], W=[tmp2])
        k.tt("dve", tmp[:], tmp[:], rc[:, 1:2, :].to_broadcast([128, 4, 128]), ALU.mult, R=[tmp, rc], W=[tmp])
        k.tt("dve", tmp2[:], tmp2[:], rc[:, 3:4, :].to_broadcast([128, 4, 128]), ALU.mult, R=[tmp2, rc], W=[tmp2])
        k.tt("dve", MT[:], tmp[:], tmp2[:], ALU.add, R=[tmp, tmp2], W=[MT])
        for d in range(2):
            k.ts("dve", qdc[:, :, d], lg[:, d, :], rcol[:, d:d + 1], None, ALU.mult, R=[lg, rcol], W=[qdc])
            k.ts("dve", kdc[:, :, d], lg[:, d, :], rcol[:, 2 + d:3 + d], None, ALU.mult, R=[lg, rcol], W=[kdc])
        k.act(qdc[:], qdc[:], AF.Exp, R=[qdc], W=[qdc])
        k.act(kdc[:], kdc[:], AF.Exp, R=[kdc], W=[kdc])
        dc = k.sb([128, 4], F32)
        k.act(dc[0:64, :], lg[0:64, 0, :], AF.Exp, R=[lg], W=[dc], scale=128.0)
        k.act(dc[64:128, :], lg[64:128, 1, :], AF.Exp, R=[lg], W=[dc], scale=128.0)
        k.cp("dve", decC[:], dc[:].unsqueeze(2).to_broadcast([128, 4, 128]), R=[dc], W=[decC])
        k.pop()
        return lam_init

    def phase_mod(l):
        k.push()
        cT = k.sb([128, 2, 8], F32); sT = k.sb([128, 2, 8], F32); rep = k.sb([128, 16, 128], BF16)
        for v in range(2):
            k.dma(cT[:, v, :], cvec[v].rearrange("(kc p) -> p kc", p=128), W=[cT], allow_slow_non_contiguous=True)
        k.act(sT[:], cT[:], AF.Silu, R=[cT], W=[sT])
        k.cp("dve", rep[:], sT[:].rearrange("p v c -> p (v c)").unsqueeze(2).to_broadcast([128, 16, 128]), R=[sT], W=[rep])
        wm = [k.sb([128, 8, 512], BF16) for _ in range(2)]
        bm = [k.sb([128, 512], F32) for _ in range(2)]
        res = [k.sb([128, 512], F32) for _ in range(2)]
        for n in range(12):
            w_, b_ = wm[n % 2], bm[n % 2]
            wload(w_, Wt["w_mod"][l][:, n * 512:(n + 1) * 512])
            bload(b_[:], Wt["b_mod"][l][n * 512:(n + 1) * 512], W=[b_])
            for v in range(2):
                bank = PS[v]
                for kc in range(8):
                    k.mm(bank[:], rep[:, v * 8 + kc, :], w_[:, kc, :], kc == 0, kc == 7, R=[rep, w_], W=[bank])
                r_ = res[v]
                k.tt("dve", r_[:], bank[:], b_[:], ALU.add, R=[bank, b_], W=[r_])
                k.dma(modrow[v:v + 1, n * 512:(n + 1) * 512], r_[0:1, :], R=[r_])
        k.pop()

    def rope_mul(zv, H, Gap, G, rt, out3, tmpA, tmpB, R, W, scale_bcast=None):
        gc = k.tile("rope_gc", [128, 64], F32); gs = k.tile("rope_gs", [128, 64], F32)
        k.tt("dve", gc[:], Gap, rt[:, 0:64], ALU.mult, R=[G, rt], W=[gc])
        g4 = Gap.rearrange("p (r h x) -> p r h x", r=2, h=2)
        s4 = rt[:, 64:128].rearrange("p (r h x) -> p r h x", r=2, h=2)
        gs4 = gs[:].rearrange("p (r h x) -> p r h x", r=2, h=2)
        k.tt("dve", gs4[:, :, 0, :], g4[:, :, 1, :], s4[:, :, 0, :], ALU.mult, R=[G, rt], W=[gs])
        k.tt("dve", gs4[:, :, 1, :], g4[:, :, 0, :], s4[:, :, 1, :], ALU.mult, R=[G, rt, gs], W=[gs])
        k.tt("dve", tmpA[:, 0:H, :], zv, gc[:].unsqueeze(1).to_broadcast([128, H, 64]), ALU.mult, R=list(R) + [gc], W=[tmpA])
        z5 = zv.rearrange("p h (r f x) -> p h r f x", r=2, f=2)
        o5 = tmpB[:, 0:H, :].rearrange("p h (r f x) -> p h r f x", r=2, f=2)
        g4 = gs[:].rearrange("p (r f x) -> p r f x", r=2, f=2)
        for hf in range(2):
            k.tt("dve", o5[:, :, :, hf, :], z5[:, :, :, 1 - hf, :], g4[:, :, hf, :].unsqueeze(1).to_broadcast([128, H, 2, 16]),
                 ALU.mult, R=list(R) + [gs], W=[tmpB])
        if scale_bcast is None:
            k.tt("pool", out3, tmpA[:, 0:H, :], tmpB[:, 0:H, :], ALU.add, R=[tmpA, tmpB], W=W)
        else:
            k.tt("pool", tmpA[:, 0:H, :], tmpA[:, 0:H, :], tmpB[:, 0:H, :], ALU.add, R=[tmpA, tmpB], W=[tmpA])
            k.tt("dve", out3, tmpA[:, 0:H, :], scale_bcast, ALU.mult, R=[tmpA] + list(R), W=W)

    def store_T(src, nblk, bank, dstT_ap_fn, R):
        tT = k.tile(f"stT{bank}_{nblk}", [128, nblk, 128], BF16)
        transposes([src[:, i * 128:(i + 1) * 128] for i in range(nblk)], bank, tT, R=[src])
        k.dma(dstT_ap_fn(), tT[:], R=[tT])

    def phase_A(l, xsrc):
        k.push()
        k.reorder = True
        wA = k.sb([128, 8, NPH_A], BF16, "wA"); wload(wA, Wt["w_in"][l][:, 0:NPH_A], nchunk=8)
        wqb = k.sb([128, 3, 768], BF16, "wqb"); wload(wqb, Wt["w_mla_qb"][l])
        wkvb = k.sb([128, 2, 1024], BF16, "wkvb"); wload(wkvb, Wt["w_mla_kvb"][l])
        A1, B1 = [], []
        n1 = k.sb([128, D], F32, "n1"); bload(n1[:], Wt["norm1"][l], W=[n1])
        for v in range(2):
            a = k.sb([128, D], F32, "A1"); b = k.sb([128, D], F32, "B1")
            bload(a[:], modrow[v, D:2 * D], W=[a]); bload(b[:], modrow[v, 0:D], W=[b])
            k.stt("dve", a[:], a[:], 1.0, n1[:], ALU.add, ALU.mult, R=[a, n1], W=[a])
            A1.append(a); B1.append(b)
        gq = k.sb([128, 64], F32, "gq"); bload(gq[:], Wt["diff_qn"][l], W=[gq])
        gk = k.sb([128, 64], F32, "gk"); bload(gk[:], Wt["diff_kn"][l], W=[gk])
        ones64 = k.sb([128, 64], F32, "ones"); k.memset("pool", ones64[:], 1.0, W=[ones64])
        eighth = k.sb([128, 64], F32, "eighth"); k.memset("pool", eighth[:], 0.125, W=[eighth])
        gqa = k.sb([128, 384], F32, "gqa"); bload(gqa[:], Wt["mla_qa_norm"][l], W=[gqa])
        gkva = k.sb([128, 256], F32, "gkva"); bload(gkva[:], Wt["mla_kva_norm"][l], W=[gkva])
        gmqn = k.sb([128, 192], F32, "gmqn"); bload(gmqn[:], Wt["mla_qn"][l], W=[gmqn])
        gmkn = k.sb([128, 192], F32, "gmkn"); bload(gmkn[:], Wt["mla_kn"][l], W=[gmkn])
        rt1 = k.sb([128, 128], F32, "rt1")
        k.memset("pool", rt1[:, 0:64], 1.0, W=[rt1]); k.memset("pool", rt1[:, 64:128], 0.0, W=[rt1])
        tA = k.sb([128, 8, 64], F32, "tA"); tB = k.sb([128, 8, 64], F32, "tB")

        def mla_kv(ckvb, krf, rt, col):
            ckvT = k.tile("ckvT", [128, 2, 128], BF16)
            transposes([ckvb[:, 0:128], ckvb[:, 128:256]], 5, ckvT, R=[ckvb])
            for c in range(2):
                for kc in range(2):
                    k.mm(PS[6 + c][:], ckvT[:, kc, :], wkvb[:, kc, c * 512:(c + 1) * 512], kc == 0, kc == 1, R=[ckvT, wkvb], W=[PS[6 + c]])
            sqk = k.tile("sqk", [128, 4, 128], F32); ssn = k.tile("ssn", [128, 4], F32)
            kv4 = [PS[6 + c][:].rearrange("p (h x d) -> p h x d", h=2, x=2) for c in range(2)]
            for c in range(2):
                k.act(sqk[:, 2 * c:2 * c + 2, :], kv4[c][:, :, 0, :], AF.Square, R=[PS[6 + c]], W=[sqk])
            k.red("dve", ssn[:], sqk[:], R=[sqk], W=[ssn])
            skr = k.tile("skr", [128, 1], F32); junk3 = k.tile("junk3", [128, 64], F32)
            k.act(junk3[:], krf[:], AF.Square, R=[krf], W=[junk3, skr], accum_out=skr[:])
            k.ts("dve", ssn[:], ssn[:], skr[:, 0:1], None, ALU.add, R=[ssn, skr], W=[ssn])
            rstd(ssn, 1.0 / 192)
            krg = k.tile("krg", [128, 1, 64], F32)
            rope_mul(krf[:].unsqueeze(1), 1, gmkn[:, 128:192], gmkn, rt, krg[:], tA, tB, R=[krf], W=[krg])
            mkn = k.tile("mkn", [128, 4, 128], BF16); mkr = k.tile("mkr", [128, 4, 64], BF16); mvb = k.tile("mvb", [128, 4, 128], BF16)
            for c in range(2):
                tn = k.tile("tn", [128, 2, 128], F32)
                k.tt("dve", tn[:], kv4[c][:, :, 0, :], gmkn[:, 0:128].unsqueeze(1).to_broadcast([128, 2, 128]), ALU.mult, R=[PS[6 + c], gmkn], W=[tn])
                k.tt("dve", mkn[:, 2 * c:2 * c + 2, :], tn[:], ssn[:, 2 * c:2 * c + 2].unsqueeze(2).to_broadcast([128, 2, 128]), ALU.mult, R=[tn, ssn], W=[mkn])
                k.cp("act", mvb[:, 2 * c:2 * c + 2, :], kv4[c][:, :, 1, :], R=[PS[6 + c]], W=[mvb])
            for h_ in range(4):
                k.ts("dve", mkr[:, h_, :], krg[:, 0, :], ssn[:, h_:h_ + 1], None, ALU.mult, R=[krg, ssn], W=[mkr])
            store_T(T(mkn.h[:].rearrange("p h d -> p (h d)"), mkn.b), 4, 5, lambda: mkTn[:, :, col:col + 128].rearrange("h p c -> p h c"), R=[mkn])
            store_T(T(mkr.h[:].rearrange("p h d -> p (h d)"), mkr.b), 2, 5, lambda: mkTr[:, :, col:col + 128].rearrange("h p c -> p h c"), R=[mkr])
            k.dma(mvs[col:col + 128, :], mvb[:].rearrange("p h d -> p (h d)"), R=[mvb])

        for ct in range(PAST // 128 if int(os.environ.get("A_CTX", "1")) else 0):
            col = NT + ct * 128
            rows = slice(ct * 128, (ct + 1) * 128)
            CP = int(os.environ.get("A_CTXP", "9"))
            ckb = k.tile(f"ckb{ct % 2}", [128, 512], BF16)
            k.dma(ckb[:], cdk[l, rows, :], W=[ckb], q="pool")
            if CP >= 1:
                store_T(ckb, 4, 5, lambda: dkT[:, :, col:col + 128].rearrange("h p c -> p h c"), R=[ckb])
            cvb = k.tile(f"cvb{ct % 2}", [128, 512], BF16)
            if CP >= 2:
                k.dma(cvb[:], cdv[l, rows, :], W=[cvb], q="pool")
                k.dma(dvs[col:col + 128, :], cvb[:], R=[cvb])
            cc = k.tile(f"ccb{ct % 2}", [128, 256], BF16)
            crf = k.tile(f"crf{ct % 2}", [128, 64], F32)
            if CP >= 3:
                k.dma(cc[:], cckv[l, rows, :], W=[cc], q="pool")
                k.dma(crf[:], ckr[l, rows, :], W=[crf])
            if CP >= 4:
                mla_kv(cc, crf, rt1, col)

        def loads(t):
            xt = k.tile(f"x{t % 2}", [128, D], F32); rt = k.tile(f"rt{t % 2}", [128, 128], F32)
            k.dma(xt[:], xsrc[t * 128:(t + 1) * 128, :], W=[xt])
            k.dma(rt[:], rope_tab[t * 128:(t + 1) * 128, :], W=[rt])

        loads(0)
        A_TILES = int(os.environ.get("A_TILES", "999")); A_PARTS = int(os.environ.get("A_PARTS", "99"))
        for t in range(min(NTT, A_TILES)):
            if t + 1 < NTT:
                loads(t + 1)
            s = seq_of(t); v = s["var"]; pi = s["pi"]
            lrows = slice((t - s["t0"]) * 128, (t - s["t0"] + 1) * 128)
            grow = slice(t * 128, (t + 1) * 128)
            xt = k.tile(f"x{t % 2}", [128, D], F32); rt = k.tile(f"rt{t % 2}", [128, 128], F32)
            junk = k.tile("junk", [128, D], BF16); ssx = k.tile("ssx", [128, 1], F32)
            k.act(junk[:], xt[:], AF.Square, R=[xt], W=[junk, ssx], accum_out=ssx[:])
            rstd(ssx, 1.0 / D)
            tmp = k.tile("htmp", [128, D], F32); hb = k.tile("hb", [128, D], BF16)
            k.stt("dve", tmp[:], xt[:], ssx[:, 0:1], A1[v][:], ALU.mult, ALU.mult, R=[xt, ssx, A1[v]], W=[tmp])
            k.tt("pool", hb[:], tmp[:], B1[v][:], ALU.add, R=[tmp, B1[v]], W=[hb])
            hT = k.tile(f"hT{t % 2}", [128, 8, 128], BF16)
            transposes([hb[:, i * 128:(i + 1) * 128] for i in range(8)], 4, hT, R=[hb])
            k.dma(hTs[t], hT[:], R=[hT])

            def zmm(c0, c1, bank):
                for kc in range(8):
                    k.mm(PS[bank][:, 0:c1 - c0], hT[:, kc, :], wA[:, kc, c0:c1], kc == 0, kc == 7, R=[hT, wA], W=[PS[bank]])

            def qknorm(bank, G, name, f32out):
                z3 = PS[bank][:].rearrange("p (h d) -> p h d", d=64)
                sq = k.tile("sq", [128, 512], F32); ss8 = k.tile("ss8", [128, 8], F32)
                k.act(sq[:], PS[bank][:], AF.Square, R=[PS[bank]], W=[sq])
                k.red("dve", ss8[:], sq[:].rearrange("p (h d) -> p h d", d=64), R=[sq], W=[ss8])
                rstd(ss8, 1.0 / 64)
                ob = k.tile(name + "b", [128, 512], BF16)
                sc = ss8[:].unsqueeze(2).to_broadcast([128, 8, 64])
                if f32out:
                    of = k.tile(name + "f", [128, 512], F32)
                    rope_mul(z3, 8, G[:], G, rt, of[:].rearrange("p (h d) -> p h d", d=64), tA, tB, R=[PS[bank], ss8], W=[of], scale_bcast=sc)
                    k.cp("act", ob[:], of[:], R=[of], W=[ob])
                    return ob, of
                rope_mul(z3, 8, G[:], G, rt, ob[:].rearrange("p (h d) -> p h d", d=64), tA, tB, R=[PS[bank], ss8], W=[ob], scale_bcast=sc)
                return ob, None

            if A_PARTS <= 0:
                continue
            zmm(0, 512, 0)
            qb, _ = qknorm(0, gq, "dq", False)
            store_T(qb, 4, 5, lambda: dqT[:, :, grow].rearrange("h p c -> p h c"), R=[qb])
            if A_PARTS <= 1:
                continue
            zmm(512, 1024, 1)
            API = int(os.environ.get("A_PI", "7"))
            kb_, kf_ = qknorm(1, gk, "dk", pi is not None and (API & 1))
            store_T(kb_, 4, 5, lambda: dkT[:, :, grow].rearrange("h p c -> p h c"), R=[kb_])
            if pi is not None and (API & 1):
                k.dma(ndk[pi, l, lrows, :], kf_[:], R=[kf_])
            if A_PARTS <= 2:
                continue
            zmm(1024, 1536, 2)
            dvb = k.tile("dvb", [128, 512], BF16)
            k.cp("act", dvb[:], PS[2][:], R=[PS[2]], W=[dvb])
            k.dma(dvs[grow, :], dvb[:], R=[dvb])
            if pi is not None and (API & 2):
                dvf = k.tile("dvf", [128, 512], F32)
                k.cp("act", dvf[:], PS[2][:], R=[PS[2]], W=[dvf])
                k.dma(ndv[pi, l, lrows, :], dvf[:], R=[dvf])
            if A_PARTS <= 3:
                continue
            zmm(1536, 2048, 3)
            z = PS[3]
            rqf = k.tile("rqf", [128, 4, 64], F32); rkf = k.tile("rkf", [128, 4, 64], F32)
            rope_mul(z[:, 0:256].rearrange("p (h d) -> p h d", d=64), 4, ones64[:], ones64, rt, rqf[:], tA, tB, R=[z], W=[rqf])
            rope_mul(z[:, 256:512].rearrange("p (h d) -> p h d", d=64), 4, eighth[:], eighth, rt, rkf[:], tA, tB, R=[z], W=[rkf])
            rqb = k.tile("rqb", [128, 256], BF16); rkb = k.tile("rkb", [128, 256], BF16)
            k.cp("act", rqb[:], rqf[:].rearrange("p h d -> p (h d)"), R=[rqf], W=[rqb])
            k.cp("act", rkb[:], rkf[:].rearrange("p h d -> p (h d)"), R=[rkf], W=[rkb])
            rq2 = k.tile("rq2", [128, 4, 2, 64], BF16); kk = k.tile("kk", [128, 4, 2, 64], BF16)
            for d_ in range(2):
                k.tt("dve", rq2[:, :, d_, :], rqf[:], qdc[:, :, d_:d_ + 1].to_broadcast([128, 4, 64]), ALU.mult, R=[rqf, qdc], W=[rq2])
                k.tt("dve", kk[:, :, d_, :], rkf[:], kdc[:, :, d_:d_ + 1].to_broadcast([128, 4, 64]), ALU.mult, R=[rkf, kdc], W=[kk])
            store_T(rqb, 2, 5, lambda: rqT[:, :, grow].rearrange("h p c -> p h c"), R=[rqb])
            store_T(rkb, 2, 5, lambda: rkT[:, :, grow].rearrange("h p c -> p h c"), R=[rkb])
            store_T(T(rq2.h[:].rearrange("p h a d -> p (h a d)"), rq2.b), 4, 5, lambda: QQ[:, :, grow].rearrange("h p c -> p h c"), R=[rq2])
            k.dma(kks[grow, :], kk[:].rearrange("p h a d -> p (h a d)"), R=[kk])
            if A_PARTS <= 4:
                continue
            zmm(2048, 2560, 0)
            rvb = k.tile("rvb", [128, 512], BF16)
            k.cp("act", rvb[:], PS[0][:], R=[PS[0]], W=[rvb])
            k.dma(rvs[grow, :], rvb[:], R=[rvb])
            if A_PARTS <= 5:
                continue
            zmm(2560, 3072, 1)
            rgf = k.tile("rgf", [128, 512], F32)
            k.cp("act", rgf[:], PS[1][:], R=[PS[1]], W=[rgf])
            k.dma(rgs[grow, :], rgf[:], R=[rgf])
            if A_PARTS <= 6:
                continue
            zmm(3072, 3456, 2)
            z = PS[2]
            ssq = k.tile("ssq", [128, 1], F32); junk2 = k.tile("junk2", [128, 384], BF16)
            k.act(junk2[:], z[:, 0:384], AF.Square, R=[z], W=[junk2, ssq], accum_out=ssq[:])
            rstd(ssq, 1.0 / 384)
            qab = k.tile("qab", [128, 384], BF16)
            k.stt("dve", qab[:], z[:, 0:384], ssq[:, 0:1], gqa[:], ALU.mult, ALU.mult, R=[z, ssq, gqa], W=[qab])
            qaT = k.tile("qaT", [128, 3, 128], BF16)
            transposes([qab[:, i * 128:(i + 1) * 128] for i in range(3)], 5, qaT, R=[qab])
            mqn = k.tile("mqn", [128, 4, 128], BF16); mqr = k.tile("mqr", [128, 4, 64], BF16)
            for c in range(2):
                bk = PS[6 + c]
                for kc in range(3):
                    k.mm(bk[:, 0:384], qaT[:, kc, :], wqb[:, kc, c * 384:(c + 1) * 384], kc == 0, kc == 2, R=[qaT, wqb], W=[bk])
                sq = k.tile("sq", [128, 512], F32); ss2 = k.tile("ss2", [128, 2], F32)
                k.act(sq[:, 0:384], bk[:, 0:384], AF.Square, R=[bk], W=[sq])
                k.red("dve", ss2[:], sq[:, 0:384].rearrange("p (h d) -> p h d", d=192), R=[sq], W=[ss2])
                rstd(ss2, 1.0 / 192)
                tq = k.tile("tq", [128, 2, 192], F32); tqr = k.tile("tqr", [128, 2, 64], F32)
                k.tt("dve", tq[:], bk[:, 0:384].rearrange("p (h d) -> p h d", d=192), gmqn[:].unsqueeze(1).to_broadcast([128, 2, 192]), ALU.mult, R=[bk, gmqn], W=[tq])
                rope_mul(tq[:, :, 128:192], 2, ones64[:], ones64, rt, tqr[:], tA, tB, R=[tq], W=[tqr])
                k.tt("dve", mqn[:, 2 * c:2 * c + 2, :], tq[:, :, 0:128], ss2[:].unsqueeze(2).to_broadcast([128, 2, 128]), ALU.mult, R=[tq, ss2], W=[mqn])
                k.tt("dve", mqr[:, 2 * c:2 * c + 2, :], tqr[:], ss2[:].unsqueeze(2).to_broadcast([128, 2, 64]), ALU.mult, R=[tqr, ss2], W=[mqr])
            store_T(T(mqn.h[:].rearrange("p h d -> p (h d)"), mqn.b), 4, 5, lambda: mqTn[:, :, grow].rearrange("h p c -> p h c"), R=[mqn])
            store_T(T(mqr.h[:].rearrange("p h d -> p (h d)"), mqr.b), 2, 5, lambda: mqTr[:, :, grow].rearrange("h p c -> p h c"), R=[mqr])
            if A_PARTS <= 7:
                continue
            zmm(3456, 3776, 3)
            z = PS[3]
            ssk = k.tile("ssk", [128, 1], F32)
            k.act(junk2[:, 0:256], z[:, 0:256], AF.Square, R=[z], W=[junk2, ssk], accum_out=ssk[:])
            rstd(ssk, 1.0 / 256)
            ckvf = k.tile("ckvf", [128, 256], F32); krf = k.tile("krf", [128, 64], F32); ckvb = k.tile("ckvb", [128, 256], BF16)
            k.stt("dve", ckvf[:], z[:, 0:256], ssk[:, 0:1], gkva[:], ALU.mult, ALU.mult, R=[z, ssk, gkva], W=[ckvf])
            k.cp("act", krf[:], z[:, 256:320], R=[z], W=[krf])
            k.cp("act", ckvb[:], ckvf[:], R=[ckvf], W=[ckvb])
            if pi is not None and (API & 4):
                k.dma(nckv[pi, l, lrows, :], ckvf[:], R=[ckvf])
                k.dma(nkr[pi, l, lrows, :], krf[:], R=[krf])
            mla_kv(ckvb, krf, rt, t * 128)
        k.pop()
        k.reorder = False

    def phase_ret(l):
        for s in seqs:
            k.push()
            nch = s["nt"]; t0 = s["t0"]; pi = s["pi"]
            gn = k.sb([128, 128], F32, "gn"); bload(gn[:], Wt["ret_gn"][l], W=[gn])
            rv = k.sb([128, nch, 512], BF16, "rv")
            k.dma(rv[:], rvs[t0 * 128:(t0 + nch) * 128, :].rearrange("(c p) f -> p c f", p=128), W=[rv])
            U = k.sb([128, nch, 512], F32, "U"); RR = k.sb([128, nch, 512], BF16, "RR"); S = k.sb([128, 512], F32, "S")
            Ub = [k.buf() for _ in range(nch)]; RRf = [k.buf() for _ in range(nch)]; RRb = [k.buf() for _ in range(nch)]
            Sf = k.buf(); Sb = k.buf()
            for c in range(nch):
                kkt = k.tile(f"kkt{c % 2}", [128, 512], BF16)
                k.dma(kkt[:], kks[(t0 + c) * 128:(t0 + c + 1) * 128, :], W=[kkt])
                bank = PS[c % 2]
                for h in range(4):
                    hs = slice(h * 128, (h + 1) * 128)
                    k.mm(bank[:, hs], kkt[:, hs], rv[:, c, hs], True, True, R=[kkt, rv], W=[bank])
                k.cp("act", U[:, c, :], bank[:], R=[bank], W=[Ub[c]])
            RS = int(os.environ.get("R_STOP", "9"))
            if RS < 1:
                k.pop(); k.barrier(); continue
            if s["ctx"]:
                for d in range(2):
                    k.dma(S[64 * d:64 * d + 64, :].rearrange("k (h v) -> k h v", h=4), sret[l, d].rearrange("h k v -> k h v"), W=[Sf if d == 0 else Sb])
            else:
                k.memset("dve", S[0:64, :], 0.0, W=[Sf]); k.memset("pool", S[64:128, :], 0.0, W=[Sb])
            dec2 = decC[:].rearrange("p h e -> p (h e)")
            for c in range(nch):
                k.cp("dve", RR[0:64, c, :], S[0:64, :], R=[Sf], W=[RRf[c]])
                k.tt("dve", S[0:64, :], S[0:64, :], dec2[0:64, :], ALU.mult, R=[Sf, decC], W=[Sf])
                k.tt("dve", S[0:64, :], S[0:64, :], U[0:64, c, :], ALU.add, R=[Sf, Ub[c]], W=[Sf])
            for c in reversed(range(nch)):
                k.cp("act", RR[64:128, c, :], S[64:128, :], R=[Sb], W=[RRb[c]])
                k.tt("pool", S[64:128, :], S[64:128, :], dec2[64:128, :], ALU.mult, R=[Sb, decC], W=[Sb])
                k.tt("pool", S[64:128, :], S[64:128, :], U[64:128, c, :], ALU.add, R=[Sb, Ub[c]], W=[Sb])
            if RS < 2:
                k.pop(); k.barrier(); continue
            if pi is not None:
                k.dma(nsr[pi, l, 0].rearrange("h k v -> k h v"), S[0:64, :].rearrange("k (h v) -> k h v", h=4), R=[Sf])
                k.dma(nsr[pi, l, 1].rearrange("h k v -> k h v"), S[64:128, :].rearrange("k (h v) -> k h v", h=4), R=[Sb])
            if RS < 3:
                k.pop(); k.barrier(); continue
            for c in range(nch):
                t = t0 + c; cols = slice(t * 128, (t + 1) * 128)
                kT = k.tile(f"rkT{c % 2}", [128, 2, 128], BF16); qT = k.tile(f"rqT{c % 2}", [128, 2, 128], BF16)
                qq = k.tile(f"rqq{c % 2}", [128, 4, 128], BF16); rg = k.tile(f"rrg{c % 2}", [128, 512], F32)
                k.dma(kT[:], rkT[:, :, cols].rearrange("h p c -> p h c"), W=[kT])
                k.dma(qT[:], rqT[:, :, cols].rearrange("h p c -> p h c"), W=[qT])
                k.dma(qq[:], QQ[:, :, cols].rearrange("h p c -> p h c"), W=[qq])
                k.dma(rg[:], rgs[cols, :], W=[rg])
                bsr = [PS[2 - 2 * (c % 2)], PS[3 - 2 * (c % 2)]]; bo = PS[4 + c % 2]
                for h in range(4):
                    pp = slice(64 * (h % 2), 64 * (h % 2) + 64)
                    k.mm(bsr[h % 2][:, (h // 2) * 128:(h // 2 + 1) * 128], kT[pp, h // 2, :], qT[pp, h // 2, :], True, True, R=[kT, qT], W=[bsr[h % 2]])
                P = k.tile(f"rP{c % 2}", [128, 512], BF16)
                P4 = P[:].rearrange("p (a r i) -> p a r i", a=2, r=2)
                MT4 = MT[:].rearrange("p (a r) i -> p a r i", r=2)
                for r_ in range(2):
                    k.tt("dve", P4[:, :, r_, :], bsr[r_][:, 0:256].rearrange("p (a i) -> p a i", a=2), MT4[:, :, r_, :], ALU.mult, R=[bsr[r_], MT], W=[P])
                for h in range(4):
                    hs = slice(h * 128, (h + 1) * 128)
                    k.mm(bo[:, hs], P[:, hs], rv[:, c, hs], True, False, R=[P, rv], W=[bo])
                    k.mm(bo[:, hs], qq[:, h, :], RR[:, c, hs], False, True, R=[qq, RRf[c], RRb[c]], W=[bo])
                sq = k.tile("rsq", [128, 512], F32); ss4 = k.tile("rss4", [128, 4], F32)
                k.act(sq[:], bo[:], AF.Square, R=[bo], W=[sq])
                k.red("dve", ss4[:], sq[:].rearrange("p (h e) -> p h e", h=4), R=[sq], W=[ss4])
                rstd(ss4, 1.0 / 128)
                to = k.tile("rto", [128, 4, 128], F32)
                k.tt("dve", to[:], bo[:].rearrange("p (h e) -> p h e", h=4), ss4[:].unsqueeze(2).to_broadcast([128, 4, 128]), ALU.mult, R=[bo, ss4], W=[to])
                k.tt("dve", to[:], to[:], gn[:].unsqueeze(1).to_broadcast([128, 4, 128]), ALU.mult, R=[to, gn], W=[to])
                sg = k.tile("rsg", [128, 512], F32)
                k.act(sg[:], rg[:], AF.Silu, R=[rg], W=[sg])
                orb = k.tile("orb", [128, 512], BF16)
                k.tt("pool", orb[:], to[:].rearrange("p h e -> p (h e)"), sg[:], ALU.mult, R=[to, sg], W=[orb])
                store_T(orb, 4, 6, lambda: oT[1][:, :, cols].rearrange("h p c -> p h c"), R=[orb])
            k.pop()
            k.barrier()

    def attn_core(pairs_fn, V, nkt, QC, scale, R):
        nqt = QC // 128

        def st(kt):
            bank = PS[kt % 2]
            prs = pairs_fn(kt)
            for i, (a, b) in enumerate(prs):
                k.mm(bank[:, 0:QC], a, b, i == 0, i == len(prs) - 1, R=R, W=[bank])

        def pv(kt):
            bank = PS[kt % 2]
            PT = k.tile(f"PT{kt % 3}", [128, 512], BF16)
            k.act(PT[:, 0:QC], bank[:, 0:QC], AF.Exp, R=[bank], W=[PT], scale=scale)
            for qt in range(nqt):
                k.mm(PS[2 + qt][:, 0:129], PT[:, qt * 128:(qt + 1) * 128], V[:, kt, :], kt == 0, kt == nkt - 1, R=[PT, V], W=[PS[2 + qt]])
        st(0)
        for kt in range(nkt):
            if kt + 1 < nkt:
                st(kt + 1)
            pv(kt)

    def phase_attn(l, lam_init):
        k.push()
        gsub = k.sb([128, 128], F32, "gsub"); bload(gsub[:], Wt["diff_subln"][l], W=[gsub])
        k.ts("dve", gsub[:], gsub[:], 1.0 - lam_init, None, ALU.mult, R=[gsub], W=[gsub])
        for s in seqs:
            k.push()
            n = s["nt"] * 128; q0 = s["t0"] * 128
            nctx = PAST if s["ctx"] else 0
            nkt = (n + nctx) // 128
            QC = min(512, n); nqt = QC // 128
            V = k.sb([128, nkt, 129], BF16, "V"); kTa = k.sb([128, nkt * 128], BF16, "kTa"); kTb = k.sb([128, nkt * 128], BF16, "kTb")
            k.memset("dve", V[:], 1.0, W=[V])

            def load_keys(dst, src2d):
                k.dma(dst[:, 0:n], src2d[:, q0:q0 + n], W=[dst])
                if nctx:
                    k.dma(dst[:, n:n + nctx], src2d[:, NT:NT + nctx], W=[dst])

            def load_V(src, h):
                k.dma(V[:, 0:n // 128, 0:128], src[q0:q0 + n, h * 128:(h + 1) * 128].rearrange("(c p) e -> p c e", p=128), W=[V])
                if nctx:
                    k.dma(V[:, n // 128:nkt, 0:128], src[NT:NT + nctx, h * 128:(h + 1) * 128].rearrange("(c p) e -> p c e", p=128), W=[V])

            for h in range(4):
                load_keys(kTa, dkT[h]); load_V(dvs, h)
                for qc in range(n // QC):
                    qs = slice(q0 + qc * QC, q0 + (qc + 1) * QC)
                    qp = []
                    for m in range(2):
                        nm = f"qTd{m}_{qc % 2}"
                        fresh = nm not in k.cache
                        t_ = k.tile(nm, [128, QC], BF16)
                        if fresh:
                            k.memset("dve", t_[64 * (1 - m):64 * (1 - m) + 64, :], 0.0, W=[t_])
                        k.dma(t_[64 * m:64 * m + 64, :], dqT[h][64 * m:64 * m + 64, qs], W=[t_])
                        qp.append(t_)
                    Oa = k.tile("Oa", [128, nqt, 128], F32); ob = k.tile("odb", [128, nqt * 128], BF16)
                    for m in range(2):
                        pp = slice(64 * m, 64 * m + 64)
                        attn_core(lambda kt: [(kTa[:, kt * 128:(kt + 1) * 128], qp[m][:, :])], V, nkt, QC, 0.125, R=[kTa, qp[m]])
                        for qt in range(nqt):
                            O = PS[2 + qt]
                            rinv = k.tile("rinv", [128, 1], F32)
                            k.op("dve", lambda e, O=O, rinv=rinv: e.reciprocal(out=rinv[:], in_=O[:, 128:129]), R=[O], W=[rinv])
                            if m == 0:
                                k.ts("dve", Oa[:, qt, :], O[:, 0:128], rinv[:, 0:1], None, ALU.mult, R=[O, rinv], W=[Oa])
                            else:
                                k.tt("dve", rinv[:], rinv[:], neglam[:], ALU.mult, R=[rinv, neglam], W=[rinv])
                                od = k.tile("od", [128, 128], F32)
                                k.stt("dve", od[:], O[:, 0:128], rinv[:, 0:1], Oa[:, qt, :], ALU.mult, ALU.add, R=[O, rinv, Oa], W=[od])
                                ssd = k.tile("ssd", [128, 1], F32); junk = k.tile("ajunk", [128, 128], BF16)
                                k.act(junk[:], od[:], AF.Square, R=[od], W=[junk, ssd], accum_out=ssd[:])
                                rstd(ssd, 1.0 / 128)
                                k.stt("dve", ob[:, qt * 128:(qt + 1) * 128], od[:], ssd[:, 0:1], gsub[:], ALU.mult, ALU.mult, R=[od, ssd, gsub], W=[ob])
                    store_T(ob, nqt, 6, lambda: oT[0, h][:, qs].rearrange("p (i c) -> p i c", c=128), R=[ob])
            for h in range(4):
                load_keys(kTa, mkTn[h]); load_keys(kTb, mkTr[h // 2]); load_V(mvs, h)
                pp = slice(64 * (h % 2), 64 * (h % 2) + 64)
                for qc in range(n // QC):
                    qs = slice(q0 + qc * QC, q0 + (qc + 1) * QC)
                    qTn = k.tile(f"qTn{qc % 2}", [128, QC], BF16)
                    nm = f"qTr{h % 2}_{qc % 2}"
                    fresh = nm not in k.cache
                    qTr = k.tile(nm, [128, QC], BF16)
                    if fresh:
                        k.memset("dve", qTr[64 * (1 - h % 2):64 * (1 - h % 2) + 64, :], 0.0, W=[qTr])
                    k.dma(qTn[:], mqTn[h][:, qs], W=[qTn]); k.dma(qTr[pp, :], mqTr[h // 2][pp, qs], W=[qTr])
                    attn_core(lambda kt: [(kTa[:, kt * 128:(kt + 1) * 128], qTn[:, :]), (kTb[:, kt * 128:(kt + 1) * 128], qTr[:, :])],
                              V, nkt, QC, 192 ** -0.5, R=[kTa, kTb, qTn, qTr])
                    omb = k.tile("omb", [128, nqt * 128], BF16)
                    for qt in range(nqt):
                        O = PS[2 + qt]
                        rinv = k.tile("rinv", [128, 1], F32)
                        k.op("dve", lambda e, O=O, rinv=rinv: e.reciprocal(out=rinv[:], in_=O[:, 128:129]), R=[O], W=[rinv])
                        k.ts("dve", omb[:, qt * 128:(qt + 1) * 128], O[:, 0:128], rinv[:, 0:1], None, ALU.mult, R=[O, rinv], W=[omb])
                    store_T(omb, nqt, 6, lambda: oT[2, h][:, qs].rearrange("p (i c) -> p i c", c=128), R=[omb])
            k.pop()
            k.barrier()
        k.pop()

    def mod_tiles(l, which):
        out = {}
        for name, idx in which:
            out[name] = []
            for v in range(2):
                a = k.sb([128, D], F32, name)
                bload(a[:], modrow[v, idx * D:(idx + 1) * D], W=[a])
                out[name].append(a)
        return out

    def phase_merge(l, xsrc):
        k.push()
        k.reorder = True
        wg = k.sb([128, 8, 3072], BF16, "wg"); wload(wg, Wt["w_in"][l][:, NPH_A:D_IN], nchunk=6)
        wb = k.sb([128, 12, 1024], BF16, "wb")
        for i in range(3):
            wload(T(wb.h[:, 4 * i:4 * i + 4, :], wb.b), Wt["w_branch"][l, i])
        wo = k.sb([128, 8, 1024], BF16, "wo"); wload(wo, Wt["w_out"][l], nchunk=2)
        m = mod_tiles(l, [("G1", 2), ("B2", 3), ("A2", 4)])
        n2 = k.sb([128, D], F32, "n2"); bload(n2[:], Wt["norm2"][l], W=[n2])
        for v in range(2):
            a = m["A2"][v]
            k.stt("dve", a[:], a[:], 1.0, n2[:], ALU.add, ALU.mult, R=[a, n2], W=[a])

        def loads(t):
            cols = slice(t * 128, (t + 1) * 128)
            xt = k.tile(f"x{t % 2}", [128, D], F32); hT = k.tile(f"hT{t % 2}", [128, 8, 128], BF16)
            oTt = k.tile(f"oTt{t % 2}", [128, 12, 128], BF16)
            k.dma(xt[:], xsrc[cols, :], W=[xt]); k.dma(hT[:], hTs[t], W=[hT])
            for i in range(3):
                k.dma(oTt[:, 4 * i:4 * i + 4, :], oT[i][:, :, cols].rearrange("h p c -> p h c"), W=[oTt])
        loads(0)
        for t in range(NTT):
            if t + 1 < NTT:
                loads(t + 1)
            v = seq_of(t)["var"]; grow = slice(t * 128, (t + 1) * 128)
            xt = k.tile(f"x{t % 2}", [128, D], F32); hT = k.tile(f"hT{t % 2}", [128, 8, 128], BF16)
            oTt = k.tile(f"oTt{t % 2}", [128, 12, 128], BF16)
            mer = k.tile("mer", [128, D], F32)
            for nn in range(2):
                cs = slice(nn * 512, (nn + 1) * 512)
                for i in range(3):
                    j = nn * 3 + i
                    bg = PS[j % 2]; bo = PS[2 + j % 2]
                    for kc in range(8):
                        k.mm(bg[:], hT[:, kc, :], wg[:, kc, i * 1024 + nn * 512:i * 1024 + nn * 512 + 512], kc == 0, kc == 7, R=[hT, wg], W=[bg])
                    for kc in range(4):
                        k.mm(bo[:], oTt[:, i * 4 + kc, :], wb[:, i * 4 + kc, cs], kc == 0, kc == 3, R=[oTt, wb], W=[bo])
                    sig = k.tile(f"sig{j % 2}", [128, 512], F32)
                    k.act(sig[:], bg[:], AF.Sigmoid, R=[bg], W=[sig])
                    if i == 0:
                        k.tt("dve", mer[:, cs], sig[:], bo[:], ALU.mult, R=[sig, bo], W=[mer])
                    else:
                        tmpm = k.tile(f"tmpm{j % 2}", [128, 512], F32)
                        k.tt("dve", tmpm[:], sig[:], bo[:], ALU.mult, R=[sig, bo], W=[tmpm])
                        k.tt("pool", mer[:, cs], mer[:, cs], tmpm[:], ALU.add, R=[mer, tmpm], W=[mer])
            merb = k.tile("merb", [128, D], BF16)
            k.cp("act", merb[:], mer[:], R=[mer], W=[merb])
            mT = k.tile("mT", [128, 8, 128], BF16)
            transposes([merb[:, i * 128:(i + 1) * 128] for i in range(8)], 6, mT, R=[merb])
            x1 = k.tile("xout", [128, D], F32)
            for nn in range(2):
                cs = slice(nn * 512, (nn + 1) * 512)
                bo = PS[4 + nn]
                for kc in range(8):
                    k.mm(bo[:], mT[:, kc, :], wo[:, kc, cs], kc == 0, kc == 7, R=[mT, wo], W=[bo])
                tmpo = k.tile(f"tmpo{nn}", [128, 512], F32)
                k.tt("dve", tmpo[:], bo[:], m["G1"][v][:, cs], ALU.mult, R=[bo, m["G1"][v]], W=[tmpo])
                k.tt("pool", x1[:, cs], xt[:, cs], tmpo[:], ALU.add, R=[xt, tmpo], W=[x1])
            k.dma(xa[grow, :], x1[:], R=[x1])
            junk = k.tile("junk", [128, D], BF16); ssx = k.tile("ssx", [128, 1], F32)
            k.act(junk[:], x1[:], AF.Square, R=[x1], W=[junk, ssx], accum_out=ssx[:])
            rstd(ssx, 1.0 / D)
            tmp = k.tile("htmp", [128, D], F32); hb = k.tile("hb", [128, D], BF16)
            k.stt("dve", tmp[:], x1[:], ssx[:, 0:1], m["A2"][v][:], ALU.mult, ALU.mult, R=[x1, ssx, m["A2"][v]], W=[tmp])
            k.tt("pool", hb[:], tmp[:], m["B2"][v][:], ALU.add, R=[tmp, m["B2"][v]], W=[hb])
            h2T = k.tile("h2T", [128, 8, 128], BF16)
            transposes([hb[:, i * 128:(i + 1) * 128] for i in range(8)], 7, h2T, R=[hb])
            k.dma(h2Ts[t], h2T[:], R=[h2T])
        k.pop()
        k.reorder = False

    def phase_mlp(l, xdst):
        k.push()
        wu = k.sb([128, 8, 4096], BF16, "wu"); wload(wu, Wt["w_up"][l], nchunk=8)
        wd = k.sb([128, 32, 1024], BF16, "wd"); wload(wd, Wt["w_down"][l], nchunk=4)
        m = mod_tiles(l, [("G2", 5)])
        uT = k.sb([128, 32, 512], BF16, "uT"); uTb = [k.buf() for _ in range(32)]
        groups = []
        for s in seqs:
            ts_ = list(range(s["t0"], s["t0"] + s["nt"]))
            for i in range(0, len(ts_), 4):
                groups.append((s["var"], ts_[i:i + 4]))
        for gi, (v, tiles) in enumerate(groups):
            ncol = len(tiles) * 128
            h2 = k.tile("h2g", [128, 8, 512], BF16)
            for j, t in enumerate(tiles):
                k.dma(h2[:, :, j * 128:(j + 1) * 128], h2Ts[t], W=[h2])
            for f in range(32):
                bank = PS[f % 2]
                for kc in range(8):
                    k.mm(bank[:, 0:ncol], wu[:, kc, f * 128:(f + 1) * 128], h2[:, kc, 0:ncol], kc == 0, kc == 7, R=[wu, h2], W=[bank])
                r = k.tile(f"relu{f % 2}", [128, 512], F32)
                k.act(r[:, 0:ncol], bank[:, 0:ncol], AF.Relu, R=[bank], W=[r])
                k.tt("dve" if f % 2 else "pool", uT[:, f, 0:ncol], r[:, 0:ncol], r[:, 0:ncol], ALU.mult, R=[r], W=[uTb[f]])
            for j, t in enumerate(tiles):
                rows = slice(t * 128, (t + 1) * 128)
                x1 = k.tile(f"mx{j % 2}", [128, D], F32); xo = k.tile("mxo", [128, D], F32)
                k.dma(x1[:], xa[rows, :], W=[x1])
                for nn in range(2):
                    cs = slice(nn * 512, (nn + 1) * 512)
                    bank = PS[2 + (j * 2 + nn) % 4]
                    for f in range(32):
                        k.mm(bank[:], uT[:, f, j * 128:(j + 1) * 128], wd[:, f, cs], f == 0, f == 31, R=[uTb[f], wd], W=[bank])
                    tmp = k.tile(f"mtmp{nn}", [128, 512], F32)
                    k.tt("dve", tmp[:], bank[:], m["G2"][v][:, cs], ALU.mult, R=[bank, m["G2"][v]], W=[tmp])
                    k.tt("pool", xo[:, cs], x1[:, cs], tmp[:], ALU.add, R=[x1, tmp], W=[xo])
                k.dma(xdst[rows, :], xo[:], R=[xo])
        k.pop()

    nph = 0
    for l in range(L):
        lam_init = 0.8 - 0.6 * math.exp(-0.3 * l)
        phases = [lambda: layer_consts(l), lambda: phase_mod(l), lambda: phase_A(l, x_all if l == 0 else xb),
                  lambda: phase_ret(l), lambda: phase_attn(l, lam_init), lambda: phase_merge(l, x_all if l == 0 else xb),
                  lambda: phase_mlp(l, xb if l < L - 1 else y_all)]
        for ph in phases:
            if nph < stop:
                ph(); k.barrier()
            nph += 1
    k.emit()
    return k


def _rope_table(NS, NP):
    n_rows = NS // GRID_W
    row = np.repeat(np.arange(n_rows, dtype=np.float32), GRID_W)
    col = np.tile(np.arange(GRID_W, dtype=np.float32), n_rows)
    inv = (10000.0 ** (-np.arange(0, 32, 2, dtype=np.float32) / 32)).astype(np.float32)
    ar = (row[:, None] * inv[None, :]).astype(np.float32); ac = (col[:, None] * inv[None, :]).astype(np.float32)
    cr, sr, cc, sc = np.cos(ar), np.sin(ar), np.cos(ac), np.sin(ac)
    tab = np.concatenate([cr, cr, cc, cc, -sr, sr, -sc, sc], axis=1).astype(np.float32)
    ptab = np.concatenate([np.ones((2 * NP, 64), np.float32), np.zeros((2 * NP, 64), np.float32)], axis=1)
    return np.ascontiguousarray(np.concatenate([tab, ptab], axis=0))


def _ret_consts():
    C = 128
    j = np.arange(C, dtype=np.float32)[:, None]; i = np.arange(C, dtype=np.float32)[None, :]
    relf = np.maximum(i - j, 0.0); maskf = (i >= j).astype(np.float32)
    relb = np.maximum(j - i, 0.0); maskb = (j > i).astype(np.float32)
    retc = np.stack([relf, maskf, relb, maskb]).astype(np.float32)
    p = np.arange(C, dtype=np.float32)
    retcol = np.stack([p + 1.0, C - p, C - 1.0 - p, p], axis=1).astype(np.float32)
    return np.ascontiguousarray(retc), np.ascontiguousarray(retcol)


_CACHE = {}


def _run(inputs, NS, NP, PAST, dbg=(), stop=99):
    key = (NS, NP, PAST, tuple(dbg), stop)
    if key not in _CACHE:
        _CACHE[key] = build(NS, NP, PAST, dbg, stop)
    kb = _CACHE[key]
    f = lambda a: np.ascontiguousarray(np.asarray(a, dtype=np.float32))
    rope_tab = _rope_table(NS, NP); retc, retcol = _ret_consts()
    xp = f(inputs["x_prompt"]); xs = f(inputs["x_sample"])
    L = DEPTH
    in_maps = []
    for c in range(8):
        m = {
            "x_all": np.ascontiguousarray(np.concatenate([xs[c], xp[2 * c], xp[2 * c + 1]], axis=0)),
            "cvec": np.ascontiguousarray(np.stack([f(inputs["c"])[c], f(inputs["c_ctx"])])),
            "cdk": f(inputs["cache_diff_k"])[c].reshape(L, PAST, 512),
            "cdv": f(inputs["cache_diff_v"])[c].reshape(L, PAST, 512),
            "cckv": f(inputs["cache_mla_ckv"])[c], "ckr": f(inputs["cache_mla_krope"])[c],
            "sret": f(inputs["state_ret"])[c],
            "rope_tab": rope_tab, "retc": retc, "retcol": retcol,
        }
        for n, _ in W_SPECS:
            m[n] = f(inputs[n])
        in_maps.append({a: np.ascontiguousarray(b) for a, b in m.items()})
    res = run_bass_kernel_spmd(kb.nc, in_maps, core_ids=list(range(8)))
    return res.results


def kernel(**inputs):
    NS = inputs["x_sample"].shape[1]; NP = inputs["x_prompt"].shape[1]; PAST = inputs["cache_diff_k"].shape[2]
    B = inputs["x_prompt"].shape[0]
    r = _run(inputs, NS, NP, PAST)
    L = DEPTH
    y_prompt = np.zeros((B, NP, D), np.float32); y_sample = np.zeros((8, NS, D), np.float32)
    ndk = np.zeros((B, L, NP, 8, 64), np.float32); ndv = np.zeros((B, L, NP, 4, 128), np.float32)
    nckv = np.zeros((B, L, NP, 256), np.float32); nkr = np.zeros((B, L, NP, 64), np.float32)
    nsr = np.zeros((B, L, 2, 4, 64, 128), np.float32)
    for c in range(8):
        ya = r[c]["y_all"]
        y_sample[c] = ya[:NS]
        for p in range(2):
            b = 2 * c + p
            y_prompt[b] = ya[NS + p * NP: NS + (p + 1) * NP]
            ndk[b] = r[c]["ndk"][p].reshape(L, NP, 8, 64); ndv[b] = r[c]["ndv"][p].reshape(L, NP, 4, 128)
            nckv[b] = r[c]["nckv"][p]; nkr[b] = r[c]["nkr"][p]; nsr[b] = r[c]["nsr"][p]
    return (y_prompt, y_sample, ndk, ndv, nckv, nkr, nsr)
```
